# Optimizing a Trainium2 kernel written in Bass

```python
import jax, jax.numpy as jnp
from jax import lax
import numpy as np

D_MODEL = 2048
BATCH = 2
SEQ = 8192
DEPTH = 4

HEAD_DIM = 128
D_MIX = D_MODEL
POOL_WIDTH = D_MIX // 4
N_POOL_GROUPS = 4
POOL_GROUP_DIM = POOL_WIDTH // N_POOL_GROUPS
POOL_WINDOWS = (2, 4, 8, 16)
ATTN_WIDTH = D_MIX // 2
N_ATTN_HEADS = ATTN_WIDTH // HEAD_DIM
LRU_WIDTH = D_MIX - POOL_WIDTH - ATTN_WIDTH
N_LRU_BLOCKS = 4
LRU_BLOCK_DIM = LRU_WIDTH // N_LRU_BLOCKS
LRU_CONV_WIDTH = 4
LRU_C = 8.0
D_FF = ((8 * D_MODEL // 3 + 255) // 256) * 256
FFN_CONV_WIDTH = 3
Q_BLOCK = 128
N_IN = POOL_WIDTH + 3 * ATTN_WIDTH + N_ATTN_HEADS + 2 * LRU_WIDTH
EPS = 1e-6

kernel_name = "hymba_style_pool_fox_rglru_convffn"


def rmsnorm(x, g):
    xf = x.astype(jnp.float32)
    y = xf * lax.rsqrt(jnp.mean(xf * xf, axis=-1, keepdims=True) + EPS)
    return (y * g.astype(jnp.float32)).astype(x.dtype)


def causal_dwconv(u, w, b):
    K = w.shape[0]
    S = u.shape[1]
    up = jnp.pad(u, ((0, 0), (K - 1, 0), (0, 0)))
    out = b + up[:, 0:S] * w[0]
    for k in range(1, K):
        out = out + up[:, k:k + S] * w[k]
    return out


def pool_mixer(u, w, scale):
    B, S, _ = u.shape
    uf = u.astype(jnp.float32).reshape(B, S, N_POOL_GROUPS, POOL_GROUP_DIM)
    cs = jnp.cumsum(uf, axis=1)
    pos = jnp.arange(1, S + 1, dtype=jnp.float32)
    outs = []
    for g, win in enumerate(POOL_WINDOWS):
        csg = cs[:, :, g]
        lag = jnp.pad(csg, ((0, 0), (win, 0), (0, 0)))[:, :S]
        mean = (csg - lag) / jnp.minimum(pos, float(win))[None, :, None]
        outs.append(mean - uf[:, :, g])
    d = jnp.stack(outs, axis=2).astype(u.dtype)
    y = jnp.einsum('bsgc,gcd->bsgd', d, w).reshape(B, S, POOL_WIDTH)
    return y * scale


def forgetting_attention(q, k, v, f_logit, b_f):
    B, S, _ = q.shape
    H = N_ATTN_HEADS
    q = q.reshape(B, S, H, HEAD_DIM).transpose(0, 2, 1, 3)
    k = k.reshape(B, S, H, HEAD_DIM).transpose(0, 2, 1, 3)
    v = v.reshape(B, S, H, HEAD_DIM).transpose(0, 2, 1, 3)
    log_f = jax.nn.log_sigmoid(f_logit.astype(jnp.float32) + b_f.astype(jnp.float32))
    F = jnp.cumsum(log_f, axis=1).transpose(0, 2, 1)
    nb = S // Q_BLOCK
    qb = q.reshape(B, H, nb, Q_BLOCK, HEAD_DIM).transpose(2, 0, 1, 3, 4)
    Fb = F.reshape(B, H, nb, Q_BLOCK).transpose(2, 0, 1, 3)
    kpos = jnp.arange(S)
    scale = HEAD_DIM ** -0.5

    def block(args):
        q_blk, F_blk, i = args
        qpos = i * Q_BLOCK + jnp.arange(Q_BLOCK)
        s = (jnp.einsum('bhqd,bhkd->bhqk', q_blk, k).astype(jnp.float32) * scale
             + F_blk[..., None] - F[:, :, None, :])
        s = jnp.where(kpos[None, :] <= qpos[:, None], s, -jnp.inf)
        p = jax.nn.softmax(s, axis=-1).astype(v.dtype)
        return jnp.einsum('bhqk,bhkd->bhqd', p, v)

    o = lax.map(block, (qb, Fb, jnp.arange(nb)))
    return o.transpose(1, 0, 3, 2, 4).reshape(B, S, H * HEAD_DIM)


def rg_lru_branch(xb, yb, conv_w, conv_b, wa, ba, wi, bi, lam):
    B, S, _ = xb.shape
    xc = causal_dwconv(xb, conv_w, conv_b)
    xh = xc.reshape(B, S, N_LRU_BLOCKS, LRU_BLOCK_DIM)
    gate_r = jax.nn.sigmoid((jnp.einsum('bshc,hcd->bshd', xh, wa).reshape(B, S, LRU_WIDTH) + ba).astype(jnp.float32))
    gate_i = jax.nn.sigmoid((jnp.einsum('bshc,hcd->bshd', xh, wi).reshape(B, S, LRU_WIDTH) + bi).astype(jnp.float32))
    log_a = -LRU_C * gate_r * jax.nn.softplus(-lam.astype(jnp.float32))
    a = jnp.exp(log_a)
    inp = jnp.sqrt(-jnp.expm1(2.0 * log_a)) * (gate_i * xc.astype(jnp.float32))

    def combine(left, right):
        a1, b1 = left
        a2, b2 = right
        return a1 * a2, a2 * b1 + b2

    _, h = lax.associative_scan(combine, (a, inp), axis=1)
    return (h * jax.nn.gelu(yb.astype(jnp.float32))).astype(xb.dtype)


def hybrid_mixer(h, w_in, b_f, pool_w, pool_scale, lru_conv_w, lru_conv_b,
                 lru_wa, lru_ba, lru_wi, lru_bi, lru_lambda, w_out):
    z = h @ w_in
    sizes = [POOL_WIDTH, ATTN_WIDTH, ATTN_WIDTH, ATTN_WIDTH, N_ATTN_HEADS, LRU_WIDTH]
    offsets = [int(o) for o in np.cumsum(sizes)]
    zp, zq, zk, zv, zf, zx, zy = jnp.split(z, offsets, axis=-1)
    y_pool = pool_mixer(zp, pool_w, pool_scale)
    y_attn = forgetting_attention(zq, zk, zv, zf, b_f)
    y_lru = rg_lru_branch(zx, zy, lru_conv_w, lru_conv_b, lru_wa, lru_ba, lru_wi, lru_bi, lru_lambda)
    y = jnp.concatenate([y_pool, y_attn.astype(h.dtype), y_lru], axis=-1)
    return y @ w_out


def conv_glu_ffn(h, w_gate, w_up, conv_w, conv_b, w_down):
    g = causal_dwconv(h @ w_gate, conv_w, conv_b)
    return (jax.nn.silu(g) * (h @ w_up)) @ w_down


def setup_inputs(seed: int = 0) -> dict:
    key = jax.random.key(seed)
    ks = jax.random.split(key, 26)
    f32 = jnp.float32

    def nrm(k, shape, s):
        return jax.random.normal(k, shape, f32) * s

    u_lam = jax.random.uniform(ks[15], (DEPTH, LRU_WIDTH), f32, 0.9, 0.999)
    s_lam = u_lam ** (1.0 / LRU_C)
    lru_lambda = jnp.log(s_lam) - jnp.log1p(-s_lam)
    return {
        "x": nrm(ks[0], (BATCH, SEQ, D_MODEL), 1.0),
        "c": nrm(ks[1], (BATCH, D_MODEL), 1.0),
        "w_ada": nrm(ks[2], (DEPTH, D_MODEL, 6 * D_MODEL), 0.5 * D_MODEL ** -0.5),
        "b_ada": nrm(ks[3], (DEPTH, 6 * D_MODEL), 0.01),
        "g_mix": 1.0 + nrm(ks[4], (DEPTH, D_MODEL), 0.05),
        "w_in": nrm(ks[5], (DEPTH, D_MODEL, N_IN), D_MODEL ** -0.5),
        "b_f": jax.random.uniform(ks[6], (DEPTH, N_ATTN_HEADS), f32, 1.0, 5.0),
        "pool_w": nrm(ks[7], (DEPTH, N_POOL_GROUPS, POOL_GROUP_DIM, POOL_GROUP_DIM), POOL_GROUP_DIM ** -0.5),
        "pool_scale": 1.0 + nrm(ks[8], (DEPTH, POOL_WIDTH), 0.1),
        "lru_conv_w": nrm(ks[9], (DEPTH, LRU_CONV_WIDTH, LRU_WIDTH), LRU_CONV_WIDTH ** -0.5),
        "lru_conv_b": nrm(ks[10], (DEPTH, LRU_WIDTH), 0.01),
        "lru_wa": nrm(ks[11], (DEPTH, N_LRU_BLOCKS, LRU_BLOCK_DIM, LRU_BLOCK_DIM), LRU_BLOCK_DIM ** -0.5),
        "lru_ba": nrm(ks[12], (DEPTH, LRU_WIDTH), 0.01),
        "lru_wi": nrm(ks[13], (DEPTH, N_LRU_BLOCKS, LRU_BLOCK_DIM, LRU_BLOCK_DIM), LRU_BLOCK_DIM ** -0.5),
        "lru_bi": nrm(ks[14], (DEPTH, LRU_WIDTH), 0.01),
        "lru_lambda": lru_lambda,
        "w_out": nrm(ks[16], (DEPTH, D_MIX, D_MODEL), D_MIX ** -0.5),
        "g_ffn": 1.0 + nrm(ks[17], (DEPTH, D_MODEL), 0.05),
        "w_ffn_gate": nrm(ks[18], (DEPTH, D_MODEL, D_FF), D_MODEL ** -0.5),
        "w_ffn_up": nrm(ks[19], (DEPTH, D_MODEL, D_FF), D_MODEL ** -0.5),
        "ffn_conv_w": nrm(ks[20], (DEPTH, FFN_CONV_WIDTH, D_FF), FFN_CONV_WIDTH ** -0.5),
        "ffn_conv_b": nrm(ks[21], (DEPTH, D_FF), 0.01),
        "w_ffn_down": nrm(ks[22], (DEPTH, D_FF, D_MODEL), D_FF ** -0.5),
        "final_g": 1.0 + nrm(ks[23], (D_MODEL,), 0.05),
    }


def reference(x, c, w_ada, b_ada, g_mix, w_in, b_f, pool_w, pool_scale, lru_conv_w, lru_conv_b,
              lru_wa, lru_ba, lru_wi, lru_bi, lru_lambda, w_out, g_ffn, w_ffn_gate, w_ffn_up,
              ffn_conv_w, ffn_conv_b, w_ffn_down, final_g):
    c_act = jax.nn.silu(c)
    for l in range(DEPTH):
        mod = c_act @ w_ada[l] + b_ada[l]
        sh1, sc1, gt1, sh2, sc2, gt2 = jnp.split(mod[:, None, :], 6, axis=-1)
        h = rmsnorm(x, g_mix[l]) * (1.0 + sc1) + sh1
        x = x + gt1 * hybrid_mixer(h, w_in[l], b_f[l], pool_w[l], pool_scale[l], lru_conv_w[l], lru_conv_b[l],
                                   lru_wa[l], lru_ba[l], lru_wi[l], lru_bi[l], lru_lambda[l], w_out[l])
        h = rmsnorm(x, g_ffn[l]) * (1.0 + sc2) + sh2
        x = x + gt2 * conv_glu_ffn(h, w_ffn_gate[l], w_ffn_up[l], ffn_conv_w[l], ffn_conv_b[l], w_ffn_down[l])
    return rmsnorm(x, final_g)
```

```python
import math
from contextlib import ExitStack, contextmanager

import numpy as np
import ml_dtypes

import concourse.bass as bass
import concourse.mybir as mybir
from concourse.bass_utils import run_bass_kernel_spmd

F32 = mybir.dt.float32
BF16 = mybir.dt.bfloat16
AF = mybir.ActivationFunctionType
ALU = mybir.AluOpType

D = 2048
NKC = 16
DFF = 5632
NJ = 44
DEPTH = 4
NCORE = 8
EPS = 1e-6
POOL_WINDOWS = (2, 4, 8, 16)
NIN = 4616
C_Q, C_K, C_ZX, C_ZY, C_V, C_ZP, C_F = 0, 256, 512, 640, 768, 1024, 1152
NIN_G = 1154
MASKNEG = -30000.0


class Sched:
    EPOCH = 30000

    def __init__(self, nc, es):
        self.nc = nc
        self.es = es
        self.engs = {"pe": nc.tensor, "act": nc.scalar, "dve": nc.vector, "pool": nc.gpsimd, "sp": nc.sync}
        self.sems = {}
        self.count = {}
        self.step = {}
        self.waited = {e: {} for e in self.engs}
        self.lastw = {}
        self.readers = {}
        self.ninstr = 0
        self.outs = {}
        self.multiw = {}
        self.log = {e: [] for e in self.engs}

    def _sem(self, chan, ep):
        key = (chan, ep)
        if key not in self.sems:
            self.sems[key] = self.es.enter_context(self.nc.semaphore("s_%s_%d" % (chan, ep)))
        return self.sems[key]

    def _chan(self, chan, step):
        if chan not in self.count:
            self.count[chan] = 0
            self.step[chan] = step

    def _wait(self, eng, chan, total):
        if total <= self.waited[eng].get(chan, 0):
            return
        self.waited[eng][chan] = total
        esz = self.EPOCH * self.step[chan]
        ep = (total - 1) // esz
        self.engs[eng].wait_ge(self._sem(chan, ep), total - ep * esz)
        self.log[eng].append(("wait", (chan, ep), total - ep * esz))
        self.ninstr += 1

    def _deps(self, eng, me, reads, writes):
        for r in reads:
            lw = self.lastw.get(r)
            if lw is not None:
                self._wait(eng, lw[0], lw[1])
        for w in writes:
            lw = self.lastw.get(w)
            if lw is not None and lw[0] != me:
                self._wait(eng, lw[0], lw[1])
            for ch, tot in self.readers.get(w, {}).items():
                if ch != me or me not in ("pe",):
                    self._wait(eng, ch, tot)

    def _record(self, me, total, reads, writes):
        for w in writes:
            self.lastw[w] = (me, total)
            self.readers[w] = {}
        for r in reads:
            d = self.readers.setdefault(r, {})
            d[me] = max(d.get(me, 0), total)

    def op(self, eng, reads, writes, fn, sig=True):
        self._chan(eng, 1)
        for r in reads:
            lw = self.lastw.get(r)
            if lw is not None:
                self._wait(eng, lw[0], lw[1])
        for w in writes:
            lw = self.lastw.get(w)
            if lw is not None and not (eng == "pe" and lw[0] == "pe"):
                self._wait(eng, lw[0], lw[1])
            for ch, tot in self.readers.get(w, {}).items():
                if ch != eng:
                    self._wait(eng, ch, tot)
        ins = fn(self.engs[eng])
        self.ninstr += 1
        total = self.count[eng] + 1
        if sig:
            esz = self.EPOCH
            ep = (total - 1) // esz
            ins.then_inc(self._sem(eng, ep), 1)
            self.log[eng].append(("inc", (eng, ep), 1))
            self.count[eng] = total
        else:
            self.log[eng].append(("nop", None, 0))
        self._record(eng, total, reads, writes)
        return ins

    def dma(self, queue, chan, out, in_, reads, writes, is_out=False, multi_w=()):
        self._chan(chan, 16)
        for r in reads:
            lw = self.lastw.get(r)
            if lw is not None:
                self._wait(queue, lw[0], lw[1])
        for w in writes:
            lw = self.lastw.get(w)
            if lw is not None and lw[0] != chan:
                self._wait(queue, lw[0], lw[1])
            for ch, tot in self.readers.get(w, {}).items():
                self._wait(queue, ch, tot)
        ins = self.engs[queue].dma_start(out=out, in_=in_)
        self.ninstr += 1
        self.count[chan] += 16
        total = self.count[chan]
        assert total < self.EPOCH * 16
        ins.then_inc(self._sem(chan, 0), 16)
        self.log[queue].append(("inc", (chan, 0), 16))
        self._record(chan, total, reads, writes)
        if is_out:
            self.outs[chan] = total
        for r in multi_w:
            self.multiw.setdefault(r, {})[chan] = total
        return ins

    def simulate(self):
        sem = {}
        pos = {e: 0 for e in self.engs}
        progress = True
        while progress:
            progress = False
            for e, lg in self.log.items():
                while pos[e] < len(lg):
                    kind, key, val = lg[pos[e]]
                    if kind == "wait":
                        if sem.get(key, 0) < val:
                            break
                    elif kind == "inc":
                        sem[key] = sem.get(key, 0) + val
                    pos[e] += 1
                    progress = True
        stuck = {e: (pos[e], len(lg), lg[pos[e]] if pos[e] < len(lg) else None) for e, lg in self.log.items() if pos[e] < len(lg)}
        return stuck

    def barrier(self):
        for e in self.engs:
            for ch, tot in self.count.items():
                if ch != e and tot > 0:
                    self._wait(e, ch, tot)
        self.lastw.clear()
        self.readers.clear()
        self.multiw.clear()

    def collective(self, kind, in_t, out_t, reads, writes):
        self._chan("cc", 1)
        for r in reads:
            lw = self.lastw.get(r)
            if lw is not None:
                self._wait("pool", lw[0], lw[1])
            for ch, tot in self.multiw.get(r, {}).items():
                self._wait("pool", ch, tot)
        for w in writes:
            lw = self.lastw.get(w)
            if lw is not None:
                self._wait("pool", lw[0], lw[1])
            for ch, tot in self.readers.get(w, {}).items():
                self._wait("pool", ch, tot)
        ins = self.engs["pool"].collective_compute(kind, ALU.add, replica_groups=[[0, 1, 2, 3], [4, 5, 6, 7]],
                                                   ins=[in_t.ap().opt()], outs=[out_t.ap().opt()])
        self.ninstr += 1
        self.count["cc"] += 1
        ins.then_inc(self._sem("cc", 0))
        self.log["pool"].append(("inc", ("cc", 0), 1))
        self._record("cc", self.count["cc"], reads, writes)
        return ins

    def finish(self, eng):
        for ch, tot in self.outs.items():
            self._wait(eng, ch, tot)


class KB:
    def __init__(self):
        self.nc = bass.Bass("TRN2", target_bir_lowering=False)
        self.es = ExitStack()
        self.s = Sched(self.nc, self.es)
        self._n = 0

    def din(self, name, shape, dt=F32):
        return self.nc.dram_tensor(name, list(shape), dt, kind="ExternalInput").ap()

    def dout(self, name, shape, dt=F32):
        return self.nc.dram_tensor(name, list(shape), dt, kind="ExternalOutput").ap()

    def sb(self, name, shape, dt=F32):
        self._n += 1
        return self.es.enter_context(self.nc.sbuf_tensor("%s_u%d" % (name, self._n), list(shape), dt))

    def ps(self, name, shape=(128, 512), dt=F32):
        self._n += 1
        return self.es.enter_context(self.nc.psum_tensor("%s_u%d" % (name, self._n), list(shape), dt))

    @contextmanager
    def phase(self):
        old = self.es
        self.es = ExitStack()
        try:
            yield
        finally:
            self.es.close()
            self.es = old

    def dscr(self, name, shape, dt=F32):
        return self.nc.dram_tensor(name, list(shape), dt)

    def close(self):
        self.es.close()
        return self.nc


class Rot:
    def __init__(self, items):
        self.items = items
        self.i = 0

    def next(self):
        it = self.items[self.i % len(self.items)]
        self.i += 1
        return it


class WStream:
    def __init__(self, k, name, nslot, slot_elems, prefetch):
        self.k = k
        self.slots = [k.sb("%s_slot%d" % (name, i), [128, slot_elems], BF16) for i in range(nslot)]
        self.name = name
        self.nslot = nslot
        self.pf = prefetch
        self.loads = []
        self.issued = 0

    def add(self, fn):
        self.loads.append(fn)
        return len(self.loads) - 1

    def res(self, i):
        return "%s_w%d" % (self.name, i % self.nslot)

    def _issue(self, i):
        slot = self.slots[i % self.nslot]
        view, pairs = self.loads[i](slot)
        for (o, a) in pairs:
            self.k.s.dma("pool", "%s_c%d" % (self.name, i % self.nslot), o, a, [], [self.res(i)])
        return view

    def get(self, i):
        while self.issued < min(len(self.loads), i + 1 + self.pf):
            self._issue(self.issued)
            self.issued += 1
        slot = self.slots[i % self.nslot]
        view, _ = self.loads[i](slot)
        return view, self.res(i)


def v3(t, a, b):
    return t[:, 0:a * b].rearrange("p (a b) -> p a b", b=b)


def emit_norm(k, tag, x3, xres, W, aeff, sq3, sqres, ones_bf, ss_ps, ss_res, rstd, rstd_res, tmps, sink):
    s = k.s
    s.op("act", [xres], [sqres], lambda e: e.activation(out=sq3[:, 0:NKC, 0:W], in_=x3[:, 0:NKC, 0:W], func=AF.Square))
    for kc in range(NKC):
        s.op("pe", [sqres], [ss_res],
             lambda e, kc=kc: e.matmul(ss_ps[:, 0:W], ones_bf[:, :], sq3[:, kc, 0:W], start=(kc == 0), stop=(kc == NKC - 1)),
             sig=(kc == NKC - 1))
    s.op("act", [ss_res], [rstd_res],
         lambda e: e.activation(out=rstd[:, 0:W], in_=ss_ps[:, 0:W], func=AF.Sqrt, bias=float(D * EPS)))
    s.op("dve", [rstd_res], [rstd_res], lambda e: e.reciprocal(out=rstd[:, 0:W], in_=rstd[:, 0:W]))
    for kc in range(NKC):
        tres, tmp = tmps.next()
        s.op("dve", [xres, rstd_res], [tres],
             lambda e, kc=kc, tmp=tmp: e.scalar_tensor_tensor(out=tmp[:, 0:W], in0=x3[:, kc, 0:W], scalar=aeff[:, kc:kc + 1],
                                                              in1=rstd[:, 0:W], op0=ALU.mult, op1=ALU.mult))
        sink(kc, tmp, tres)


def emit_aeff(k, prm, cg, csc, aeff, res_in, res_out):
    s = k.s
    s.op("dve", [res_in], [res_out],
         lambda e: e.tensor_scalar(out=aeff[:, :], in0=prm[:, csc:csc + 16], scalar1=1.0, scalar2=float(math.sqrt(D)),
                                   op0=ALU.add, op1=ALU.mult))
    s.op("dve", [res_in, res_out], [res_out],
         lambda e: e.tensor_tensor(out=aeff[:, :], in0=aeff[:, :], in1=prm[:, cg:cg + 16], op=ALU.mult))


def emit_aeff2(k, sc_ap, g_ap, aeff, res_out):
    s = k.s
    s.op("dve", ["mods"], [res_out],
         lambda e: e.tensor_scalar(out=aeff[:, :], in0=sc_ap, scalar1=1.0, scalar2=float(math.sqrt(D)), op0=ALU.add, op1=ALU.mult))
    s.op("dve", ["gn", res_out], [res_out], lambda e: e.tensor_tensor(out=aeff[:, :], in0=aeff[:, :], in1=g_ap, op=ALU.mult))


class HSlots:
    def __init__(self, k, G, sh_ap, tag):
        self.k = k
        self.G = G
        s = k.s
        self.shm = k.sb("shm_" + tag, [128, 4, 16])
        for sl in range(4):
            s.op("dve", ["mods", "msk"], ["shm"],
                 lambda e, sl=sl: e.tensor_scalar(out=self.shm[:, sl, :], in0=sh_ap, scalar1=G["msk"][:, sl:sl + 1], scalar2=None, op0=ALU.mult))
        self.stg = Rot([("stg%d" % i, k.sb("stg%d_%s" % (i, tag), [128, 4, 2, 512], BF16)) for i in range(2)])
        self.cur = None

    def sink(self, t, kc, tmp, tres):
        s = self.k.s
        G = self.G
        if kc % 2 == 0:
            self.cur = self.stg.next()
        sres, st = self.cur
        for sl in range(4):
            s.op("act", [tres, "shm", "msk"], [sres],
                 lambda e, sl=sl, st=st: e.activation(out=st[:, sl, kc % 2, :], in_=tmp[:, :], func=AF.Identity,
                                                      bias=self.shm[:, sl, kc:kc + 1], scale=G["msk"][:, sl:sl + 1]))
        if kc % 2 == 1:
            hf, col = (t * 512) // G["PW"], (t * 512) % G["PW"]
            for sl in range(4):
                hi_ = sl * G["NH"] + hf
                hbv = G["HB"][hi_].ap().rearrange("(kc p) t -> p kc t", p=128)
                s.dma("sp", "sthb_" + sres, hbv[:, kc - 1:kc + 1, col:col + 512], st[:, sl, :, :], [sres], [], multi_w=["HB%d" % hi_])


def emit_e1(k, G, which="all"):
    for i in range(4 * G["NH"]):
        early = (G["NH"] == 2 and i % 2 == 0)
        if which == "all" or (which == "early") == early:
            k.s.collective("AllReduce", G["HB"][i], G["HG"][i], ["HB%d" % i], ["HG%d" % i])


def phase_m(k, G):
    s = k.s
    with k.phase():
        c_sb = k.sb("c_sb", [128, 16])
        sig = k.sb("sig", [128, 16])
        cact = k.sb("cact", [128, 16], BF16)
        mq = k.sb("mq", [128, 96])
        mq4 = k.sb("mq4", [128, 4, 96])
        bada = k.sb("bada_sb", [128, 384])
        mq_ps = k.ps("mq_ps")
        s.dma("sp", "ld", c_sb[:, :], G["cT"][:, :], [], ["c_sb"])
        s.dma("sp", "ld2", bada[:, :], G["bada"][:, :], [], ["bada"])
        s.dma("sp", "ld3", G["gn"][:, :], G["gains"][:, :], [], ["gn"])
        s.dma("sp", "ld4", G["msk"][:, :], G["msk_d"][:, :], [], ["msk"])
        s.op("dve", [], ["zero"], lambda e: e.memset(G["zero"][:, :], 0.0))
        s.op("act", ["c_sb"], ["sig"], lambda e: e.activation(out=sig[:, :], in_=c_sb[:, :], func=AF.Sigmoid))
        s.op("dve", ["c_sb", "sig"], ["cact"], lambda e: e.tensor_tensor(out=cact[:, :], in0=c_sb[:, :], in1=sig[:, :], op=ALU.mult))
        ws = WStream(k, "wm", 3, 8192, 2)
        for l in range(DEPTH):
            wv = G["wada"][l].rearrange("(kc p) n -> p kc n", p=128)
            for gq in range(6):
                ws.add(lambda slot, wv=wv, gq=gq: (v3(slot, 16, 512), [(v3(slot, 16, 512), wv[:, :, gq * 512:(gq + 1) * 512])]))
        i = 0
        for l in range(DEPTH):
            for gq in range(6):
                w3, wres = ws.get(i)
                i += 1
                for nn in range(4):
                    col = l * 24 + gq * 4 + nn
                    for kc in range(NKC):
                        s.op("pe", ["cact", wres], ["mq_ps"],
                             lambda e, kc=kc, w3=w3, nn=nn, col=col: e.matmul(mq_ps[:, col:col + 1], w3[:, kc, nn * 128:(nn + 1) * 128], cact[:, kc:kc + 1],
                                                                              start=(kc == 0), stop=(kc == NKC - 1)), sig=(kc == NKC - 1))
        s.op("act", ["mq_ps"], ["mq"], lambda e: e.mul(out=mq[:, :], in_=mq_ps[:, 0:96], mul=1.0))
        for sl in range(4):
            s.op("dve", ["mq", "msk"], ["mq4"],
                 lambda e, sl=sl: e.tensor_scalar(out=mq4[:, sl, :], in0=mq[:, :], scalar1=G["msk"][:, sl:sl + 1], scalar2=None, op0=ALU.mult))
        s.dma("sp", "stm", G["MB"].ap().rearrange("(s p) n -> p s n", p=128), mq4[:, :, :], ["mq4"], ["MB"])
        s.collective("AllReduce", G["MB"], G["MG"], ["MB"], ["MG"])
        mods_v = G["mods"][:, :].rearrange("p (l q i) -> p l q i", q=4, i=24)
        for q in range(4):
            s.dma("sp", "ldm%d" % q, mods_v[:, :, q, :], G["MG"].ap()[q * 128:(q + 1) * 128, :].rearrange("p (l i) -> p l i", i=24), ["MG"], ["mods"])
        s.op("dve", ["mods", "bada"], ["mods"], lambda e: e.tensor_tensor(out=G["mods"][:, :], in0=G["mods"][:, :], in1=bada[:, :], op=ALU.add))
        zerob = k.sb("zerob", [128, 16, 2], BF16)
        s.op("dve", [], ["zerob"], lambda e: e.memset(zerob[:, :, :], 0.0))
        s.dma("sp", "stz", G["YS"][0].ap()[0:D, 0:2].rearrange("(kc p) c -> p kc c", p=128), zerob[:, :, :], ["zerob"], ["YSz"])
        s.dma("sp", "stx0", G["XS"].ap()[:, :], G["xT"][:, :], [], ["XS"])
        s.barrier()


def phase_a(k, G, TOK):
    s = k.s
    NT = TOK // 512
    with k.phase():
        xv = G["xT"].rearrange("(kc p) t -> p kc t", p=128)
        aeff = k.sb("aeff", [128, 16])
        ones_bf = k.sb("ones_bf", [128, 128], BF16)
        xts = [("x%d" % i, k.sb("xa%d" % i, [128, 16, 512])) for i in range(2)]
        sq3 = k.sb("sqa", [128, 16, 512], BF16)
        rstd = k.sb("rstd", [128, 512])
        tmps = Rot([("tmp%d" % i, k.sb("tmpa%d" % i, [128, 512])) for i in range(3)])
        ss_ps = k.ps("ss_ps")
        s.op("dve", [], ["ones"], lambda e: e.memset(ones_bf[:, :], 1.0))
        M = G["mods"]
        emit_aeff2(k, M[:, 16:32], G["gn"][:, 0:16], aeff, "aeff")
        hs = HSlots(k, G, M[:, 0:16], "a")
        for t in range(NT):
            xres, x3 = xts[t % 2]
            s.dma("sp", "ldx%d" % (t % 2), x3[:, :, :], xv[:, :, 2 + t * 512:2 + (t + 1) * 512], [], [xres])
            emit_norm(k, "n", x3, xres, 512, aeff, sq3, "sq", ones_bf, ss_ps, "ss_ps", rstd, "rstd", tmps,
                      lambda kc, tmp, tres, t=t: hs.sink(t, kc, tmp, tres))
        s.barrier()
        emit_e1(k, G)


def phase_b(k, G, l, S):
    s = k.s
    NT = S // 512
    NB = S // 128
    TOK = S // 4
    with k.phase():
        wv = G["win"][l].rearrange("(kc p) n -> p kc n", p=128)
        ysvs = [ys.ap().rearrange("(j g r p) t -> j g p r t", j=4, g=4, p=128) for ys in G["YS"]]
        nh = len(G["YS"])
        win = k.sb("win_sb", [128, 16, 770], BF16)
        wvz = k.sb("wvz", [128, 16, 384], BF16)
        pwm = k.sb("pwm_sb", [128, 384], BF16)
        pmat = k.sb("pmat_sb", [128, 384], BF16)
        cst = k.sb("cst_sb", [128, 640])
        pb = k.sb("pb_sb", [128, 16])
        bfs = k.sb("bf_sb", [1, 2])
        nbf = k.sb("nbf", [1, 2])
        cA = k.sb("cA", [128, 2])
        lt = k.sb("lt", [128, 2])
        ones_bf = k.sb("ones_bf", [128, 128], BF16)
        ones2 = k.sb("ones2", [128, 128], BF16)
        ones_row = k.sb("ones_row", [1, 512])
        hts = [("ht%d" % i, k.sb("ht%d" % i, [128, 16, 512], BF16)) for i in range(2)]
        kT = k.sb("kT", [128, 2, S], BF16)
        vtok = k.sb("vtok", [128, NB, 256], BF16)
        zptok = k.sb("zptok", [128, 8, 128], BF16)
        qT = k.sb("qT", [128, 2, 512], BF16)
        FQ = k.sb("FQ", [128, 2, 512], BF16)
        Gk = k.sb("Gk", [128, 2, NB])
        Gq = [k.sb("Gq%d" % h, [1, 512]) for h in range(2)]
        Gc = k.sb("Gc", [1, 2])
        sp1 = k.sb("sp1", [1, 512])
        lsp = k.sb("lsp", [1, 512])
        hib = k.sb("hib", [1, 512], BF16)
        hif = lsp
        lo = sp1
        gnb = [k.sb("gnb%d" % h, [128, 512]) for h in range(2)]
        nlo = k.sb("nlo", [1, 512], BF16)
        pts = Rot([("pt%d" % i, k.sb("pt%d" % i, [128, 512], BF16)) for i in range(4)])
        dgs = Rot([("dg%d" % i, k.sb("dg%d" % i, [128, 512])) for i in range(1)])
        rden = k.sb("rden", [128, 512])
        yb = k.sb("yb", [128, 4, 512], BF16)
        yres = "yb"
        yms = Rot([("ym%d" % i, k.sb("ym%d" % i, [128, 4, 512], BF16)) for i in range(2)])
        zxb = k.sb("zxb", [128, 516])
        zy = k.sb("zy", [128, 512])
        dT = k.sb("dT", [128, 512], BF16)
        xc = k.sb("xc", [128, 512])
        xcb = k.sb("xcb", [128, 512], BF16)
        gr = k.sb("gr", [128, 512])
        gi = k.sb("gi", [128, 512])
        av = k.sb("av", [128, 512])
        a2 = k.sb("a2", [128, 512])
        mm_ = k.sb("mm", [128, 512])
        inp = k.sb("inp", [128, 512])
        hh = k.sb("hh", [128, 512])
        hc = k.sb("hc", [128, 1])
        uu = k.sb("uu", [128, 512])
        sg = k.sb("sg", [128, 512])
        pj = Rot([("pj%d" % i, k.ps("pj%d" % i)) for i in range(2)])
        sts = Rot([("st%d" % i, k.ps("st%d" % i)) for i in range(4)])
        o_one = k.ps("o_ps")
        d_one = k.ps("d_ps")
        o_ps = [o_one, o_one]
        d_ps = [d_one, d_one]
        ident = cst[:, 0:128]
        trif = cst[:, 128:640]
        msk = G["msk"]

        for q4 in range(4):
            s.dma("pool", "ldw", win[:, 4 * q4:4 * q4 + 4, 0:768], wv[:, 4 * q4:4 * q4 + 4, 0:768], [], ["win"])
        s.dma("pool", "ldw", win[:, :, 768:770], wv[:, :, C_F:C_F + 2], [], ["win"])
        s.dma("pool", "ldw4", wvz[:, :, :], wv[:, :, C_V:C_V + 384], [], ["wvz"])
        s.dma("pool", "ldw2", pwm[:, :], G["pwm"][l], [], ["pwm"])
        s.dma("pool", "ldw3", pmat[:, :], G["pmat"][:, :], [], ["pmat"])
        s.dma("sp", "ldc", cst[:, :], G["cst"][:, :], [], ["cst"])
        s.dma("sp", "ldc2", pb[:, :], G["pbp"][l], [], ["pb"])
        s.dma("sp", "ldc3", bfs[:, :], G["bfp"][l], [], ["bf"])
        s.op("dve", [], ["ones"], lambda e: e.memset(ones_bf[:, :], 1.0))
        s.op("dve", [], ["ones2"], lambda e: e.memset(ones2[:, :], 0.0))
        s.op("dve", ["ones2"], ["ones2"], lambda e: e.memset(ones2[0:2, :], 1.0))
        s.op("dve", [], ["ones_row"], lambda e: e.memset(ones_row[:, :], 1.0))
        s.op("dve", [], ["FQ0", "FQ1"], lambda e: e.memset(FQ[:, :, :], 0.0))
        s.op("dve", [], ["zxb"], lambda e: e.memset(zxb[:, 0:4], 0.0))
        s.op("dve", [], ["hc"], lambda e: e.memset(hc[:, :], 0.0))
        s.op("dve", [], ["Gc0", "Gc1"], lambda e: e.memset(Gc[:, :], 0.0))
        s.op("dve", ["bf"], ["nbf"], lambda e: e.tensor_scalar(out=nbf[:, :], in0=bfs[:, :], scalar1=-1.0, scalar2=None, op0=ALU.mult))
        s.op("act", ["pb"], ["lt"], lambda e: e.activation(out=lt[:, 0:1], in_=pb[:, 8:9], func=AF.Exp, scale=-1.0))
        s.op("act", ["lt"], ["lt"], lambda e: e.activation(out=lt[:, 1:2], in_=lt[:, 0:1], func=AF.Ln, bias=1.0))
        s.op("dve", ["lt"], ["cA"], lambda e: e.tensor_scalar(out=cA[:, 0:1], in0=lt[:, 1:2], scalar1=-8.0, scalar2=None, op0=ALU.mult))
        s.op("dve", ["lt", "cA"], ["cA"], lambda e: e.tensor_scalar(out=cA[:, 1:2], in0=lt[:, 1:2], scalar1=-16.0, scalar2=None, op0=ALU.mult))

        QSCALE = float(128 ** -0.5)

        def fm_proj(col, ht3, hres):
            pres, ps = pj.next()
            for kc in range(NKC):
                s.op("pe", [hres, "win"], [pres],
                     lambda e, kc=kc, ps=ps: e.matmul(ps[:, :], win[:, kc, col:col + 128], ht3[:, kc, :], start=(kc == 0), stop=(kc == NKC - 1)),
                     sig=(kc == NKC - 1))
            return pres, ps

        tps = TOK // 512

        def load_h(T):
            hres, ht3 = hts[T % 2]
            sl_, tt = T // tps, T % tps
            hgi = sl_ * G["NH"] + (tt * 512) // G["PW"]
            hcol = (tt * 512) % G["PW"]
            hgv = G["HG"][hgi].ap().rearrange("(kc p) t -> p kc t", p=128)
            s.dma("sp", "ldh%d" % (T % 2), ht3[:, :, :], hgv[:, :, hcol:hcol + 512], ["HG%d" % hgi], [hres])

        load_h(0)
        for T in range(NT):
            t0 = T * 512
            hres, ht3 = hts[T % 2]
            if T + 1 < NT:
                load_h(T + 1)

            tp_res, tp_ps = pj.next()
            f_pss = []
            for h in range(2):
                pres, ps = sts.next()
                for kc in range(NKC):
                    s.op("pe", [hres, "win"], [pres],
                         lambda e, kc=kc, ps=ps, h=h: e.matmul(ps[0:1, :], win[:, kc, 768 + h:769 + h], ht3[:, kc, :],
                                                               start=(kc == 0), stop=(kc == NKC - 1)), sig=(kc == NKC - 1))
                f_pss.append((pres, ps))
            for h in range(2):
                pres, ps = f_pss[h]
                s.op("act", [pres, "nbf"], ["sp1"],
                     lambda e, ps=ps, h=h: e.activation(out=sp1[:, :], in_=ps[0:1, :], func=AF.Exp, bias=nbf[0:1, h:h + 1], scale=-1.0))
                s.op("act", ["sp1"], ["lsp"], lambda e: e.activation(out=lsp[:, :], in_=sp1[:, :], func=AF.Ln, bias=1.0))
                s.op("dve", ["lsp", "ones_row", "Gc%d" % h], ["Gq%d" % h],
                     lambda e, h=h: e.tensor_tensor_scan(out=Gq[h][:, :], data0=ones_row[:, :], data1=lsp[:, :], initial=Gc[0:1, h:h + 1],
                                                         op0=ALU.mult, op1=ALU.add))
                s.op("dve", ["Gq%d" % h], ["Gc%d" % h], lambda e, h=h: e.tensor_copy(out=Gc[0:1, h:h + 1], in_=Gq[h][:, 511:512]))
                s.op("dve", ["Gq%d" % h], ["hib"], lambda e, h=h: e.tensor_copy(out=hib[:, :], in_=Gq[h][:, :]))
                s.op("dve", ["hib", "lsp"], ["lsp"], lambda e: e.tensor_copy(out=hif[:, :], in_=hib[:, :]))
                s.op("dve", ["Gq%d" % h, "lsp", "sp1"], ["sp1"], lambda e, h=h: e.tensor_tensor(out=lo[:, :], in0=Gq[h][:, :], in1=hif[:, :], op=ALU.subtract))
                s.op("dve", ["lsp"], ["FQ%d" % h],
                     lambda e, h=h: e.tensor_scalar(out=FQ[0:1, h, :], in0=hif[:, :], scalar1=-1.0, scalar2=None, op0=ALU.mult))
                s.op("dve", ["sp1"], ["nlo"], lambda e: e.tensor_scalar(out=nlo[:, :], in0=lo[:, :], scalar1=-1.0, scalar2=None, op0=ALU.mult))
                s.dma("sp", "fq%d" % h, FQ[1:2, h, :], nlo[0:1, :], ["nlo"], ["FQ%d" % h])
                gres_, gps_ = sts.next()
                s.op("pe", ["ones2", "FQ%d" % h], [gres_], lambda e, h=h, gps_=gps_: e.matmul(gps_[:, :], ones2[:, :], FQ[:, h, :], start=True, stop=True))
                s.op("act", [gres_], ["gnb%d" % h], lambda e, h=h, gps_=gps_: e.mul(out=gnb[h][:, :], in_=gps_[:, :], mul=1.0))
                for jb in range(4):
                    s.op("pe", ["Gq%d" % h, "cst"], [tp_res],
                         lambda e, h=h, jb=jb: e.transpose(tp_ps[:, h * 4 + jb:h * 4 + jb + 1], Gq[h][0:1, jb * 128:(jb + 1) * 128], ident[0:1, 0:1]),
                         sig=(jb == 3))
                s.op("dve", [tp_res], ["Gk"], lambda e, h=h, T=T: e.tensor_copy(out=Gk[:, h, 4 * T:4 * T + 4], in_=tp_ps[:, h * 4:h * 4 + 4]))

            for h in range(2):
                pres, ps = fm_proj(C_Q + h * 128, ht3, hres)
                s.op("act", [pres], ["qT%d" % h], lambda e, ps=ps, h=h: e.mul(out=qT[:, h, :], in_=ps[:, :], mul=QSCALE))
            for h in range(2):
                pres, ps = fm_proj(C_K + h * 128, ht3, hres)
                s.op("dve", [pres], ["kT"], lambda e, ps=ps, h=h: e.tensor_copy(out=kT[:, h, t0:t0 + 512], in_=ps[:, :]))
            pres, ps = fm_proj(C_ZX, ht3, hres)
            s.op("act", [pres], ["zxb"], lambda e, ps=ps: e.mul(out=zxb[:, 4:516], in_=ps[:, :], mul=1.0))
            pres, ps = fm_proj(C_ZY, ht3, hres)
            s.op("act", [pres], ["zy"], lambda e, ps=ps: e.mul(out=zy[:, :], in_=ps[:, :], mul=1.0))
            for jb in range(4):
                blk = 4 * T + jb
                pres, ps = pj.next()
                for kc in range(NKC):
                    s.op("pe", [hres, "wvz"], [pres],
                         lambda e, kc=kc, ps=ps, jb=jb: e.matmul(ps[:, 0:384], ht3[:, kc, jb * 128:(jb + 1) * 128], wvz[:, kc, :],
                                                                 start=(kc == 0), stop=(kc == NKC - 1)), sig=(kc == NKC - 1))
                s.op("act", [pres], ["vtok"], lambda e, ps=ps, blk=blk: e.mul(out=vtok[:, blk, :], in_=ps[:, 0:256], mul=1.0))
                s.op("act", [pres], ["zptok"], lambda e, ps=ps, blk=blk: e.mul(out=zptok[:, blk % 8, :], in_=ps[:, 256:384], mul=1.0))

            s.op("dve", ["zxb", "pb"], ["xc"],
                 lambda e: e.tensor_scalar(out=xc[:, :], in0=zxb[:, 1:513], scalar1=pb[:, 1:2], scalar2=pb[:, 5:6], op0=ALU.mult, op1=ALU.add))
            for kk in range(1, 4):
                s.op("dve", ["zxb", "pb", "xc"], ["xc"],
                     lambda e, kk=kk: e.scalar_tensor_tensor(out=xc[:, :], in0=zxb[:, 1 + kk:513 + kk], scalar=pb[:, 1 + kk:2 + kk], in1=xc[:, :],
                                                             op0=ALU.mult, op1=ALU.add))
            s.op("dve", ["zxb"], ["zxb"], lambda e: e.tensor_copy(out=zxb[:, 1:4], in_=zxb[:, 513:516]))
            s.op("act", ["xc"], ["xcb"], lambda e: e.mul(out=xcb[:, :], in_=xc[:, :], mul=1.0))

            nkb = 4 * T + 4
            blocks = [(h, kb) for h in range(2) for kb in range(nkb)]
            info = {}

            def emit_qk(i):
                h, kb = blocks[i]
                jj = kb - 4 * T
                c0 = jj * 128 if jj >= 0 else 0
                stres, st = sts.next()
                ptres, pt = pts.next()
                info[i] = (ptres, pt, c0)
                s.op("pe", ["kT", "qT%d" % h], [stres],
                     lambda e: e.matmul(st[:, c0:512], kT[:, h, kb * 128:(kb + 1) * 128], qT[:, h, c0:512], start=True, stop=True))
                s.op("dve", [stres, "gnb%d" % h], [stres],
                     lambda e: e.tensor_tensor(out=st[:, c0:512], in0=st[:, c0:512], in1=gnb[h][:, c0:512], op=ALU.add))
                if jj >= 0:
                    dgres, dg = dgs.next()
                    n = 512 - c0
                    s.op("dve", [stres, "cst"], [dgres],
                         lambda e: e.tensor_tensor(out=dg[:, 0:n], in0=st[:, c0:512], in1=trif[:, 0:n], op=ALU.add))
                    s.op("act", [dgres, "Gk"], [ptres],
                         lambda e: e.activation(out=pt[:, c0:512], in_=dg[:, 0:n], func=AF.Exp, bias=Gk[:, h, kb:kb + 1]))
                else:
                    s.op("act", [stres, "Gk"], [ptres],
                         lambda e: e.activation(out=pt[:, :], in_=st[:, :], func=AF.Exp, bias=Gk[:, h, kb:kb + 1]))

            def emit_pv(i):
                h, kb = blocks[i]
                ptres, pt, c0 = info.pop(i)
                last = (kb == nkb - 1)
                s.op("pe", ["vtok", ptres], ["o_ps"],
                     lambda e: e.matmul(o_ps[h][:, c0:512], vtok[:, kb, h * 128:(h + 1) * 128], pt[:, c0:512], start=(kb == 0), stop=last), sig=False)
                s.op("pe", ["ones", ptres], ["d_ps"],
                     lambda e: e.matmul(d_ps[h][:, c0:512], ones_bf[:, :], pt[:, c0:512], start=(kb == 0), stop=last))
                if last:
                    s.op("dve", ["d_ps"], ["rden"], lambda e: e.reciprocal(out=rden[:, :], in_=d_ps[h][:, :]))
                    s.op("dve", ["o_ps", "d_ps", "rden"], [yres],
                         lambda e: e.tensor_tensor(out=yb[:, 1 + h, :], in0=o_ps[h][:, :], in1=rden[:, :], op=ALU.mult))

            LA = 3
            for i in range(min(LA, len(blocks))):
                emit_qk(i)
            for i in range(len(blocks)):
                if i + LA < len(blocks):
                    emit_qk(i + LA)
                emit_pv(i)

            pres, ps = pj.next()
            for jb in range(4):
                blk = 4 * T + jb
                pc0 = 0 if blk == 0 else 128
                s.op("pe", ["zptok", "pmat"], [pres],
                     lambda e, ps=ps, jb=jb, blk=blk, pc0=pc0: e.matmul(ps[:, jb * 128:(jb + 1) * 128], zptok[:, blk % 8, :], pmat[:, pc0:pc0 + 128],
                                                                        start=True, stop=(blk == 0)), sig=(blk == 0))
                if blk > 0:
                    s.op("pe", ["zptok", "pmat"], [pres],
                         lambda e, ps=ps, jb=jb, blk=blk: e.matmul(ps[:, jb * 128:(jb + 1) * 128], zptok[:, (blk - 1) % 8, :], pmat[:, 256:384],
                                                                   start=False, stop=True), sig=True)
            s.op("act", [pres], ["dT"], lambda e, ps=ps: e.mul(out=dT[:, :], in_=ps[:, :], mul=1.0))
            pres2, ps2 = pj.next()
            s.op("pe", ["dT", "pwm"], [pres2], lambda e, ps2=ps2: e.matmul(ps2[:, :], pwm[:, 0:128], dT[:, :], start=True, stop=True))
            s.op("dve", [pres2, "pb"], [yres],
                 lambda e, ps2=ps2: e.tensor_scalar(out=yb[:, 0, :], in0=ps2[:, :], scalar1=pb[:, 0:1], scalar2=None, op0=ALU.mult))

            rres, rps = pj.next()
            s.op("pe", ["xcb", "pwm"], [rres], lambda e, rps=rps: e.matmul(rps[:, :], pwm[:, 128:256], xcb[:, :], start=True, stop=True))
            ires, ips = pj.next()
            s.op("pe", ["xcb", "pwm"], [ires], lambda e, ips=ips: e.matmul(ips[:, :], pwm[:, 256:384], xcb[:, :], start=True, stop=True))
            s.op("act", [rres, "pb"], ["gr"], lambda e, rps=rps: e.activation(out=gr[:, :], in_=rps[:, :], func=AF.Sigmoid, bias=pb[:, 6:7]))
            s.op("act", [ires, "pb"], ["gi"], lambda e, ips=ips: e.activation(out=gi[:, :], in_=ips[:, :], func=AF.Sigmoid, bias=pb[:, 7:8]))
            s.op("act", ["gr", "cA"], ["av"], lambda e: e.activation(out=av[:, :], in_=gr[:, :], func=AF.Exp, scale=cA[:, 0:1]))
            s.op("act", ["gr", "cA"], ["a2"], lambda e: e.activation(out=a2[:, :], in_=gr[:, :], func=AF.Exp, scale=cA[:, 1:2]))
            s.op("dve", ["a2"], ["mm"], lambda e: e.tensor_scalar(out=mm_[:, :], in0=a2[:, :], scalar1=-1.0, scalar2=1.0, op0=ALU.mult, op1=ALU.add))
            s.op("dve", ["mm"], ["mm"], lambda e: e.tensor_scalar(out=mm_[:, :], in0=mm_[:, :], scalar1=1e-30, scalar2=None, op0=ALU.max))
            s.op("act", ["mm"], ["mm"], lambda e: e.activation(out=mm_[:, :], in_=mm_[:, :], func=AF.Sqrt))
            s.op("dve", ["gi", "xc"], ["inp"], lambda e: e.tensor_tensor(out=inp[:, :], in0=gi[:, :], in1=xc[:, :], op=ALU.mult))
            s.op("dve", ["inp", "mm"], ["inp"], lambda e: e.tensor_tensor(out=inp[:, :], in0=inp[:, :], in1=mm_[:, :], op=ALU.mult))
            s.op("dve", ["av", "inp", "hc"], ["hh"],
                 lambda e: e.tensor_tensor_scan(out=hh[:, :], data0=av[:, :], data1=inp[:, :], initial=hc[:, 0:1], op0=ALU.mult, op1=ALU.add))
            s.op("dve", ["hh"], ["hc"], lambda e: e.tensor_copy(out=hc[:, :], in_=hh[:, 511:512]))
            s.op("dve", ["zy"], ["uu"], lambda e: e.tensor_tensor(out=uu[:, :], in0=zy[:, :], in1=zy[:, :], op=ALU.mult))
            s.op("dve", ["uu"], ["uu"], lambda e: e.tensor_scalar(out=uu[:, :], in0=uu[:, :], scalar1=0.044715, scalar2=1.0, op0=ALU.mult, op1=ALU.add))
            s.op("dve", ["uu", "zy"], ["uu"], lambda e: e.tensor_tensor(out=uu[:, :], in0=uu[:, :], in1=zy[:, :], op=ALU.mult))
            s.op("act", ["uu"], ["sg"], lambda e: e.activation(out=sg[:, :], in_=uu[:, :], func=AF.Sigmoid, scale=1.5957691216057308))
            s.op("dve", ["sg", "zy"], ["sg"], lambda e: e.tensor_tensor(out=sg[:, :], in0=sg[:, :], in1=zy[:, :], op=ALU.mult))
            s.op("dve", ["sg", "hh"], [yres], lambda e: e.tensor_tensor(out=yb[:, 3, :], in0=sg[:, :], in1=hh[:, :], op=ALU.mult))

            j = T // tps
            tt = T % tps
            if nh == 2 and tt >= tps // 2:
                half, cbase = 1, (tt - tps // 2) * 512
            else:
                half, cbase = 0, 2 + tt * 512
            for g2 in range(4):
                ymres, ym = yms.next()
                s.op("dve", [yres, "msk"], [ymres],
                     lambda e, ym=ym, g2=g2: e.tensor_scalar(out=ym[:, :, :], in0=yb[:, :, :], scalar1=msk[:, g2:g2 + 1], scalar2=None, op0=ALU.mult))
                s.dma("sp", "sty_" + ymres, ysvs[half][j, g2, :, :, cbase:cbase + 512], ym[:, :, :], [ymres], [], multi_w=["YS%d" % half])
                if tt == tps - 1 and j < 3:
                    s.dma("sp", "sty_" + ymres, ysvs[0][j + 1, g2, :, :, 0:2], ym[:, :, 510:512], [ymres], [], multi_w=["YS0"])
            if nh == 2 and T == 3 * tps + tps // 2 - 1:
                s.collective("ReduceScatter", G["YS"][0], G["YR"][0], ["YS0"], ["YR0"])
        if nh == 1:
            s.collective("ReduceScatter", G["YS"][0], G["YR"][0], ["YS0"], ["YR0"])
            s.barrier()
        else:
            s.barrier()
            s.collective("ReduceScatter", G["YS"][1], G["YR"][1], ["YS1"], ["YR1"])


def phase_c(k, G, l, TOK):
    s = k.s
    NT = TOK // 512
    final = (l == DEPTH - 1)
    with k.phase():
        xv = G["XS"].ap().rearrange("(kc p) t -> p kc t", p=128)
        yvs = [yr.ap().rearrange("(kc p) t -> p kc t", p=128) for yr in G["YR"]]
        nh = len(yvs)
        ov = G["out"].rearrange("(kc p) t -> p kc t", p=128)
        wov = G["w_out"][l].rearrange("(kc p) n -> p kc n", p=128)
        wgv = G["w_gate"][l].rearrange("(kc p) n -> p kc n", p=128)
        wuv = G["w_up"][l].rearrange("(kc p) n -> p kc n", p=128)
        w_down = G["w_down"][l]
        M = G["mods"]
        GN = G["gn"]
        msk = G["msk"]
        mo = l * 96
        gt1 = M[:, mo + 32:mo + 48]
        sh2 = M[:, mo + 48:mo + 64]
        sc2 = M[:, mo + 64:mo + 80]
        gt2 = M[:, mo + 80:mo + 96]
        if not final:
            g_n = GN[:, (l + 1) * 16:(l + 2) * 16]
            sh_n = M[:, mo + 96:mo + 112]
            sc_n = M[:, mo + 112:mo + 128]
        else:
            g_n = GN[:, 128:144]
            sh_n = G["zero"][:, 0:16]
            sc_n = G["zero"][:, 0:16]

        cv = k.sb("cv_sb", [128, NJ * 4])
        aeff2 = k.sb("aeff2", [128, 16])
        aeffn = k.sb("aeffn", [128, 16])
        ones_bf = k.sb("ones_bf", [128, 128], BF16)
        x3 = k.sb("x3", [128, 16, 512])
        y3 = k.sb("y3", [128, 16, 512], BF16)
        h23 = y3
        act3 = k.sb("act3", [128, NJ, 512], BF16)
        sq3 = act3
        xh = k.sb("xh", [128, 16, 2])
        yh = k.sb("yh", [128, 16, 2], BF16)
        h2h = k.sb("h2h", [128, 16, 2], BF16)
        gprev = k.sb("gprev", [128, NJ, 2])
        xl4 = k.sb("xl4", [128, 4, 16, 2])
        xg4 = k.sb("xg4", [128, 4, 16, 2])
        gbuf = [("gbuf%d" % i, k.sb("gbuf%d" % i, [128, 516])) for i in range(2)]
        acc = [("acc%d" % i, k.sb("acc%d" % i, [128, 512])) for i in range(2)]
        sil = [("sil%d" % i, k.sb("sil%d" % i, [128, 512])) for i in range(2)]
        rstd = k.sb("rstd", [128, 512])
        tmps = Rot([("tmp%d" % i, k.sb("tmp%d" % i, [128, 512])) for i in range(3)])
        g_ps = [("g_ps%d" % i, k.ps("g_ps%d" % i)) for i in range(2)]
        u_ps = [("u_ps%d" % i, k.ps("u_ps%d" % i)) for i in range(2)]
        m_ps = Rot([("m_ps%d" % i, k.ps("m_ps%d" % i)) for i in range(3)])
        ss_ps = k.ps("ss_ps")
        if final:
            hst = Rot([("hst%d" % i, k.sb("hst%d" % i, [128, 512])) for i in range(2)])
            hs = None
        else:
            hs = HSlots(k, G, sh_n, "c")

        split_e1 = (not final) and G["NH"] == 2 and NT == 4
        s.dma("sp", "ldp2", cv[:, :], G["cv"][l], [], ["cv"])
        s.op("dve", [], ["ones"], lambda e: e.memset(ones_bf[:, :], 1.0))
        emit_aeff2(k, sc2, GN[:, 64 + l * 16:80 + l * 16], aeff2, "aeff2")
        emit_aeff2(k, sc_n, g_n, aeffn, "aeffn")

        ws = WStream(k, "w", 4, 8192, 2)
        plan = {}

        def add_out(n):
            return ws.add(lambda slot: (v3(slot, 16, 512), [(v3(slot, 16, 512), wov[:, :, n * 512:(n + 1) * 512])]))

        def add_gu(wview, jg):
            return ws.add(lambda slot: (v3(slot, 16, 512), [(v3(slot, 16, 512), wview[:, :, jg * 512:(jg + 1) * 512])]))

        def add_down(m):
            return ws.add(lambda slot: (v3(slot, NJ, 128), [(slot[:, 0:NJ * 128], w_down[m, :, :])]))

        for t in range(NT):
            for n in range(4):
                plan[(t, "out", n)] = add_out(n)
            for jg in range(NJ // 4):
                plan[(t, "g", jg)] = add_gu(wgv, jg)
                plan[(t, "u", jg)] = add_gu(wuv, jg)
            for m in range(16):
                plan[(t, "d", m)] = add_down(m)

        def outproj(keyfn, segs):
            for n in range(4):
                w3, wres = ws.get(plan[keyfn(n)])
                for mm in range(4):
                    m = n * 4 + mm
                    for (yy3, yres, xx3, xres, W) in segs:
                        pres, ps = m_ps.next()
                        for kc in range(NKC):
                            s.op("pe", [yres, wres], [pres],
                                 lambda e, kc=kc, w3=w3, ps=ps, mm=mm, yy3=yy3, W=W: e.matmul(ps[:, 0:W], w3[:, kc, mm * 128:(mm + 1) * 128], yy3[:, kc, 0:W],
                                                                                              start=(kc == 0), stop=(kc == NKC - 1)), sig=(kc == NKC - 1))
                        s.op("dve", [pres, xres, "mods"], [xres],
                             lambda e, m=m, ps=ps, xx3=xx3, W=W: e.scalar_tensor_tensor(out=xx3[:, m, 0:W], in0=ps[:, 0:W], scalar=gt1[:, m:m + 1],
                                                                                        in1=xx3[:, m, 0:W], op0=ALU.mult, op1=ALU.add))

        s.dma("sp", "ldh", xh[:, :, :], xv[:, :, 0:2], ["XS"], ["xh"])
        s.dma("sp", "ldh2", yh[:, :, :], yvs[0][:, :, 0:2], ["YR0"], ["yh"])

        def sink_h(kc, tmp, tres):
            s.op("act", [tres, "mods"], ["h2h"],
                 lambda e: e.activation(out=h2h[:, kc, :], in_=tmp[:, 0:2], func=AF.Identity, bias=sh2[:, kc:kc + 1]))

        for t in range(NT):
            c0 = 2 + t * 512
            if nh == 2 and t >= NT // 2:
                yh_i, yc0 = 1, (t - NT // 2) * 512
            else:
                yh_i, yc0 = 0, 2 + t * 512
            s.dma("sp", "ldx", x3[:, :, :], xv[:, :, c0:c0 + 512], ["XS"], ["x3"])
            if t == 0:
                s.dma("sp", "ldy", y3[:, :, :], yvs[yh_i][:, :, yc0:yc0 + 512], ["YR%d" % yh_i], ["y3"])
            segs = [(y3, "y3", x3, "x3", 512)]
            if t == 0:
                segs = [(yh, "yh", xh, "xh", 2)] + segs
            outproj(lambda n, t=t: (t, "out", n), segs)
            if split_e1 and t == NT // 2:
                emit_e1(k, G, "early")
            if t == 0:
                emit_norm(k, "nh", xh, "xh", 2, aeff2, sq3, "act3", ones_bf, ss_ps, "ss_ps", rstd, "rstd", tmps, sink_h)

            def sink2(kc, tmp, tres):
                s.op("act", [tres, "mods"], ["y3"],
                     lambda e: e.activation(out=h23[:, kc, :], in_=tmp[:, :], func=AF.Identity, bias=sh2[:, kc:kc + 1]))

            emit_norm(k, "n2", x3, "x3", 512, aeff2, sq3, "act3", ones_bf, ss_ps, "ss_ps", rstd, "rstd", tmps, sink2)

            for jg in range(NJ // 4):
                wg3, wgres = ws.get(plan[(t, "g", jg)])
                wu3, wures = ws.get(plan[(t, "u", jg)])
                for jj in range(4):
                    j = jg * 4 + jj
                    gres, gp = g_ps[j % 2]
                    ures, up = u_ps[j % 2]
                    bres, gb = gbuf[j % 2]
                    ares, ac = acc[j % 2]
                    sres, sl = sil[j % 2]
                    if t == 0:
                        pres, ps = m_ps.next()
                        for kc in range(NKC):
                            s.op("pe", ["h2h", wgres], [pres],
                                 lambda e, kc=kc, wg3=wg3, ps=ps, jj=jj: e.matmul(ps[:, 0:2], wg3[:, kc, jj * 128:(jj + 1) * 128], h2h[:, kc, :],
                                                                                  start=(kc == 0), stop=(kc == NKC - 1)), sig=(kc == NKC - 1))
                        s.op("dve", [pres, "msk"], ["gprev"],
                             lambda e, j=j, ps=ps: e.tensor_scalar(out=gprev[:, j, :], in0=ps[:, 0:2], scalar1=msk[:, 8:9], scalar2=None, op0=ALU.mult))
                    for kc in range(NKC):
                        s.op("pe", ["y3", wgres], [gres],
                             lambda e, kc=kc, gp=gp, jj=jj, wg3=wg3: e.matmul(gp[:, :], wg3[:, kc, jj * 128:(jj + 1) * 128], h23[:, kc, :],
                                                                              start=(kc == 0), stop=(kc == NKC - 1)), sig=(kc == NKC - 1))
                    for kc in range(NKC):
                        s.op("pe", ["y3", wures], [ures],
                             lambda e, kc=kc, up=up, jj=jj, wu3=wu3: e.matmul(up[:, :], wu3[:, kc, jj * 128:(jj + 1) * 128], h23[:, kc, :],
                                                                              start=(kc == 0), stop=(kc == NKC - 1)), sig=(kc == NKC - 1))
                    s.op("act", [gres], [bres], lambda e, gb=gb, gp=gp: e.mul(out=gb[:, 4:516], in_=gp[:, :], mul=1.0))
                    s.op("dve", ["gprev"], [bres], lambda e, gb=gb, j=j: e.tensor_copy(out=gb[:, 2:4], in_=gprev[:, j, :]))
                    s.op("dve", [bres], ["gprev"], lambda e, gb=gb, j=j: e.tensor_copy(out=gprev[:, j, :], in_=gb[:, 514:516]))
                    s.op("dve", [bres, "cv"], [ares],
                         lambda e, gb=gb, ac=ac, j=j: e.tensor_scalar(out=ac[:, :], in0=gb[:, 2:514], scalar1=cv[:, 4 * j:4 * j + 1],
                                                                      scalar2=cv[:, 4 * j + 3:4 * j + 4], op0=ALU.mult, op1=ALU.add))
                    s.op("dve", [bres, "cv", ares], [ares],
                         lambda e, gb=gb, ac=ac, j=j: e.scalar_tensor_tensor(out=ac[:, :], in0=gb[:, 3:515], scalar=cv[:, 4 * j + 1:4 * j + 2],
                                                                             in1=ac[:, :], op0=ALU.mult, op1=ALU.add))
                    s.op("dve", [bres, "cv", ares], [ares],
                         lambda e, gb=gb, ac=ac, j=j: e.scalar_tensor_tensor(out=ac[:, :], in0=gb[:, 4:516], scalar=cv[:, 4 * j + 2:4 * j + 3],
                                                                             in1=ac[:, :], op0=ALU.mult, op1=ALU.add))
                    s.op("act", [ares], [sres], lambda e, ac=ac, sl=sl: e.activation(out=sl[:, :], in_=ac[:, :], func=AF.Silu))
                    s.op("dve", [sres, ures], ["act3"],
                         lambda e, sl=sl, up=up, j=j: e.tensor_tensor(out=act3[:, j, :], in0=sl[:, :], in1=up[:, :], op=ALU.mult))

            if t + 1 < NT:
                if nh == 2 and t + 1 >= NT // 2:
                    nyh, nyc = 1, (t + 1 - NT // 2) * 512
                else:
                    nyh, nyc = 0, 2 + (t + 1) * 512
                s.dma("sp", "ldy", y3[:, :, :], yvs[nyh][:, :, nyc:nyc + 512], ["YR%d" % nyh], ["y3"])
            for m in range(16):
                wd3, wdres = ws.get(plan[(t, "d", m)])
                pres, ps = m_ps.next()
                for j in range(NJ):
                    s.op("pe", ["act3", wdres], [pres],
                         lambda e, j=j, wd3=wd3, ps=ps: e.matmul(ps[:, :], wd3[:, j, :], act3[:, j, :], start=(j == 0), stop=(j == NJ - 1)),
                         sig=(j == NJ - 1))
                s.op("dve", [pres, "x3", "mods"], ["x3"],
                     lambda e, m=m, ps=ps: e.scalar_tensor_tensor(out=x3[:, m, :], in0=ps[:, :], scalar=gt2[:, m:m + 1],
                                                                  in1=x3[:, m, :], op0=ALU.mult, op1=ALU.add))
            if not final:
                s.dma("sp", "stx", xv[:, :, c0:c0 + 512], x3[:, :, :], ["x3"], ["XS"])
                if t == NT - 1:
                    for sl_ in range(4):
                        s.op("dve", ["x3", "msk"], ["xl4"],
                             lambda e, sl_=sl_: e.tensor_scalar(out=xl4[:, sl_, :, :], in0=x3[:, :, 510:512], scalar1=msk[:, 4 + sl_:5 + sl_], scalar2=None,
                                                                op0=ALU.mult))
                    s.dma("sp", "stxl", G["XL"].ap().rearrange("(s kc p) c -> p s kc c", s=4, p=128), xl4[:, :, :, :], ["xl4"], ["XL"])

            if final:
                def sinkn(kc, tmp, tres, t=t):
                    hres, hsb_ = hst.next()
                    s.op("act", [tres, "zero"], [hres],
                         lambda e: e.activation(out=hsb_[:, :], in_=tmp[:, :], func=AF.Identity, bias=sh_n[:, kc:kc + 1]))
                    s.dma("sp", "sth_" + hres, ov[:, kc, t * 512:(t + 1) * 512], hsb_[:, :], [hres], [], is_out=True)
            else:
                def sinkn(kc, tmp, tres, t=t):
                    hs.sink(t, kc, tmp, tres)

            emit_norm(k, "nn", x3, "x3", 512, aeffn, sq3, "act3", ones_bf, ss_ps, "ss_ps", rstd, "rstd", tmps, sinkn)

        if not final:
            s.collective("AllReduce", G["XL"], G["XG"], ["XL"], ["XG"])
            s.dma("sp", "ldxg", xg4[:, :, :, :], G["XG"].ap().rearrange("(s kc p) c -> p s kc c", s=4, p=128), ["XG"], ["xg4"])
            s.op("dve", ["xg4", "msk"], ["xh"],
                 lambda e: e.tensor_scalar(out=xh[:, :, :], in0=xg4[:, 0, :, :], scalar1=msk[:, 0:1], scalar2=None, op0=ALU.mult))
            for sl_ in range(1, 4):
                s.op("dve", ["xg4", "msk", "xh"], ["xh"],
                     lambda e, sl_=sl_: e.scalar_tensor_tensor(out=xh[:, :, :], in0=xg4[:, sl_, :, :], scalar=msk[:, sl_:sl_ + 1], in1=xh[:, :, :],
                                                               op0=ALU.mult, op1=ALU.add))
            s.dma("sp", "stxh", xv[:, :, 0:2], xh[:, :, :], ["xh"], ["XS"])
        s.barrier()
        if not final:
            emit_e1(k, G, "late" if split_e1 else "all")


def build_fused(S):
    k = KB()
    TOK = S // 4
    G = {}
    G["xT"] = k.din("xT", [D, TOK + 2])
    G["cT"] = k.din("cT", [128, 16])
    G["wada"] = k.din("wada", [DEPTH, D, 3072])
    G["bada"] = k.din("bada", [128, 384])
    G["gains"] = k.din("gains", [128, 144])
    G["msk_d"] = k.din("msk", [128, 9])
    G["win"] = k.din("win", [DEPTH, D, NIN_G])
    G["pwm"] = k.din("pwm", [DEPTH, 128, 384])
    G["pmat"] = k.din("pmat", [128, 384])
    G["cst"] = k.din("cst", [128, 640])
    G["pbp"] = k.din("pbp", [DEPTH, 128, 16])
    G["bfp"] = k.din("bfp", [DEPTH, 1, 2])
    G["w_out"] = k.din("w_out", [DEPTH, D, D])
    G["w_gate"] = k.din("w_gate", [DEPTH, D, DFF])
    G["w_up"] = k.din("w_up", [DEPTH, D, DFF])
    G["w_down"] = k.din("w_down", [DEPTH, 16, 128, NJ * 128])
    G["cv"] = k.din("cv", [DEPTH, 128, NJ * 4])
    G["out"] = k.dout("out", [D, TOK])
    G["XS"] = k.dscr("XS", [D, TOK + 2], F32)
    G["NH"] = max(1, TOK // 1024)
    G["PW"] = TOK // G["NH"]
    G["HB"] = [k.dscr("HB%d" % i, [D, G["PW"]], BF16) for i in range(4 * G["NH"])]
    G["HG"] = [k.dscr("HG%d" % i, [D, G["PW"]], BF16) for i in range(4 * G["NH"])]
    if TOK // 512 >= 2:
        G["YS"] = [k.dscr("YSa", [4 * D, TOK // 2 + 2], BF16), k.dscr("YSb", [4 * D, TOK // 2], BF16)]
        G["YR"] = [k.dscr("YRa", [D, TOK // 2 + 2], BF16), k.dscr("YRb", [D, TOK // 2], BF16)]
    else:
        G["YS"] = [k.dscr("YS", [4 * D, TOK + 2], BF16)]
        G["YR"] = [k.dscr("YR", [D, TOK + 2], BF16)]
    G["MB"] = k.dscr("MB", [512, 96], F32)
    G["MG"] = k.dscr("MG", [512, 96], F32)
    G["XL"] = k.dscr("XL", [4 * D, 2], F32)
    G["XG"] = k.dscr("XG", [4 * D, 2], F32)
    G["mods"] = k.sb("mods", [128, 384])
    G["gn"] = k.sb("gn", [128, 144])
    G["msk"] = k.sb("msk_sb", [128, 9])
    G["zero"] = k.sb("zero", [128, 32])
    phase_m(k, G)
    phase_a(k, G, TOK)
    for l in range(DEPTH):
        phase_b(k, G, l, S)
        phase_c(k, G, l, TOK)
    k.s.finish("sp")
    global _LAST_SCHED
    _LAST_SCHED = k.s
    return k.close()


def _fm(v):
    v = np.asarray(v, np.float32)
    return np.ascontiguousarray(v.reshape(-1, 128).T)


def _pool_mats(win):
    s_ = np.arange(128)[:, None]
    t_ = np.arange(128)[None, :]
    band = ((t_ - s_) >= 0) & ((t_ - s_) < win)
    eye = np.eye(128, dtype=np.float32)
    pd = band.astype(np.float32) / win - eye
    pd0 = band.astype(np.float32) / np.minimum(t_ + 1, win).astype(np.float32) - eye
    pp = (((t_ + 128 - s_) < win)).astype(np.float32) / win
    return np.ascontiguousarray(np.concatenate([pd0, pd, pp], axis=1).astype(np.float32))


_PROGS = {}
_LAST_SCHED = None


def kernel(x, c, w_ada, b_ada, g_mix, w_in, b_f, pool_w, pool_scale, lru_conv_w, lru_conv_b,
           lru_wa, lru_ba, lru_wi, lru_bi, lru_lambda, w_out, g_ffn, w_ffn_gate, w_ffn_up,
           ffn_conv_w, ffn_conv_b, w_ffn_down, final_g):
    f32 = np.float32
    A = lambda v: np.asarray(v, f32)
    x = A(x)
    B, S, _ = x.shape
    TOK = S // 4
    c, w_ada, b_ada, g_mix, w_in, b_f = A(c), A(w_ada), A(b_ada), A(g_mix), A(w_in), A(b_f)
    pool_w, pool_scale, lru_conv_w, lru_conv_b = A(pool_w), A(pool_scale), A(lru_conv_w), A(lru_conv_b)
    lru_wa, lru_ba, lru_wi, lru_bi, lru_lambda = A(lru_wa), A(lru_ba), A(lru_wi), A(lru_bi), A(lru_lambda)
    w_out, g_ffn, w_ffn_gate, w_ffn_up = A(w_out), A(g_ffn), A(w_ffn_gate), A(w_ffn_up)
    ffn_conv_w, ffn_conv_b, w_ffn_down, final_g = A(ffn_conv_w), A(ffn_conv_b), A(w_ffn_down), A(final_g)

    if S not in _PROGS:
        _PROGS[S] = build_fused(S)
    nc = _PROGS[S]

    bada = np.ascontiguousarray(np.concatenate([_fm(b_ada[l]) for l in range(DEPTH)], axis=1))
    gains = np.ascontiguousarray(np.concatenate([_fm(g_mix[l]) for l in range(DEPTH)] + [_fm(g_ffn[l]) for l in range(DEPTH)]
                                                + [_fm(final_g)], axis=1))
    cst = np.ascontiguousarray(np.concatenate([np.eye(128, dtype=f32),
                                               np.where(np.arange(128)[None, :] >= np.arange(128)[:, None], 0.0, MASKNEG).astype(f32),
                                               np.zeros((128, 384), f32)], axis=1))
    perm = []
    for g in range(4):
        perm += list(range(g * 128, (g + 1) * 128))
        perm += list(range(512 + 2 * g * 128, 512 + (2 * g + 2) * 128))
        perm += list(range(1536 + g * 128, 1536 + (g + 1) * 128))
    w_out_p = np.ascontiguousarray(w_out[:, perm, :])
    wd_l = np.ascontiguousarray(w_ffn_down.reshape(DEPTH, NJ, 128, 16, 128).transpose(0, 3, 2, 1, 4).reshape(DEPTH, 16, 128, NJ * 128))
    cvv = np.zeros((DEPTH, 128, NJ * 4), f32)
    for l in range(DEPTH):
        for kk in range(3):
            cvv[l][:, kk::4] = _fm(ffn_conv_w[l][kk])
        cvv[l][:, 3::4] = _fm(ffn_conv_b[l])
    per_g = []
    for g in range(4):
        cols = []
        for h in range(2):
            cols += list(range(512 + (2 * g + h) * 128, 512 + (2 * g + h + 1) * 128))
        for h in range(2):
            cols += list(range(1536 + (2 * g + h) * 128, 1536 + (2 * g + h + 1) * 128))
        cols += list(range(3592 + g * 128, 3592 + (g + 1) * 128))
        cols += list(range(4104 + g * 128, 4104 + (g + 1) * 128))
        for h in range(2):
            cols += list(range(2560 + (2 * g + h) * 128, 2560 + (2 * g + h + 1) * 128))
        cols += list(range(g * 128, (g + 1) * 128))
        cols += [3584 + 2 * g, 3584 + 2 * g + 1]
        gs = slice(g * 128, (g + 1) * 128)
        pbv = np.zeros((DEPTH, 128, 16), f32)
        for l in range(DEPTH):
            pbv[l][:, 0] = pool_scale[l][gs]
            for kk in range(4):
                pbv[l][:, 1 + kk] = lru_conv_w[l][kk][gs]
            pbv[l][:, 5] = lru_conv_b[l][gs]
            pbv[l][:, 6] = lru_ba[l][gs]
            pbv[l][:, 7] = lru_bi[l][gs]
            pbv[l][:, 8] = lru_lambda[l][gs]
        per_g.append({
            "win": np.ascontiguousarray(w_in[:, :, cols]),
            "pwm": np.ascontiguousarray(np.concatenate([pool_w[:, g], lru_wa[:, g], lru_wi[:, g]], axis=2)),
            "pmat": _pool_mats(POOL_WINDOWS[g]),
            "pbp": pbv,
            "bfp": np.ascontiguousarray(b_f[:, None, 2 * g:2 * g + 2]),
            "wada": np.ascontiguousarray(w_ada[:, :, g * 3072:(g + 1) * 3072]),
        })
    maps = []
    for cid in range(NCORE):
        b, j = cid // 4, cid % 4
        xin = np.zeros((D, TOK + 2), f32)
        if j == 0:
            xin[:, 2:] = x[b, 0:TOK, :].T
        else:
            xin[:, :] = x[b, j * TOK - 2:(j + 1) * TOK, :].T
        msk = np.zeros((128, 9), f32)
        msk[:, j] = 1.0
        if j + 1 < 4:
            msk[:, 4 + j + 1] = 1.0
        msk[:, 8] = 0.0 if j == 0 else 1.0
        m = {"xT": xin, "cT": _fm(c[b]), "bada": bada, "gains": gains, "msk": msk, "cst": cst,
             "w_out": w_out_p, "w_gate": w_ffn_gate, "w_up": w_ffn_up, "w_down": wd_l, "cv": cvv}
        m.update(per_g[j])
        maps.append(m)
    res = run_bass_kernel_spmd(nc, maps, core_ids=list(range(NCORE)))
    r = res.results
    out = np.empty((B, S, D), f32)
    for cid in range(NCORE):
        b, j = cid // 4, cid % 4
        out[b, j * TOK:(j + 1) * TOK, :] = r[cid]["out"].T
    return out
```

```python
import math
from contextlib import ExitStack, contextmanager

import numpy as np
import ml_dtypes

import concourse.bass as bass
import concourse.mybir as mybir
from concourse.bass_utils import run_bass_kernel_spmd

F32 = mybir.dt.float32
BF16 = mybir.dt.bfloat16
AF = mybir.ActivationFunctionType
ALU = mybir.AluOpType

D = 2048
NKC = 16
DFF = 5632
NJ = 44
DEPTH = 4
NCORE = 8
EPS = 1e-6
POOL_WINDOWS = (2, 4, 8, 16)
NIN = 4616
C_Q, C_K, C_ZX, C_ZY, C_V, C_ZP, C_F = 0, 256, 512, 640, 768, 1024, 1152
NIN_G = 1154
MASKNEG = -30000.0


class Sched:
    EPOCH = 30000

    def __init__(self, nc, es):
        self.nc = nc
        self.es = es
        self.engs = {"pe": nc.tensor, "act": nc.scalar, "dve": nc.vector, "pool": nc.gpsimd, "sp": nc.sync}
        self.sems = {}
        self.count = {}
        self.step = {}
        self.waited = {e: {} for e in self.engs}
        self.lastw = {}
        self.readers = {}
        self.ninstr = 0
        self.outs = {}
        self.multiw = {}
        self.log = {e: [] for e in self.engs}

    def _sem(self, chan, ep):
        key = (chan, ep)
        if key not in self.sems:
            self.sems[key] = self.es.enter_context(self.nc.semaphore("s_%s_%d" % (chan, ep)))
        return self.sems[key]

    def _chan(self, chan, step):
        if chan not in self.count:
            self.count[chan] = 0
            self.step[chan] = step

    def _wait(self, eng, chan, total):
        if total <= self.waited[eng].get(chan, 0):
            return
        self.waited[eng][chan] = total
        esz = self.EPOCH * self.step[chan]
        ep = (total - 1) // esz
        self.engs[eng].wait_ge(self._sem(chan, ep), total - ep * esz)
        self.log[eng].append(("wait", (chan, ep), total - ep * esz))
        self.ninstr += 1

    def _deps(self, eng, me, reads, writes):
        for r in reads:
            lw = self.lastw.get(r)
            if lw is not None:
                self._wait(eng, lw[0], lw[1])
        for w in writes:
            lw = self.lastw.get(w)
            if lw is not None and lw[0] != me:
                self._wait(eng, lw[0], lw[1])
            for ch, tot in self.readers.get(w, {}).items():
                if ch != me or me not in ("pe",):
                    self._wait(eng, ch, tot)

    def _record(self, me, total, reads, writes):
        for w in writes:
            self.lastw[w] = (me, total)
            self.readers[w] = {}
        for r in reads:
            d = self.readers.setdefault(r, {})
            d[me] = max(d.get(me, 0), total)

    def op(self, eng, reads, writes, fn, sig=True):
        self._chan(eng, 1)
        for r in reads:
            lw = self.lastw.get(r)
            if lw is not None:
                self._wait(eng, lw[0], lw[1])
        for w in writes:
            lw = self.lastw.get(w)
            if lw is not None and not (eng == "pe" and lw[0] == "pe"):
                self._wait(eng, lw[0], lw[1])
            for ch, tot in self.readers.get(w, {}).items():
                if ch != eng:
                    self._wait(eng, ch, tot)
        ins = fn(self.engs[eng])
        self.ninstr += 1
        total = self.count[eng] + 1
        if sig:
            esz = self.EPOCH
            ep = (total - 1) // esz
            ins.then_inc(self._sem(eng, ep), 1)
            self.log[eng].append(("inc", (eng, ep), 1))
            self.count[eng] = total
        else:
            self.log[eng].append(("nop", None, 0))
        self._record(eng, total, reads, writes)
        return ins

    def dma(self, queue, chan, out, in_, reads, writes, is_out=False, multi_w=()):
        self._chan(chan, 16)
        for r in reads:
            lw = self.lastw.get(r)
            if lw is not None:
                self._wait(queue, lw[0], lw[1])
        for w in writes:
            lw = self.lastw.get(w)
            if lw is not None and lw[0] != chan:
                self._wait(queue, lw[0], lw[1])
            for ch, tot in self.readers.get(w, {}).items():
                self._wait(queue, ch, tot)
        ins = self.engs[queue].dma_start(out=out, in_=in_)
        self.ninstr += 1
        self.count[chan] += 16
        total = self.count[chan]
        assert total < self.EPOCH * 16
        ins.then_inc(self._sem(chan, 0), 16)
        self.log[queue].append(("inc", (chan, 0), 16))
        self._record(chan, total, reads, writes)
        if is_out:
            self.outs[chan] = total
        for r in multi_w:
            self.multiw.setdefault(r, {})[chan] = total
        return ins

    def simulate(self):
        sem = {}
        pos = {e: 0 for e in self.engs}
        progress = True
        while progress:
            progress = False
            for e, lg in self.log.items():
                while pos[e] < len(lg):
                    kind, key, val = lg[pos[e]]
                    if kind == "wait":
                        if sem.get(key, 0) < val:
                            break
                    elif kind == "inc":
                        sem[key] = sem.get(key, 0) + val
                    pos[e] += 1
                    progress = True
        stuck = {e: (pos[e], len(lg), lg[pos[e]] if pos[e] < len(lg) else None) for e, lg in self.log.items() if pos[e] < len(lg)}
        return stuck

    def barrier(self):
        for e in self.engs:
            for ch, tot in self.count.items():
                if ch != e and tot > 0:
                    self._wait(e, ch, tot)
        self.lastw.clear()
        self.readers.clear()
        self.multiw.clear()

    def collective(self, kind, in_t, out_t, reads, writes):
        self._chan("cc", 1)
        for r in reads:
            lw = self.lastw.get(r)
            if lw is not None:
                self._wait("pool", lw[0], lw[1])
            for ch, tot in self.multiw.get(r, {}).items():
                self._wait("pool", ch, tot)
        for w in writes:
            lw = self.lastw.get(w)
            if lw is not None:
                self._wait("pool", lw[0], lw[1])
            for ch, tot in self.readers.get(w, {}).items():
                self._wait("pool", ch, tot)
        ins = self.engs["pool"].collective_compute(kind, ALU.add, replica_groups=[[0, 1, 2, 3], [4, 5, 6, 7]],
                                                   ins=[in_t.ap().opt()], outs=[out_t.ap().opt()])
        self.ninstr += 1
        self.count["cc"] += 1
        ins.then_inc(self._sem("cc", 0))
        self.log["pool"].append(("inc", ("cc", 0), 1))
        self._record("cc", self.count["cc"], reads, writes)
        return ins

    def finish(self, eng):
        for ch, tot in self.outs.items():
            self._wait(eng, ch, tot)


class KB:
    def __init__(self):
        self.nc = bass.Bass("TRN2", target_bir_lowering=False)
        self.es = ExitStack()
        self.s = Sched(self.nc, self.es)
        self._n = 0

    def din(self, name, shape, dt=F32):
        return self.nc.dram_tensor(name, list(shape), dt, kind="ExternalInput").ap()

    def dout(self, name, shape, dt=F32):
        return self.nc.dram_tensor(name, list(shape), dt, kind="ExternalOutput").ap()

    def sb(self, name, shape, dt=F32):
        self._n += 1
        return self.es.enter_context(self.nc.sbuf_tensor("%s_u%d" % (name, self._n), list(shape), dt))

    def ps(self, name, shape=(128, 512), dt=F32):
        self._n += 1
        return self.es.enter_context(self.nc.psum_tensor("%s_u%d" % (name, self._n), list(shape), dt))

    @contextmanager
    def phase(self):
        old = self.es
        self.es = ExitStack()
        try:
            yield
        finally:
            self.es.close()
            self.es = old

    def dscr(self, name, shape, dt=F32):
        return self.nc.dram_tensor(name, list(shape), dt)

    def close(self):
        self.es.close()
        return self.nc


class Rot:
    def __init__(self, items):
        self.items = items
        self.i = 0

    def next(self):
        it = self.items[self.i % len(self.items)]
        self.i += 1
        return it


class WStream:
    def __init__(self, k, name, nslot, slot_elems, prefetch):
        self.k = k
        self.slots = [k.sb("%s_slot%d" % (name, i), [128, slot_elems], BF16) for i in range(nslot)]
        self.name = name
        self.nslot = nslot
        self.pf = prefetch
        self.loads = []
        self.issued = 0

    def add(self, fn):
        self.loads.append(fn)
        return len(self.loads) - 1

    def res(self, i):
        return "%s_w%d" % (self.name, i % self.nslot)

    def _issue(self, i):
        slot = self.slots[i % self.nslot]
        view, pairs = self.loads[i](slot)
        for (o, a) in pairs:
            self.k.s.dma("pool", "%s_c%d" % (self.name, i % self.nslot), o, a, [], [self.res(i)])
        return view

    def get(self, i):
        while self.issued < min(len(self.loads), i + 1 + self.pf):
            self._issue(self.issued)
            self.issued += 1
        slot = self.slots[i % self.nslot]
        view, _ = self.loads[i](slot)
        return view, self.res(i)


def v3(t, a, b):
    return t[:, 0:a * b].rearrange("p (a b) -> p a b", b=b)


def emit_norm(k, tag, x3, xres, W, aeff, sq3, sqres, ones_bf, ss_ps, ss_res, rstd, rstd_res, tmps, sink):
    s = k.s
    s.op("act", [xres], [sqres], lambda e: e.activation(out=sq3[:, 0:NKC, 0:W], in_=x3[:, 0:NKC, 0:W], func=AF.Square))
    for kc in range(NKC):
        s.op("pe", [sqres], [ss_res],
             lambda e, kc=kc: e.matmul(ss_ps[:, 0:W], ones_bf[:, :], sq3[:, kc, 0:W], start=(kc == 0), stop=(kc == NKC - 1)),
             sig=(kc == NKC - 1))
    s.op("act", [ss_res], [rstd_res],
         lambda e: e.activation(out=rstd[:, 0:W], in_=ss_ps[:, 0:W], func=AF.Sqrt, bias=float(D * EPS)))
    s.op("dve", [rstd_res], [rstd_res], lambda e: e.reciprocal(out=rstd[:, 0:W], in_=rstd[:, 0:W]))
    for kc in range(NKC):
        tres, tmp = tmps.next()
        s.op("dve", [xres, rstd_res], [tres],
             lambda e, kc=kc, tmp=tmp: e.scalar_tensor_tensor(out=tmp[:, 0:W], in0=x3[:, kc, 0:W], scalar=aeff[:, kc:kc + 1],
                                                              in1=rstd[:, 0:W], op0=ALU.mult, op1=ALU.mult))
        sink(kc, tmp, tres)


def emit_aeff(k, prm, cg, csc, aeff, res_in, res_out):
    s = k.s
    s.op("dve", [res_in], [res_out],
         lambda e: e.tensor_scalar(out=aeff[:, :], in0=prm[:, csc:csc + 16], scalar1=1.0, scalar2=float(math.sqrt(D)),
                                   op0=ALU.add, op1=ALU.mult))
    s.op("dve", [res_in, res_out], [res_out],
         lambda e: e.tensor_tensor(out=aeff[:, :], in0=aeff[:, :], in1=prm[:, cg:cg + 16], op=ALU.mult))


def emit_aeff2(k, sc_ap, g_ap, aeff, res_out):
    s = k.s
    s.op("dve", ["mods"], [res_out],
         lambda e: e.tensor_scalar(out=aeff[:, :], in0=sc_ap, scalar1=1.0, scalar2=float(math.sqrt(D)), op0=ALU.add, op1=ALU.mult))
    s.op("dve", ["gn", res_out], [res_out], lambda e: e.tensor_tensor(out=aeff[:, :], in0=aeff[:, :], in1=g_ap, op=ALU.mult))


class HSlots:
    def __init__(self, k, G, sh_ap, tag):
        self.k = k
        self.G = G
        s = k.s
        self.shm = k.sb("shm_" + tag, [128, 4, 16])
        for sl in range(4):
            s.op("dve", ["mods", "msk"], ["shm"],
                 lambda e, sl=sl: e.tensor_scalar(out=self.shm[:, sl, :], in0=sh_ap, scalar1=G["msk"][:, sl:sl + 1], scalar2=None, op0=ALU.mult))
        self.stg = Rot([("stg%d" % i, k.sb("stg%d_%s" % (i, tag), [128, 4, 2, 512], BF16)) for i in range(2)])
        self.cur = None

    def sink(self, t, kc, tmp, tres):
        s = self.k.s
        G = self.G
        if kc % 2 == 0:
            self.cur = self.stg.next()
        sres, st = self.cur
        for sl in range(4):
            s.op("act", [tres, "shm", "msk"], [sres],
                 lambda e, sl=sl, st=st: e.activation(out=st[:, sl, kc % 2, :], in_=tmp[:, :], func=AF.Identity,
                                                      bias=self.shm[:, sl, kc:kc + 1], scale=G["msk"][:, sl:sl + 1]))
        if kc % 2 == 1:
            hf, col = (t * 512) // G["PW"], (t * 512) % G["PW"]
            for sl in range(4):
                hi_ = sl * G["NH"] + hf
                hbv = G["HB"][hi_].ap().rearrange("(kc p) t -> p kc t", p=128)
                s.dma("sp", "sthb_" + sres, hbv[:, kc - 1:kc + 1, col:col + 512], st[:, sl, :, :], [sres], [], multi_w=["HB%d" % hi_])


def emit_e1(k, G, which="all"):
    for i in range(4 * G["NH"]):
        early = (G["NH"] == 2 and i % 2 == 0)
        if which == "all" or (which == "early") == early:
            k.s.collective("AllReduce", G["HB"][i], G["HG"][i], ["HB%d" % i], ["HG%d" % i])


def phase_m(k, G):
    s = k.s
    with k.phase():
        c_sb = k.sb("c_sb", [128, 16])
        sig = k.sb("sig", [128, 16])
        cact = k.sb("cact", [128, 16], BF16)
        mq = k.sb("mq", [128, 96])
        mq4 = k.sb("mq4", [128, 4, 96])
        bada = k.sb("bada_sb", [128, 384])
        mq_ps = k.ps("mq_ps")
        s.dma("sp", "ld", c_sb[:, :], G["cT"][:, :], [], ["c_sb"])
        s.dma("sp", "ld2", bada[:, :], G["bada"][:, :], [], ["bada"])
        s.dma("sp", "ld3", G["gn"][:, :], G["gains"][:, :], [], ["gn"])
        s.dma("sp", "ld4", G["msk"][:, :], G["msk_d"][:, :], [], ["msk"])
        s.op("dve", [], ["zero"], lambda e: e.memset(G["zero"][:, :], 0.0))
        s.op("act", ["c_sb"], ["sig"], lambda e: e.activation(out=sig[:, :], in_=c_sb[:, :], func=AF.Sigmoid))
        s.op("dve", ["c_sb", "sig"], ["cact"], lambda e: e.tensor_tensor(out=cact[:, :], in0=c_sb[:, :], in1=sig[:, :], op=ALU.mult))
        ws = WStream(k, "wm", 3, 8192, 2)
        for l in range(DEPTH):
            wv = G["wada"][l].rearrange("(kc p) n -> p kc n", p=128)
            for gq in range(6):
                ws.add(lambda slot, wv=wv, gq=gq: (v3(slot, 16, 512), [(v3(slot, 16, 512), wv[:, :, gq * 512:(gq + 1) * 512])]))
        i = 0
        for l in range(DEPTH):
            for gq in range(6):
                w3, wres = ws.get(i)
                i += 1
                for nn in range(4):
                    col = l * 24 + gq * 4 + nn
                    for kc in range(NKC):
                        s.op("pe", ["cact", wres], ["mq_ps"],
                             lambda e, kc=kc, w3=w3, nn=nn, col=col: e.matmul(mq_ps[:, col:col + 1], w3[:, kc, nn * 128:(nn + 1) * 128], cact[:, kc:kc + 1],
                                                                              start=(kc == 0), stop=(kc == NKC - 1)), sig=(kc == NKC - 1))
        s.op("act", ["mq_ps"], ["mq"], lambda e: e.mul(out=mq[:, :], in_=mq_ps[:, 0:96], mul=1.0))
        for sl in range(4):
            s.op("dve", ["mq", "msk"], ["mq4"],
                 lambda e, sl=sl: e.tensor_scalar(out=mq4[:, sl, :], in0=mq[:, :], scalar1=G["msk"][:, sl:sl + 1], scalar2=None, op0=ALU.mult))
        s.dma("sp", "stm", G["MB"].ap().rearrange("(s p) n -> p s n", p=128), mq4[:, :, :], ["mq4"], ["MB"])
        s.collective("AllReduce", G["MB"], G["MG"], ["MB"], ["MG"])
        mods_v = G["mods"][:, :].rearrange("p (l q i) -> p l q i", q=4, i=24)
        for q in range(4):
            s.dma("sp", "ldm%d" % q, mods_v[:, :, q, :], G["MG"].ap()[q * 128:(q + 1) * 128, :].rearrange("p (l i) -> p l i", i=24), ["MG"], ["mods"])
        s.op("dve", ["mods", "bada"], ["mods"], lambda e: e.tensor_tensor(out=G["mods"][:, :], in0=G["mods"][:, :], in1=bada[:, :], op=ALU.add))
        zerob = k.sb("zerob", [128, 16, 2], BF16)
        s.op("dve", [], ["zerob"], lambda e: e.memset(zerob[:, :, :], 0.0))
        s.dma("sp", "stz", G["YS"][0].ap()[0:D, 0:2].rearrange("(kc p) c -> p kc c", p=128), zerob[:, :, :], ["zerob"], ["YSz"])
        s.dma("sp", "stx0", G["XS"].ap()[:, :], G["xT"][:, :], [], ["XS"])
        s.barrier()


def phase_a(k, G, TOK):
    s = k.s
    NT = TOK // 512
    with k.phase():
        xv = G["xT"].rearrange("(kc p) t -> p kc t", p=128)
        aeff = k.sb("aeff", [128, 16])
        ones_bf = k.sb("ones_bf", [128, 128], BF16)
        xts = [("x%d" % i, k.sb("xa%d" % i, [128, 16, 512])) for i in range(2)]
        sq3 = k.sb("sqa", [128, 16, 512], BF16)
        rstd = k.sb("rstd", [128, 512])
        tmps = Rot([("tmp%d" % i, k.sb("tmpa%d" % i, [128, 512])) for i in range(3)])
        ss_ps = k.ps("ss_ps")
        s.op("dve", [], ["ones"], lambda e: e.memset(ones_bf[:, :], 1.0))
        M = G["mods"]
        emit_aeff2(k, M[:, 16:32], G["gn"][:, 0:16], aeff, "aeff")
        hs = HSlots(k, G, M[:, 0:16], "a")
        for t in range(NT):
            xres, x3 = xts[t % 2]
            s.dma("sp", "ldx%d" % (t % 2), x3[:, :, :], xv[:, :, 2 + t * 512:2 + (t + 1) * 512], [], [xres])
            emit_norm(k, "n", x3, xres, 512, aeff, sq3, "sq", ones_bf, ss_ps, "ss_ps", rstd, "rstd", tmps,
                      lambda kc, tmp, tres, t=t: hs.sink(t, kc, tmp, tres))
        s.barrier()
    return lambda: emit_e1(k, G)


def phase_b(k, G, l, S, after_setup=None):
    s = k.s
    NT = S // 512
    NB = S // 128
    TOK = S // 4
    with k.phase():
        wv = G["win"][l].rearrange("(kc p) n -> p kc n", p=128)
        ysvs = [ys.ap().rearrange("(j g r p) t -> j g p r t", j=4, g=4, p=128) for ys in G["YS"]]
        nh = len(G["YS"])
        win = k.sb("win_sb", [128, 16, 770], BF16)
        wvz = k.sb("wvz", [128, 16, 384], BF16)
        pwm = k.sb("pwm_sb", [128, 384], BF16)
        pmat = k.sb("pmat_sb", [128, 384], BF16)
        cst = k.sb("cst_sb", [128, 640])
        pb = k.sb("pb_sb", [128, 16])
        bfs = k.sb("bf_sb", [1, 2])
        nbf = k.sb("nbf", [1, 2])
        cA = k.sb("cA", [128, 2])
        lt = k.sb("lt", [128, 2])
        ones_bf = k.sb("ones_bf", [128, 128], BF16)
        ones2 = k.sb("ones2", [128, 128], BF16)
        ones_row = k.sb("ones_row", [1, 512])
        hts = [("ht%d" % i, k.sb("ht%d" % i, [128, 16, 512], BF16)) for i in range(2)]
        kT = k.sb("kT", [128, 2, S], BF16)
        vtok = k.sb("vtok", [128, NB, 256], BF16)
        zptok = k.sb("zptok", [128, 8, 128], BF16)
        qT = k.sb("qT", [128, 2, 512], BF16)
        FQ = k.sb("FQ", [128, 2, 512], BF16)
        Gk = k.sb("Gk", [128, 2, NB])
        Gq = [k.sb("Gq%d" % h, [1, 512]) for h in range(2)]
        Gc = k.sb("Gc", [1, 2])
        sp1 = k.sb("sp1", [1, 512])
        lsp = k.sb("lsp", [1, 512])
        hib = k.sb("hib", [1, 512], BF16)
        hif = lsp
        lo = sp1
        gnb = [k.sb("gnb%d" % h, [128, 512]) for h in range(2)]
        nlo = k.sb("nlo", [1, 512], BF16)
        pts = Rot([("pt%d" % i, k.sb("pt%d" % i, [128, 512], BF16)) for i in range(4)])
        dgs = Rot([("dg%d" % i, k.sb("dg%d" % i, [128, 512])) for i in range(1)])
        rden = k.sb("rden", [128, 512])
        yb = k.sb("yb", [128, 4, 512], BF16)
        yres = "yb"
        yms = Rot([("ym%d" % i, k.sb("ym%d" % i, [128, 4, 512], BF16)) for i in range(2)])
        zxb = k.sb("zxb", [128, 516])
        zy = k.sb("zy", [128, 512])
        dT = k.sb("dT", [128, 512], BF16)
        xc = k.sb("xc", [128, 512])
        xcb = k.sb("xcb", [128, 512], BF16)
        gr = k.sb("gr", [128, 512])
        gi = k.sb("gi", [128, 512])
        av = k.sb("av", [128, 512])
        a2 = k.sb("a2", [128, 512])
        mm_ = k.sb("mm", [128, 512])
        inp = k.sb("inp", [128, 512])
        hh = k.sb("hh", [128, 512])
        hc = k.sb("hc", [128, 1])
        uu = k.sb("uu", [128, 512])
        sg = k.sb("sg", [128, 512])
        pj = Rot([("pj%d" % i, k.ps("pj%d" % i)) for i in range(2)])
        sts = Rot([("st%d" % i, k.ps("st%d" % i)) for i in range(4)])
        o_one = k.ps("o_ps")
        d_one = k.ps("d_ps")
        o_ps = [o_one, o_one]
        d_ps = [d_one, d_one]
        ident = cst[:, 0:128]
        trif = cst[:, 128:640]
        msk = G["msk"]

        for q4 in range(4):
            s.dma("pool", "ldw", win[:, 4 * q4:4 * q4 + 4, 0:768], wv[:, 4 * q4:4 * q4 + 4, 0:768], [], ["win"])
        s.dma("pool", "ldw", win[:, :, 768:770], wv[:, :, C_F:C_F + 2], [], ["win"])
        s.dma("pool", "ldw4", wvz[:, :, :], wv[:, :, C_V:C_V + 384], [], ["wvz"])
        s.dma("pool", "ldw2", pwm[:, :], G["pwm"][l], [], ["pwm"])
        s.dma("pool", "ldw3", pmat[:, :], G["pmat"][:, :], [], ["pmat"])
        s.dma("sp", "ldc", cst[:, :], G["cst"][:, :], [], ["cst"])
        s.dma("sp", "ldc2", pb[:, :], G["pbp"][l], [], ["pb"])
        s.dma("sp", "ldc3", bfs[:, :], G["bfp"][l], [], ["bf"])
        s.op("dve", [], ["ones"], lambda e: e.memset(ones_bf[:, :], 1.0))
        s.op("dve", [], ["ones2"], lambda e: e.memset(ones2[:, :], 0.0))
        s.op("dve", ["ones2"], ["ones2"], lambda e: e.memset(ones2[0:2, :], 1.0))
        s.op("dve", [], ["ones_row"], lambda e: e.memset(ones_row[:, :], 1.0))
        s.op("dve", [], ["FQ0", "FQ1"], lambda e: e.memset(FQ[:, :, :], 0.0))
        s.op("dve", [], ["zxb"], lambda e: e.memset(zxb[:, 0:4], 0.0))
        s.op("dve", [], ["hc"], lambda e: e.memset(hc[:, :], 0.0))
        s.op("dve", [], ["Gc0", "Gc1"], lambda e: e.memset(Gc[:, :], 0.0))
        s.op("dve", ["bf"], ["nbf"], lambda e: e.tensor_scalar(out=nbf[:, :], in0=bfs[:, :], scalar1=-1.0, scalar2=None, op0=ALU.mult))
        s.op("act", ["pb"], ["lt"], lambda e: e.activation(out=lt[:, 0:1], in_=pb[:, 8:9], func=AF.Exp, scale=-1.0))
        s.op("act", ["lt"], ["lt"], lambda e: e.activation(out=lt[:, 1:2], in_=lt[:, 0:1], func=AF.Ln, bias=1.0))
        s.op("dve", ["lt"], ["cA"], lambda e: e.tensor_scalar(out=cA[:, 0:1], in0=lt[:, 1:2], scalar1=-8.0, scalar2=None, op0=ALU.mult))
        s.op("dve", ["lt", "cA"], ["cA"], lambda e: e.tensor_scalar(out=cA[:, 1:2], in0=lt[:, 1:2], scalar1=-16.0, scalar2=None, op0=ALU.mult))

        if after_setup is not None:
            after_setup()

        QSCALE = float(128 ** -0.5)

        def fm_proj(col, ht3, hres):
            pres, ps = pj.next()
            for kc in range(NKC):
                s.op("pe", [hres, "win"], [pres],
                     lambda e, kc=kc, ps=ps: e.matmul(ps[:, :], win[:, kc, col:col + 128], ht3[:, kc, :], start=(kc == 0), stop=(kc == NKC - 1)),
                     sig=(kc == NKC - 1))
            return pres, ps

        tps = TOK // 512

        def load_h(T):
            hres, ht3 = hts[T % 2]
            sl_, tt = T // tps, T % tps
            hgi = sl_ * G["NH"] + (tt * 512) // G["PW"]
            hcol = (tt * 512) % G["PW"]
            hgv = G["HG"][hgi].ap().rearrange("(kc p) t -> p kc t", p=128)
            s.dma("sp", "ldh%d" % (T % 2), ht3[:, :, :], hgv[:, :, hcol:hcol + 512], ["HG%d" % hgi], [hres])

        load_h(0)
        for T in range(NT):
            t0 = T * 512
            hres, ht3 = hts[T % 2]
            if T + 1 < NT:
                load_h(T + 1)

            f_pss = []
            for h in range(2):
                pres, ps = sts.next()
                for kc in range(NKC):
                    s.op("pe", [hres, "win"], [pres],
                         lambda e, kc=kc, ps=ps, h=h: e.matmul(ps[0:1, :], win[:, kc, 768 + h:769 + h], ht3[:, kc, :],
                                                               start=(kc == 0), stop=(kc == NKC - 1)), sig=(kc == NKC - 1))
                f_pss.append((pres, ps))
            for h in range(2):
                pres, ps = f_pss[h]
                s.op("act", [pres, "nbf"], ["sp1"],
                     lambda e, ps=ps, h=h: e.activation(out=sp1[:, :], in_=ps[0:1, :], func=AF.Exp, bias=nbf[0:1, h:h + 1], scale=-1.0))
                s.op("act", ["sp1"], ["lsp"], lambda e: e.activation(out=lsp[:, :], in_=sp1[:, :], func=AF.Ln, bias=1.0))
                s.op("dve", ["lsp", "ones_row", "Gc%d" % h], ["Gq%d" % h],
                     lambda e, h=h: e.tensor_tensor_scan(out=Gq[h][:, :], data0=ones_row[:, :], data1=lsp[:, :], initial=Gc[0:1, h:h + 1],
                                                         op0=ALU.mult, op1=ALU.add))
                s.op("dve", ["Gq%d" % h], ["Gc%d" % h], lambda e, h=h: e.tensor_copy(out=Gc[0:1, h:h + 1], in_=Gq[h][:, 511:512]))
                s.op("dve", ["Gq%d" % h], ["hib"], lambda e, h=h: e.tensor_copy(out=hib[:, :], in_=Gq[h][:, :]))
                s.op("dve", ["hib", "lsp"], ["lsp"], lambda e: e.tensor_copy(out=hif[:, :], in_=hib[:, :]))
                s.op("dve", ["Gq%d" % h, "lsp", "sp1"], ["sp1"], lambda e, h=h: e.tensor_tensor(out=lo[:, :], in0=Gq[h][:, :], in1=hif[:, :], op=ALU.subtract))
                s.op("dve", ["lsp"], ["FQ%d" % h],
                     lambda e, h=h: e.tensor_scalar(out=FQ[0:1, h, :], in0=hif[:, :], scalar1=-1.0, scalar2=None, op0=ALU.mult))
                s.op("dve", ["sp1"], ["nlo"], lambda e: e.tensor_scalar(out=nlo[:, :], in0=lo[:, :], scalar1=-1.0, scalar2=None, op0=ALU.mult))
                s.dma("sp", "fq%d" % h, FQ[1:2, h, :], nlo[0:1, :], ["nlo"], ["FQ%d" % h])

            for h in range(2):
                pres, ps = fm_proj(C_Q + h * 128, ht3, hres)
                s.op("act", [pres], ["qT%d" % h], lambda e, ps=ps, h=h: e.mul(out=qT[:, h, :], in_=ps[:, :], mul=QSCALE))
            for h in range(2):
                pres, ps = fm_proj(C_K + h * 128, ht3, hres)
                s.op("dve", [pres], ["kT"], lambda e, ps=ps, h=h: e.tensor_copy(out=kT[:, h, t0:t0 + 512], in_=ps[:, :]))
            pres, ps = fm_proj(C_ZX, ht3, hres)
            s.op("act", [pres], ["zxb"], lambda e, ps=ps: e.mul(out=zxb[:, 4:516], in_=ps[:, :], mul=1.0))
            pres, ps = fm_proj(C_ZY, ht3, hres)
            s.op("act", [pres], ["zy"], lambda e, ps=ps: e.mul(out=zy[:, :], in_=ps[:, :], mul=1.0))
            for jb in range(4):
                blk = 4 * T + jb
                pres, ps = pj.next()
                for kc in range(NKC):
                    s.op("pe", [hres, "wvz"], [pres],
                         lambda e, kc=kc, ps=ps, jb=jb: e.matmul(ps[:, 0:384], ht3[:, kc, jb * 128:(jb + 1) * 128], wvz[:, kc, :],
                                                                 start=(kc == 0), stop=(kc == NKC - 1)), sig=(kc == NKC - 1))
                s.op("act", [pres], ["vtok"], lambda e, ps=ps, blk=blk: e.mul(out=vtok[:, blk, :], in_=ps[:, 0:256], mul=1.0))
                s.op("act", [pres], ["zptok"], lambda e, ps=ps, blk=blk: e.mul(out=zptok[:, blk % 8, :], in_=ps[:, 256:384], mul=1.0))

            tp_res, tp_ps = pj.next()
            for h in range(2):
                gres_, gps_ = sts.next()
                s.op("pe", ["ones2", "FQ%d" % h], [gres_], lambda e, h=h, gps_=gps_: e.matmul(gps_[:, :], ones2[:, :], FQ[:, h, :], start=True, stop=True))
                s.op("act", [gres_], ["gnb%d" % h], lambda e, h=h, gps_=gps_: e.mul(out=gnb[h][:, :], in_=gps_[:, :], mul=1.0))
                for jb in range(4):
                    s.op("pe", ["Gq%d" % h, "cst"], [tp_res],
                         lambda e, h=h, jb=jb: e.transpose(tp_ps[:, h * 4 + jb:h * 4 + jb + 1], Gq[h][0:1, jb * 128:(jb + 1) * 128], ident[0:1, 0:1]),
                         sig=(jb == 3))
            s.op("dve", [tp_res], ["Gk"], lambda e, T=T: e.tensor_copy(out=Gk[:, 0, 4 * T:4 * T + 4], in_=tp_ps[:, 0:4]))
            s.op("dve", [tp_res], ["Gk"], lambda e, T=T: e.tensor_copy(out=Gk[:, 1, 4 * T:4 * T + 4], in_=tp_ps[:, 4:8]))

            s.op("dve", ["zxb", "pb"], ["xc"],
                 lambda e: e.tensor_scalar(out=xc[:, :], in0=zxb[:, 1:513], scalar1=pb[:, 1:2], scalar2=pb[:, 5:6], op0=ALU.mult, op1=ALU.add))
            for kk in range(1, 4):
                s.op("dve", ["zxb", "pb", "xc"], ["xc"],
                     lambda e, kk=kk: e.scalar_tensor_tensor(out=xc[:, :], in0=zxb[:, 1 + kk:513 + kk], scalar=pb[:, 1 + kk:2 + kk], in1=xc[:, :],
                                                             op0=ALU.mult, op1=ALU.add))
            s.op("dve", ["zxb"], ["zxb"], lambda e: e.tensor_copy(out=zxb[:, 1:4], in_=zxb[:, 513:516]))
            s.op("act", ["xc"], ["xcb"], lambda e: e.mul(out=xcb[:, :], in_=xc[:, :], mul=1.0))

            nkb = 4 * T + 4
            blocks = [(h, kb) for h in range(2) for kb in range(nkb)]
            info = {}

            def emit_qk(i):
                h, kb = blocks[i]
                jj = kb - 4 * T
                c0 = jj * 128 if jj >= 0 else 0
                stres, st = sts.next()
                ptres, pt = pts.next()
                info[i] = (ptres, pt, c0)
                s.op("pe", ["kT", "qT%d" % h], [stres],
                     lambda e: e.matmul(st[:, c0:512], kT[:, h, kb * 128:(kb + 1) * 128], qT[:, h, c0:512], start=True, stop=True))
                s.op("dve", [stres, "gnb%d" % h], [stres],
                     lambda e: e.tensor_tensor(out=st[:, c0:512], in0=st[:, c0:512], in1=gnb[h][:, c0:512], op=ALU.add))
                if jj >= 0:
                    dgres, dg = dgs.next()
                    n = 512 - c0
                    s.op("dve", [stres, "cst"], [dgres],
                         lambda e: e.tensor_tensor(out=dg[:, 0:n], in0=st[:, c0:512], in1=trif[:, 0:n], op=ALU.add))
                    s.op("act", [dgres, "Gk"], [ptres],
                         lambda e: e.activation(out=pt[:, c0:512], in_=dg[:, 0:n], func=AF.Exp, bias=Gk[:, h, kb:kb + 1]))
                else:
                    s.op("act", [stres, "Gk"], [ptres],
                         lambda e: e.activation(out=pt[:, :], in_=st[:, :], func=AF.Exp, bias=Gk[:, h, kb:kb + 1]))

            def emit_pv(i):
                h, kb = blocks[i]
                ptres, pt, c0 = info.pop(i)
                last = (kb == nkb - 1)
                s.op("pe", ["vtok", ptres], ["o_ps"],
                     lambda e: e.matmul(o_ps[h][:, c0:512], vtok[:, kb, h * 128:(h + 1) * 128], pt[:, c0:512], start=(kb == 0), stop=last), sig=False)
                s.op("pe", ["ones", ptres], ["d_ps"],
                     lambda e: e.matmul(d_ps[h][:, c0:512], ones_bf[:, :], pt[:, c0:512], start=(kb == 0), stop=last))
                if last:
                    s.op("dve", ["d_ps"], ["rden"], lambda e: e.reciprocal(out=rden[:, :], in_=d_ps[h][:, :]))
                    s.op("dve", ["o_ps", "d_ps", "rden"], [yres],
                         lambda e: e.tensor_tensor(out=yb[:, 1 + h, :], in0=o_ps[h][:, :], in1=rden[:, :], op=ALU.mult))

            LA = 3
            for i in range(min(LA, len(blocks))):
                emit_qk(i)
            for i in range(len(blocks)):
                if i + LA < len(blocks):
                    emit_qk(i + LA)
                emit_pv(i)

            pres, ps = pj.next()
            for jb in range(4):
                blk = 4 * T + jb
                pc0 = 0 if blk == 0 else 128
                s.op("pe", ["zptok", "pmat"], [pres],
                     lambda e, ps=ps, jb=jb, blk=blk, pc0=pc0: e.matmul(ps[:, jb * 128:(jb + 1) * 128], zptok[:, blk % 8, :], pmat[:, pc0:pc0 + 128],
                                                                        start=True, stop=(blk == 0)), sig=(blk == 0))
                if blk > 0:
                    s.op("pe", ["zptok", "pmat"], [pres],
                         lambda e, ps=ps, jb=jb, blk=blk: e.matmul(ps[:, jb * 128:(jb + 1) * 128], zptok[:, (blk - 1) % 8, :], pmat[:, 256:384],
                                                                   start=False, stop=True), sig=True)
            s.op("act", [pres], ["dT"], lambda e, ps=ps: e.mul(out=dT[:, :], in_=ps[:, :], mul=1.0))
            pres2, ps2 = pj.next()
            s.op("pe", ["dT", "pwm"], [pres2], lambda e, ps2=ps2: e.matmul(ps2[:, :], pwm[:, 0:128], dT[:, :], start=True, stop=True))
            s.op("dve", [pres2, "pb"], [yres],
                 lambda e, ps2=ps2: e.tensor_scalar(out=yb[:, 0, :], in0=ps2[:, :], scalar1=pb[:, 0:1], scalar2=None, op0=ALU.mult))

            rres, rps = pj.next()
            s.op("pe", ["xcb", "pwm"], [rres], lambda e, rps=rps: e.matmul(rps[:, :], pwm[:, 128:256], xcb[:, :], start=True, stop=True))
            ires, ips = pj.next()
            s.op("pe", ["xcb", "pwm"], [ires], lambda e, ips=ips: e.matmul(ips[:, :], pwm[:, 256:384], xcb[:, :], start=True, stop=True))
            s.op("act", [rres, "pb"], ["gr"], lambda e, rps=rps: e.activation(out=gr[:, :], in_=rps[:, :], func=AF.Sigmoid, bias=pb[:, 6:7]))
            s.op("act", [ires, "pb"], ["gi"], lambda e, ips=ips: e.activation(out=gi[:, :], in_=ips[:, :], func=AF.Sigmoid, bias=pb[:, 7:8]))
            s.op("act", ["gr", "cA"], ["av"], lambda e: e.activation(out=av[:, :], in_=gr[:, :], func=AF.Exp, scale=cA[:, 0:1]))
            s.op("act", ["gr", "cA"], ["a2"], lambda e: e.activation(out=a2[:, :], in_=gr[:, :], func=AF.Exp, scale=cA[:, 1:2]))
            s.op("dve", ["a2"], ["mm"], lambda e: e.tensor_scalar(out=mm_[:, :], in0=a2[:, :], scalar1=-1.0, scalar2=1.0, op0=ALU.mult, op1=ALU.add))
            s.op("dve", ["mm"], ["mm"], lambda e: e.tensor_scalar(out=mm_[:, :], in0=mm_[:, :], scalar1=1e-30, scalar2=None, op0=ALU.max))
            s.op("act", ["mm"], ["mm"], lambda e: e.activation(out=mm_[:, :], in_=mm_[:, :], func=AF.Sqrt))
            s.op("dve", ["gi", "xc"], ["inp"], lambda e: e.tensor_tensor(out=inp[:, :], in0=gi[:, :], in1=xc[:, :], op=ALU.mult))
            s.op("dve", ["inp", "mm"], ["inp"], lambda e: e.tensor_tensor(out=inp[:, :], in0=inp[:, :], in1=mm_[:, :], op=ALU.mult))
            s.op("dve", ["av", "inp", "hc"], ["hh"],
                 lambda e: e.tensor_tensor_scan(out=hh[:, :], data0=av[:, :], data1=inp[:, :], initial=hc[:, 0:1], op0=ALU.mult, op1=ALU.add))
            s.op("dve", ["hh"], ["hc"], lambda e: e.tensor_copy(out=hc[:, :], in_=hh[:, 511:512]))
            s.op("dve", ["zy"], ["uu"], lambda e: e.tensor_tensor(out=uu[:, :], in0=zy[:, :], in1=zy[:, :], op=ALU.mult))
            s.op("dve", ["uu"], ["uu"], lambda e: e.tensor_scalar(out=uu[:, :], in0=uu[:, :], scalar1=0.044715, scalar2=1.0, op0=ALU.mult, op1=ALU.add))
            s.op("dve", ["uu", "zy"], ["uu"], lambda e: e.tensor_tensor(out=uu[:, :], in0=uu[:, :], in1=zy[:, :], op=ALU.mult))
            s.op("act", ["uu"], ["sg"], lambda e: e.activation(out=sg[:, :], in_=uu[:, :], func=AF.Sigmoid, scale=1.5957691216057308))
            s.op("dve", ["sg", "zy"], ["sg"], lambda e: e.tensor_tensor(out=sg[:, :], in0=sg[:, :], in1=zy[:, :], op=ALU.mult))
            s.op("dve", ["sg", "hh"], [yres], lambda e: e.tensor_tensor(out=yb[:, 3, :], in0=sg[:, :], in1=hh[:, :], op=ALU.mult))

            j = T // tps
            tt = T % tps
            if nh == 2 and tt >= tps // 2:
                half, cbase = 1, (tt - tps // 2) * 512
            else:
                half, cbase = 0, 2 + tt * 512
            for g2 in range(4):
                ymres, ym = yms.next()
                s.op("dve", [yres, "msk"], [ymres],
                     lambda e, ym=ym, g2=g2: e.tensor_scalar(out=ym[:, :, :], in0=yb[:, :, :], scalar1=msk[:, g2:g2 + 1], scalar2=None, op0=ALU.mult))
                s.dma("sp", "sty_" + ymres, ysvs[half][j, g2, :, :, cbase:cbase + 512], ym[:, :, :], [ymres], [], multi_w=["YS%d" % half])
                if tt == tps - 1 and j < 3:
                    s.dma("sp", "sty_" + ymres, ysvs[0][j + 1, g2, :, :, 0:2], ym[:, :, 510:512], [ymres], [], multi_w=["YS0"])
            if nh == 2 and T == 3 * tps + tps // 2 - 1:
                s.collective("ReduceScatter", G["YS"][0], G["YR"][0], ["YS0"], ["YR0"])
        if nh == 1:
            s.collective("ReduceScatter", G["YS"][0], G["YR"][0], ["YS0"], ["YR0"])
            s.barrier()
        else:
            s.barrier()
            s.collective("ReduceScatter", G["YS"][1], G["YR"][1], ["YS1"], ["YR1"])


def phase_c(k, G, l, TOK):
    s = k.s
    NT = TOK // 512
    final = (l == DEPTH - 1)
    with k.phase():
        xv = G["XS"].ap().rearrange("(kc p) t -> p kc t", p=128)
        yvs = [yr.ap().rearrange("(kc p) t -> p kc t", p=128) for yr in G["YR"]]
        nh = len(yvs)
        ov = G["out"].rearrange("(kc p) t -> p kc t", p=128)
        wov = G["w_out"][l].rearrange("(kc p) n -> p kc n", p=128)
        wgv = G["w_gate"][l].rearrange("(kc p) n -> p kc n", p=128)
        wuv = G["w_up"][l].rearrange("(kc p) n -> p kc n", p=128)
        w_down = G["w_down"][l]
        M = G["mods"]
        GN = G["gn"]
        msk = G["msk"]
        mo = l * 96
        gt1 = M[:, mo + 32:mo + 48]
        sh2 = M[:, mo + 48:mo + 64]
        sc2 = M[:, mo + 64:mo + 80]
        gt2 = M[:, mo + 80:mo + 96]
        if not final:
            g_n = GN[:, (l + 1) * 16:(l + 2) * 16]
            sh_n = M[:, mo + 96:mo + 112]
            sc_n = M[:, mo + 112:mo + 128]
        else:
            g_n = GN[:, 128:144]
            sh_n = G["zero"][:, 0:16]
            sc_n = G["zero"][:, 0:16]

        cv = k.sb("cv_sb", [128, NJ * 4])
        aeff2 = k.sb("aeff2", [128, 16])
        aeffn = k.sb("aeffn", [128, 16])
        ones_bf = k.sb("ones_bf", [128, 128], BF16)
        x3 = k.sb("x3", [128, 16, 512])
        y3 = k.sb("y3", [128, 16, 512], BF16)
        h23 = y3
        act3 = k.sb("act3", [128, NJ, 512], BF16)
        sq3 = act3
        xh = k.sb("xh", [128, 16, 2])
        yh = k.sb("yh", [128, 16, 2], BF16)
        h2h = k.sb("h2h", [128, 16, 2], BF16)
        gprev = k.sb("gprev", [128, NJ, 2])
        xl4 = k.sb("xl4", [128, 4, 16, 2])
        xg4 = k.sb("xg4", [128, 4, 16, 2])
        gbuf = [("gbuf%d" % i, k.sb("gbuf%d" % i, [128, 516])) for i in range(2)]
        acc = [("acc%d" % i, k.sb("acc%d" % i, [128, 512])) for i in range(2)]
        sil = [("sil%d" % i, k.sb("sil%d" % i, [128, 512])) for i in range(2)]
        rstd = k.sb("rstd", [128, 512])
        tmps = Rot([("tmp%d" % i, k.sb("tmp%d" % i, [128, 512])) for i in range(3)])
        g_ps = [("g_ps%d" % i, k.ps("g_ps%d" % i)) for i in range(2)]
        u_ps = [("u_ps%d" % i, k.ps("u_ps%d" % i)) for i in range(2)]
        m_ps = Rot([("m_ps%d" % i, k.ps("m_ps%d" % i)) for i in range(3)])
        ss_ps = k.ps("ss_ps")
        if final:
            hst = Rot([("hst%d" % i, k.sb("hst%d" % i, [128, 512])) for i in range(2)])
            hs = None
        else:
            hs = HSlots(k, G, sh_n, "c")

        split_e1 = (not final) and G["NH"] == 2 and NT == 4
        s.dma("sp", "ldp2", cv[:, :], G["cv"][l], [], ["cv"])
        s.op("dve", [], ["ones"], lambda e: e.memset(ones_bf[:, :], 1.0))
        emit_aeff2(k, sc2, GN[:, 64 + l * 16:80 + l * 16], aeff2, "aeff2")
        emit_aeff2(k, sc_n, g_n, aeffn, "aeffn")

        ws = WStream(k, "w", 4, 8192, 2)
        plan = {}

        def add_out(n):
            return ws.add(lambda slot: (v3(slot, 16, 512), [(v3(slot, 16, 512), wov[:, :, n * 512:(n + 1) * 512])]))

        def add_gu(wview, jg):
            return ws.add(lambda slot: (v3(slot, 16, 512), [(v3(slot, 16, 512), wview[:, :, jg * 512:(jg + 1) * 512])]))

        def add_down(m):
            return ws.add(lambda slot: (v3(slot, NJ, 128), [(slot[:, 0:NJ * 128], w_down[m, :, :])]))

        for t in range(NT):
            for n in range(4):
                plan[(t, "out", n)] = add_out(n)
            for jg in range(NJ // 4):
                plan[(t, "g", jg)] = add_gu(wgv, jg)
                plan[(t, "u", jg)] = add_gu(wuv, jg)
            for m in range(16):
                plan[(t, "d", m)] = add_down(m)

        def outproj(keyfn, segs):
            for n in range(4):
                w3, wres = ws.get(plan[keyfn(n)])
                for mm in range(4):
                    m = n * 4 + mm
                    for (yy3, yres, xx3, xres, W) in segs:
                        pres, ps = m_ps.next()
                        for kc in range(NKC):
                            s.op("pe", [yres, wres], [pres],
                                 lambda e, kc=kc, w3=w3, ps=ps, mm=mm, yy3=yy3, W=W: e.matmul(ps[:, 0:W], w3[:, kc, mm * 128:(mm + 1) * 128], yy3[:, kc, 0:W],
                                                                                              start=(kc == 0), stop=(kc == NKC - 1)), sig=(kc == NKC - 1))
                        s.op("dve", [pres, xres, "mods"], [xres],
                             lambda e, m=m, ps=ps, xx3=xx3, W=W: e.scalar_tensor_tensor(out=xx3[:, m, 0:W], in0=ps[:, 0:W], scalar=gt1[:, m:m + 1],
                                                                                        in1=xx3[:, m, 0:W], op0=ALU.mult, op1=ALU.add))

        s.dma("sp", "ldh", xh[:, :, :], xv[:, :, 0:2], ["XS"], ["xh"])
        s.dma("sp", "ldh2", yh[:, :, :], yvs[0][:, :, 0:2], ["YR0"], ["yh"])

        def sink_h(kc, tmp, tres):
            s.op("act", [tres, "mods"], ["h2h"],
                 lambda e: e.activation(out=h2h[:, kc, :], in_=tmp[:, 0:2], func=AF.Identity, bias=sh2[:, kc:kc + 1]))

        for t in range(NT):
            c0 = 2 + t * 512
            if nh == 2 and t >= NT // 2:
                yh_i, yc0 = 1, (t - NT // 2) * 512
            else:
                yh_i, yc0 = 0, 2 + t * 512
            s.dma("sp", "ldx", x3[:, :, :], xv[:, :, c0:c0 + 512], ["XS"], ["x3"])
            if t == 0:
                s.dma("sp", "ldy", y3[:, :, :], yvs[yh_i][:, :, yc0:yc0 + 512], ["YR%d" % yh_i], ["y3"])
            segs = [(y3, "y3", x3, "x3", 512)]
            if t == 0:
                segs = [(yh, "yh", xh, "xh", 2)] + segs
            outproj(lambda n, t=t: (t, "out", n), segs)
            if split_e1 and t == NT // 2:
                emit_e1(k, G, "early")
            if t == 0:
                emit_norm(k, "nh", xh, "xh", 2, aeff2, sq3, "act3", ones_bf, ss_ps, "ss_ps", rstd, "rstd", tmps, sink_h)

            def sink2(kc, tmp, tres):
                s.op("act", [tres, "mods"], ["y3"],
                     lambda e: e.activation(out=h23[:, kc, :], in_=tmp[:, :], func=AF.Identity, bias=sh2[:, kc:kc + 1]))

            emit_norm(k, "n2", x3, "x3", 512, aeff2, sq3, "act3", ones_bf, ss_ps, "ss_ps", rstd, "rstd", tmps, sink2)

            for jg in range(NJ // 4):
                wg3, wgres = ws.get(plan[(t, "g", jg)])
                wu3, wures = ws.get(plan[(t, "u", jg)])
                for jj in range(4):
                    j = jg * 4 + jj
                    gres, gp = g_ps[j % 2]
                    ures, up = u_ps[j % 2]
                    bres, gb = gbuf[j % 2]
                    ares, ac = acc[j % 2]
                    sres, sl = sil[j % 2]
                    if t == 0:
                        pres, ps = m_ps.next()
                        for kc in range(NKC):
                            s.op("pe", ["h2h", wgres], [pres],
                                 lambda e, kc=kc, wg3=wg3, ps=ps, jj=jj: e.matmul(ps[:, 0:2], wg3[:, kc, jj * 128:(jj + 1) * 128], h2h[:, kc, :],
                                                                                  start=(kc == 0), stop=(kc == NKC - 1)), sig=(kc == NKC - 1))
                        s.op("dve", [pres, "msk"], ["gprev"],
                             lambda e, j=j, ps=ps: e.tensor_scalar(out=gprev[:, j, :], in0=ps[:, 0:2], scalar1=msk[:, 8:9], scalar2=None, op0=ALU.mult))
                    for kc in range(NKC):
                        s.op("pe", ["y3", wgres], [gres],
                             lambda e, kc=kc, gp=gp, jj=jj, wg3=wg3: e.matmul(gp[:, :], wg3[:, kc, jj * 128:(jj + 1) * 128], h23[:, kc, :],
                                                                              start=(kc == 0), stop=(kc == NKC - 1)), sig=(kc == NKC - 1))
                    for kc in range(NKC):
                        s.op("pe", ["y3", wures], [ures],
                             lambda e, kc=kc, up=up, jj=jj, wu3=wu3: e.matmul(up[:, :], wu3[:, kc, jj * 128:(jj + 1) * 128], h23[:, kc, :],
                                                                              start=(kc == 0), stop=(kc == NKC - 1)), sig=(kc == NKC - 1))
                    s.op("act", [gres], [bres], lambda e, gb=gb, gp=gp: e.mul(out=gb[:, 4:516], in_=gp[:, :], mul=1.0))
                    s.op("dve", ["gprev"], [bres], lambda e, gb=gb, j=j: e.tensor_copy(out=gb[:, 2:4], in_=gprev[:, j, :]))
                    s.op("dve", [bres], ["gprev"], lambda e, gb=gb, j=j: e.tensor_copy(out=gprev[:, j, :], in_=gb[:, 514:516]))
                    s.op("dve", [bres, "cv"], [ares],
                         lambda e, gb=gb, ac=ac, j=j: e.tensor_scalar(out=ac[:, :], in0=gb[:, 2:514], scalar1=cv[:, 4 * j:4 * j + 1],
                                                                      scalar2=cv[:, 4 * j + 3:4 * j + 4], op0=ALU.mult, op1=ALU.add))
                    s.op("dve", [bres, "cv", ares], [ares],
                         lambda e, gb=gb, ac=ac, j=j: e.scalar_tensor_tensor(out=ac[:, :], in0=gb[:, 3:515], scalar=cv[:, 4 * j + 1:4 * j + 2],
                                                                             in1=ac[:, :], op0=ALU.mult, op1=ALU.add))
                    s.op("dve", [bres, "cv", ares], [ares],
                         lambda e, gb=gb, ac=ac, j=j: e.scalar_tensor_tensor(out=ac[:, :], in0=gb[:, 4:516], scalar=cv[:, 4 * j + 2:4 * j + 3],
                                                                             in1=ac[:, :], op0=ALU.mult, op1=ALU.add))
                    s.op("act", [ares], [sres], lambda e, ac=ac, sl=sl: e.activation(out=sl[:, :], in_=ac[:, :], func=AF.Silu))
                    s.op("dve", [sres, ures], ["act3"],
                         lambda e, sl=sl, up=up, j=j: e.tensor_tensor(out=act3[:, j, :], in0=sl[:, :], in1=up[:, :], op=ALU.mult))

            if t + 1 < NT:
                if nh == 2 and t + 1 >= NT // 2:
                    nyh, nyc = 1, (t + 1 - NT // 2) * 512
                else:
                    nyh, nyc = 0, 2 + (t + 1) * 512
                s.dma("sp", "ldy", y3[:, :, :], yvs[nyh][:, :, nyc:nyc + 512], ["YR%d" % nyh], ["y3"])
            for m in range(16):
                wd3, wdres = ws.get(plan[(t, "d", m)])
                pres, ps = m_ps.next()
                for j in range(NJ):
                    s.op("pe", ["act3", wdres], [pres],
                         lambda e, j=j, wd3=wd3, ps=ps: e.matmul(ps[:, :], wd3[:, j, :], act3[:, j, :], start=(j == 0), stop=(j == NJ - 1)),
                         sig=(j == NJ - 1))
                s.op("dve", [pres, "x3", "mods"], ["x3"],
                     lambda e, m=m, ps=ps: e.scalar_tensor_tensor(out=x3[:, m, :], in0=ps[:, :], scalar=gt2[:, m:m + 1],
                                                                  in1=x3[:, m, :], op0=ALU.mult, op1=ALU.add))
            if not final:
                s.dma("sp", "stx", xv[:, :, c0:c0 + 512], x3[:, :, :], ["x3"], ["XS"])
                if t == NT - 1:
                    for sl_ in range(4):
                        s.op("dve", ["x3", "msk"], ["xl4"],
                             lambda e, sl_=sl_: e.tensor_scalar(out=xl4[:, sl_, :, :], in0=x3[:, :, 510:512], scalar1=msk[:, 4 + sl_:5 + sl_], scalar2=None,
                                                                op0=ALU.mult))
                    s.dma("sp", "stxl", G["XL"].ap().rearrange("(s kc p) c -> p s kc c", s=4, p=128), xl4[:, :, :, :], ["xl4"], ["XL"])

            if final:
                def sinkn(kc, tmp, tres, t=t):
                    hres, hsb_ = hst.next()
                    s.op("act", [tres, "zero"], [hres],
                         lambda e: e.activation(out=hsb_[:, :], in_=tmp[:, :], func=AF.Identity, bias=sh_n[:, kc:kc + 1]))
                    s.dma("sp", "sth_" + hres, ov[:, kc, t * 512:(t + 1) * 512], hsb_[:, :], [hres], [], is_out=True)
            else:
                def sinkn(kc, tmp, tres, t=t):
                    hs.sink(t, kc, tmp, tres)

            emit_norm(k, "nn", x3, "x3", 512, aeffn, sq3, "act3", ones_bf, ss_ps, "ss_ps", rstd, "rstd", tmps, sinkn)

        if not final:
            s.collective("AllReduce", G["XL"], G["XG"], ["XL"], ["XG"])
            s.dma("sp", "ldxg", xg4[:, :, :, :], G["XG"].ap().rearrange("(s kc p) c -> p s kc c", s=4, p=128), ["XG"], ["xg4"])
            s.op("dve", ["xg4", "msk"], ["xh"],
                 lambda e: e.tensor_scalar(out=xh[:, :, :], in0=xg4[:, 0, :, :], scalar1=msk[:, 0:1], scalar2=None, op0=ALU.mult))
            for sl_ in range(1, 4):
                s.op("dve", ["xg4", "msk", "xh"], ["xh"],
                     lambda e, sl_=sl_: e.scalar_tensor_tensor(out=xh[:, :, :], in0=xg4[:, sl_, :, :], scalar=msk[:, sl_:sl_ + 1], in1=xh[:, :, :],
                                                               op0=ALU.mult, op1=ALU.add))
            s.dma("sp", "stxh", xv[:, :, 0:2], xh[:, :, :], ["xh"], ["XS"])
        s.barrier()
    if not final:
        return lambda: emit_e1(k, G, "late" if split_e1 else "all")
    return None


def build_fused(S):
    k = KB()
    TOK = S // 4
    G = {}
    G["xT"] = k.din("xT", [D, TOK + 2])
    G["cT"] = k.din("cT", [128, 16])
    G["wada"] = k.din("wada", [DEPTH, D, 3072])
    G["bada"] = k.din("bada", [128, 384])
    G["gains"] = k.din("gains", [128, 144])
    G["msk_d"] = k.din("msk", [128, 9])
    G["win"] = k.din("win", [DEPTH, D, NIN_G])
    G["pwm"] = k.din("pwm", [DEPTH, 128, 384])
    G["pmat"] = k.din("pmat", [128, 384])
    G["cst"] = k.din("cst", [128, 640])
    G["pbp"] = k.din("pbp", [DEPTH, 128, 16])
    G["bfp"] = k.din("bfp", [DEPTH, 1, 2])
    G["w_out"] = k.din("w_out", [DEPTH, D, D])
    G["w_gate"] = k.din("w_gate", [DEPTH, D, DFF])
    G["w_up"] = k.din("w_up", [DEPTH, D, DFF])
    G["w_down"] = k.din("w_down", [DEPTH, 16, 128, NJ * 128])
    G["cv"] = k.din("cv", [DEPTH, 128, NJ * 4])
    G["out"] = k.dout("out", [D, TOK])
    G["XS"] = k.dscr("XS", [D, TOK + 2], F32)
    G["NH"] = max(1, TOK // 1024)
    G["PW"] = TOK // G["NH"]
    G["HB"] = [k.dscr("HB%d" % i, [D, G["PW"]], BF16) for i in range(4 * G["NH"])]
    G["HG"] = [k.dscr("HG%d" % i, [D, G["PW"]], BF16) for i in range(4 * G["NH"])]
    if TOK // 512 >= 2:
        G["YS"] = [k.dscr("YSa", [4 * D, TOK // 2 + 2], BF16), k.dscr("YSb", [4 * D, TOK // 2], BF16)]
        G["YR"] = [k.dscr("YRa", [D, TOK // 2 + 2], BF16), k.dscr("YRb", [D, TOK // 2], BF16)]
    else:
        G["YS"] = [k.dscr("YS", [4 * D, TOK + 2], BF16)]
        G["YR"] = [k.dscr("YR", [D, TOK + 2], BF16)]
    G["MB"] = k.dscr("MB", [512, 96], F32)
    G["MG"] = k.dscr("MG", [512, 96], F32)
    G["XL"] = k.dscr("XL", [4 * D, 2], F32)
    G["XG"] = k.dscr("XG", [4 * D, 2], F32)
    G["mods"] = k.sb("mods", [128, 384])
    G["gn"] = k.sb("gn", [128, 144])
    G["msk"] = k.sb("msk_sb", [128, 9])
    G["zero"] = k.sb("zero", [128, 32])
    phase_m(k, G)
    pending = phase_a(k, G, TOK)
    for l in range(DEPTH):
        phase_b(k, G, l, S, after_setup=pending)
        pending = phase_c(k, G, l, TOK)
    k.s.finish("sp")
    global _LAST_SCHED
    _LAST_SCHED = k.s
    return k.close()


def _fm(v):
    v = np.asarray(v, np.float32)
    return np.ascontiguousarray(v.reshape(-1, 128).T)


def _pool_mats(win):
    s_ = np.arange(128)[:, None]
    t_ = np.arange(128)[None, :]
    band = ((t_ - s_) >= 0) & ((t_ - s_) < win)
    eye = np.eye(128, dtype=np.float32)
    pd = band.astype(np.float32) / win - eye
    pd0 = band.astype(np.float32) / np.minimum(t_ + 1, win).astype(np.float32) - eye
    pp = (((t_ + 128 - s_) < win)).astype(np.float32) / win
    return np.ascontiguousarray(np.concatenate([pd0, pd, pp], axis=1).astype(np.float32))


_PROGS = {}
_LAST_SCHED = None


def kernel(x, c, w_ada, b_ada, g_mix, w_in, b_f, pool_w, pool_scale, lru_conv_w, lru_conv_b,
           lru_wa, lru_ba, lru_wi, lru_bi, lru_lambda, w_out, g_ffn, w_ffn_gate, w_ffn_up,
           ffn_conv_w, ffn_conv_b, w_ffn_down, final_g):
    f32 = np.float32
    A = lambda v: np.asarray(v, f32)
    x = A(x)
    B, S, _ = x.shape
    TOK = S // 4
    c, w_ada, b_ada, g_mix, w_in, b_f = A(c), A(w_ada), A(b_ada), A(g_mix), A(w_in), A(b_f)
    pool_w, pool_scale, lru_conv_w, lru_conv_b = A(pool_w), A(pool_scale), A(lru_conv_w), A(lru_conv_b)
    lru_wa, lru_ba, lru_wi, lru_bi, lru_lambda = A(lru_wa), A(lru_ba), A(lru_wi), A(lru_bi), A(lru_lambda)
    w_out, g_ffn, w_ffn_gate, w_ffn_up = A(w_out), A(g_ffn), A(w_ffn_gate), A(w_ffn_up)
    ffn_conv_w, ffn_conv_b, w_ffn_down, final_g = A(ffn_conv_w), A(ffn_conv_b), A(w_ffn_down), A(final_g)

    if S not in _PROGS:
        _PROGS[S] = build_fused(S)
    nc = _PROGS[S]

    bada = np.ascontiguousarray(np.concatenate([_fm(b_ada[l]) for l in range(DEPTH)], axis=1))
    gains = np.ascontiguousarray(np.concatenate([_fm(g_mix[l]) for l in range(DEPTH)] + [_fm(g_ffn[l]) for l in range(DEPTH)]
                                                + [_fm(final_g)], axis=1))
    cst = np.ascontiguousarray(np.concatenate([np.eye(128, dtype=f32),
                                               np.where(np.arange(128)[None, :] >= np.arange(128)[:, None], 0.0, MASKNEG).astype(f32),
                                               np.zeros((128, 384), f32)], axis=1))
    perm = []
    for g in range(4):
        perm += list(range(g * 128, (g + 1) * 128))
        perm += list(range(512 + 2 * g * 128, 512 + (2 * g + 2) * 128))
        perm += list(range(1536 + g * 128, 1536 + (g + 1) * 128))
    w_out_p = np.ascontiguousarray(w_out[:, perm, :])
    wd_l = np.ascontiguousarray(w_ffn_down.reshape(DEPTH, NJ, 128, 16, 128).transpose(0, 3, 2, 1, 4).reshape(DEPTH, 16, 128, NJ * 128))
    cvv = np.zeros((DEPTH, 128, NJ * 4), f32)
    for l in range(DEPTH):
        for kk in range(3):
            cvv[l][:, kk::4] = _fm(ffn_conv_w[l][kk])
        cvv[l][:, 3::4] = _fm(ffn_conv_b[l])
    per_g = []
    for g in range(4):
        cols = []
        for h in range(2):
            cols += list(range(512 + (2 * g + h) * 128, 512 + (2 * g + h + 1) * 128))
        for h in range(2):
            cols += list(range(1536 + (2 * g + h) * 128, 1536 + (2 * g + h + 1) * 128))
        cols += list(range(3592 + g * 128, 3592 + (g + 1) * 128))
        cols += list(range(4104 + g * 128, 4104 + (g + 1) * 128))
        for h in range(2):
            cols += list(range(2560 + (2 * g + h) * 128, 2560 + (2 * g + h + 1) * 128))
        cols += list(range(g * 128, (g + 1) * 128))
        cols += [3584 + 2 * g, 3584 + 2 * g + 1]
        gs = slice(g * 128, (g + 1) * 128)
        pbv = np.zeros((DEPTH, 128, 16), f32)
        for l in range(DEPTH):
            pbv[l][:, 0] = pool_scale[l][gs]
            for kk in range(4):
                pbv[l][:, 1 + kk] = lru_conv_w[l][kk][gs]
            pbv[l][:, 5] = lru_conv_b[l][gs]
            pbv[l][:, 6] = lru_ba[l][gs]
            pbv[l][:, 7] = lru_bi[l][gs]
            pbv[l][:, 8] = lru_lambda[l][gs]
        per_g.append({
            "win": np.ascontiguousarray(w_in[:, :, cols]),
            "pwm": np.ascontiguousarray(np.concatenate([pool_w[:, g], lru_wa[:, g], lru_wi[:, g]], axis=2)),
            "pmat": _pool_mats(POOL_WINDOWS[g]),
            "pbp": pbv,
            "bfp": np.ascontiguousarray(b_f[:, None, 2 * g:2 * g + 2]),
            "wada": np.ascontiguousarray(w_ada[:, :, g * 3072:(g + 1) * 3072]),
        })
    maps = []
    for cid in range(NCORE):
        b, j = cid // 4, cid % 4
        xin = np.zeros((D, TOK + 2), f32)
        if j == 0:
            xin[:, 2:] = x[b, 0:TOK, :].T
        else:
            xin[:, :] = x[b, j * TOK - 2:(j + 1) * TOK, :].T
        msk = np.zeros((128, 9), f32)
        msk[:, j] = 1.0
        if j + 1 < 4:
            msk[:, 4 + j + 1] = 1.0
        msk[:, 8] = 0.0 if j == 0 else 1.0
        m = {"xT": xin, "cT": _fm(c[b]), "bada": bada, "gains": gains, "msk": msk, "cst": cst,
             "w_out": w_out_p, "w_gate": w_ffn_gate, "w_up": w_ffn_up, "w_down": wd_l, "cv": cvv}
        m.update(per_g[j])
        maps.append(m)
    res = run_bass_kernel_spmd(nc, maps, core_ids=list(range(NCORE)))
    r = res.results
    out = np.empty((B, S, D), f32)
    for cid in range(NCORE):
        b, j = cid // 4, cid % 4
        out[b, j * TOK:(j + 1) * TOK, :] = r[cid]["out"].T
    return out
```

```python
import math
from contextlib import ExitStack, contextmanager

import numpy as np
import ml_dtypes

import concourse.bass as bass
import concourse.mybir as mybir
from concourse.bass_utils import run_bass_kernel_spmd

F32 = mybir.dt.float32
BF16 = mybir.dt.bfloat16
AF = mybir.ActivationFunctionType
ALU = mybir.AluOpType

D = 2048
NKC = 16
DFF = 5632
NJ = 44
DEPTH = 4
NCORE = 8
EPS = 1e-6
POOL_WINDOWS = (2, 4, 8, 16)
NIN = 4616
C_Q, C_K, C_ZX, C_ZY, C_V, C_ZP, C_F = 0, 256, 512, 640, 768, 1024, 1152
NIN_G = 1154
MASKNEG = -30000.0


class Sched:
    EPOCH = 30000

    def __init__(self, nc, es):
        self.nc = nc
        self.es = es
        self.engs = {"pe": nc.tensor, "act": nc.scalar, "dve": nc.vector, "pool": nc.gpsimd, "sp": nc.sync}
        self.sems = {}
        self.count = {}
        self.step = {}
        self.waited = {e: {} for e in self.engs}
        self.lastw = {}
        self.readers = {}
        self.ninstr = 0
        self.outs = {}
        self.multiw = {}
        self.log = {e: [] for e in self.engs}

    def _sem(self, chan, ep):
        key = (chan, ep)
        if key not in self.sems:
            self.sems[key] = self.es.enter_context(self.nc.semaphore("s_%s_%d" % (chan, ep)))
        return self.sems[key]

    def _chan(self, chan, step):
        if chan not in self.count:
            self.count[chan] = 0
            self.step[chan] = step

    def _wait(self, eng, chan, total):
        if total <= self.waited[eng].get(chan, 0):
            return
        self.waited[eng][chan] = total
        esz = self.EPOCH * self.step[chan]
        ep = (total - 1) // esz
        self.engs[eng].wait_ge(self._sem(chan, ep), total - ep * esz)
        self.log[eng].append(("wait", (chan, ep), total - ep * esz))
        self.ninstr += 1

    def _deps(self, eng, me, reads, writes):
        for r in reads:
            lw = self.lastw.get(r)
            if lw is not None:
                self._wait(eng, lw[0], lw[1])
        for w in writes:
            lw = self.lastw.get(w)
            if lw is not None and lw[0] != me:
                self._wait(eng, lw[0], lw[1])
            for ch, tot in self.readers.get(w, {}).items():
                if ch != me or me not in ("pe",):
                    self._wait(eng, ch, tot)

    def _record(self, me, total, reads, writes):
        for w in writes:
            self.lastw[w] = (me, total)
            self.readers[w] = {}
        for r in reads:
            d = self.readers.setdefault(r, {})
            d[me] = max(d.get(me, 0), total)

    def op(self, eng, reads, writes, fn, sig=True):
        self._chan(eng, 1)
        for r in reads:
            lw = self.lastw.get(r)
            if lw is not None:
                self._wait(eng, lw[0], lw[1])
        for w in writes:
            lw = self.lastw.get(w)
            if lw is not None and not (eng == "pe" and lw[0] == "pe"):
                self._wait(eng, lw[0], lw[1])
            for ch, tot in self.readers.get(w, {}).items():
                if ch != eng:
                    self._wait(eng, ch, tot)
        ins = fn(self.engs[eng])
        self.ninstr += 1
        total = self.count[eng] + 1
        if sig:
            esz = self.EPOCH
            ep = (total - 1) // esz
            ins.then_inc(self._sem(eng, ep), 1)
            self.log[eng].append(("inc", (eng, ep), 1))
            self.count[eng] = total
        else:
            self.log[eng].append(("nop", None, 0))
        self._record(eng, total, reads, writes)
        return ins

    def dma(self, queue, chan, out, in_, reads, writes, is_out=False, multi_w=()):
        self._chan(chan, 16)
        for r in reads:
            lw = self.lastw.get(r)
            if lw is not None:
                self._wait(queue, lw[0], lw[1])
        for w in writes:
            lw = self.lastw.get(w)
            if lw is not None and lw[0] != chan:
                self._wait(queue, lw[0], lw[1])
            for ch, tot in self.readers.get(w, {}).items():
                self._wait(queue, ch, tot)
        ins = self.engs[queue].dma_start(out=out, in_=in_)
        self.ninstr += 1
        self.count[chan] += 16
        total = self.count[chan]
        assert total < self.EPOCH * 16
        ins.then_inc(self._sem(chan, 0), 16)
        self.log[queue].append(("inc", (chan, 0), 16))
        self._record(chan, total, reads, writes)
        if is_out:
            self.outs[chan] = total
        for r in multi_w:
            self.multiw.setdefault(r, {})[chan] = total
        return ins

    def simulate(self):
        sem = {}
        pos = {e: 0 for e in self.engs}
        progress = True
        while progress:
            progress = False
            for e, lg in self.log.items():
                while pos[e] < len(lg):
                    kind, key, val = lg[pos[e]]
                    if kind == "wait":
                        if sem.get(key, 0) < val:
                            break
                    elif kind == "inc":
                        sem[key] = sem.get(key, 0) + val
                    pos[e] += 1
                    progress = True
        stuck = {e: (pos[e], len(lg), lg[pos[e]] if pos[e] < len(lg) else None) for e, lg in self.log.items() if pos[e] < len(lg)}
        return stuck

    def barrier(self):
        for e in self.engs:
            for ch, tot in self.count.items():
                if ch != e and tot > 0:
                    self._wait(e, ch, tot)
        self.lastw.clear()
        self.readers.clear()
        self.multiw.clear()

    def collective(self, kind, in_t, out_t, reads, writes):
        self._chan("cc", 1)
        for r in reads:
            lw = self.lastw.get(r)
            if lw is not None:
                self._wait("pool", lw[0], lw[1])
            for ch, tot in self.multiw.get(r, {}).items():
                self._wait("pool", ch, tot)
        for w in writes:
            lw = self.lastw.get(w)
            if lw is not None:
                self._wait("pool", lw[0], lw[1])
            for ch, tot in self.readers.get(w, {}).items():
                self._wait("pool", ch, tot)
        ins = self.engs["pool"].collective_compute(kind, ALU.add, replica_groups=[[0, 1, 2, 3], [4, 5, 6, 7]],
                                                   ins=[in_t.ap().opt()], outs=[out_t.ap().opt()], dma_qos="P2")
        self.ninstr += 1
        self.count["cc"] += 1
        ins.then_inc(self._sem("cc", 0))
        self.log["pool"].append(("inc", ("cc", 0), 1))
        self._record("cc", self.count["cc"], reads, writes)
        return ins

    def finish(self, eng):
        for ch, tot in self.outs.items():
            self._wait(eng, ch, tot)


class KB:
    def __init__(self):
        self.nc = bass.Bass("TRN2", target_bir_lowering=False)
        self.es = ExitStack()
        self.s = Sched(self.nc, self.es)
        self._n = 0

    def din(self, name, shape, dt=F32):
        return self.nc.dram_tensor(name, list(shape), dt, kind="ExternalInput").ap()

    def dout(self, name, shape, dt=F32):
        return self.nc.dram_tensor(name, list(shape), dt, kind="ExternalOutput").ap()

    def sb(self, name, shape, dt=F32):
        self._n += 1
        return self.es.enter_context(self.nc.sbuf_tensor("%s_u%d" % (name, self._n), list(shape), dt))

    def ps(self, name, shape=(128, 512), dt=F32):
        self._n += 1
        return self.es.enter_context(self.nc.psum_tensor("%s_u%d" % (name, self._n), list(shape), dt))

    @contextmanager
    def phase(self):
        old = self.es
        self.es = ExitStack()
        try:
            yield
        finally:
            self.es.close()
            self.es = old

    def dscr(self, name, shape, dt=F32):
        return self.nc.dram_tensor(name, list(shape), dt)

    def close(self):
        self.es.close()
        return self.nc


class Rot:
    def __init__(self, items):
        self.items = items
        self.i = 0

    def next(self):
        it = self.items[self.i % len(self.items)]
        self.i += 1
        return it


class WStream:
    def __init__(self, k, name, nslot, slot_elems, prefetch):
        self.k = k
        self.slots = [k.sb("%s_slot%d" % (name, i), [128, slot_elems], BF16) for i in range(nslot)]
        self.name = name
        self.nslot = nslot
        self.pf = prefetch
        self.loads = []
        self.issued = 0

    def add(self, fn):
        self.loads.append(fn)
        return len(self.loads) - 1

    def res(self, i):
        return "%s_w%d" % (self.name, i % self.nslot)

    def _issue(self, i):
        slot = self.slots[i % self.nslot]
        view, pairs = self.loads[i](slot)
        for (o, a) in pairs:
            self.k.s.dma("pool", "%s_c%d" % (self.name, i % self.nslot), o, a, [], [self.res(i)])
        return view

    def get(self, i):
        while self.issued < min(len(self.loads), i + 1 + self.pf):
            self._issue(self.issued)
            self.issued += 1
        slot = self.slots[i % self.nslot]
        view, _ = self.loads[i](slot)
        return view, self.res(i)


def v3(t, a, b):
    return t[:, 0:a * b].rearrange("p (a b) -> p a b", b=b)


def emit_norm(k, tag, x3, xres, W, aeff, sq3, sqres, ones_bf, ss_ps, ss_res, rstd, rstd_res, tmps, sink):
    s = k.s
    s.op("act", [xres], [sqres], lambda e: e.activation(out=sq3[:, 0:NKC, 0:W], in_=x3[:, 0:NKC, 0:W], func=AF.Square))
    for kc in range(NKC):
        s.op("pe", [sqres], [ss_res],
             lambda e, kc=kc: e.matmul(ss_ps[:, 0:W], ones_bf[:, :], sq3[:, kc, 0:W], start=(kc == 0), stop=(kc == NKC - 1)),
             sig=(kc == NKC - 1))
    s.op("act", [ss_res], [rstd_res],
         lambda e: e.activation(out=rstd[:, 0:W], in_=ss_ps[:, 0:W], func=AF.Sqrt, bias=float(D * EPS)))
    s.op("dve", [rstd_res], [rstd_res], lambda e: e.reciprocal(out=rstd[:, 0:W], in_=rstd[:, 0:W]))
    for kc in range(NKC):
        tres, tmp = tmps.next()
        s.op("dve", [xres, rstd_res], [tres],
             lambda e, kc=kc, tmp=tmp: e.scalar_tensor_tensor(out=tmp[:, 0:W], in0=x3[:, kc, 0:W], scalar=aeff[:, kc:kc + 1],
                                                              in1=rstd[:, 0:W], op0=ALU.mult, op1=ALU.mult))
        sink(kc, tmp, tres)


def emit_aeff(k, prm, cg, csc, aeff, res_in, res_out):
    s = k.s
    s.op("dve", [res_in], [res_out],
         lambda e: e.tensor_scalar(out=aeff[:, :], in0=prm[:, csc:csc + 16], scalar1=1.0, scalar2=float(math.sqrt(D)),
                                   op0=ALU.add, op1=ALU.mult))
    s.op("dve", [res_in, res_out], [res_out],
         lambda e: e.tensor_tensor(out=aeff[:, :], in0=aeff[:, :], in1=prm[:, cg:cg + 16], op=ALU.mult))


def emit_aeff2(k, sc_ap, g_ap, aeff, res_out):
    s = k.s
    s.op("dve", ["mods"], [res_out],
         lambda e: e.tensor_scalar(out=aeff[:, :], in0=sc_ap, scalar1=1.0, scalar2=float(math.sqrt(D)), op0=ALU.add, op1=ALU.mult))
    s.op("dve", ["gn", res_out], [res_out], lambda e: e.tensor_tensor(out=aeff[:, :], in0=aeff[:, :], in1=g_ap, op=ALU.mult))


class HSlots:
    def __init__(self, k, G, sh_ap, tag):
        self.k = k
        self.G = G
        s = k.s
        self.shm = k.sb("shm_" + tag, [128, 4, 16])
        for sl in range(4):
            s.op("dve", ["mods", "msk"], ["shm"],
                 lambda e, sl=sl: e.tensor_scalar(out=self.shm[:, sl, :], in0=sh_ap, scalar1=G["msk"][:, sl:sl + 1], scalar2=None, op0=ALU.mult))
        self.stg = Rot([("stg%d" % i, k.sb("stg%d_%s" % (i, tag), [128, 4, 2, 512], BF16)) for i in range(2)])
        self.cur = None

    def sink(self, t, kc, tmp, tres):
        s = self.k.s
        G = self.G
        if kc % 2 == 0:
            self.cur = self.stg.next()
        sres, st = self.cur
        for sl in range(4):
            s.op("act", [tres, "shm", "msk"], [sres],
                 lambda e, sl=sl, st=st: e.activation(out=st[:, sl, kc % 2, :], in_=tmp[:, :], func=AF.Identity,
                                                      bias=self.shm[:, sl, kc:kc + 1], scale=G["msk"][:, sl:sl + 1]))
        if kc % 2 == 1:
            hf, col = (t * 512) // G["PW"], (t * 512) % G["PW"]
            for sl in range(4):
                hi_ = sl * G["NH"] + hf
                hbv = G["HB"][hi_].ap().rearrange("(kc p) t -> p kc t", p=128)
                s.dma("sp", "sthb_" + sres, hbv[:, kc - 1:kc + 1, col:col + 512], st[:, sl, :, :], [sres], [], multi_w=["HB%d" % hi_])


def emit_e1(k, G, which="all"):
    for i in range(4 * G["NH"]):
        early = (G["NH"] == 2 and i % 2 == 0)
        if which == "all" or (which == "early") == early:
            k.s.collective("AllReduce", G["HB"][i], G["HG"][i], ["HB%d" % i], ["HG%d" % i])


def phase_m(k, G):
    s = k.s
    with k.phase():
        c_sb = k.sb("c_sb", [128, 16])
        sig = k.sb("sig", [128, 16])
        cact = k.sb("cact", [128, 16], BF16)
        mq = k.sb("mq", [128, 96])
        mq4 = k.sb("mq4", [128, 4, 96])
        bada = k.sb("bada_sb", [128, 384])
        mq_ps = k.ps("mq_ps")
        s.dma("sp", "ld", c_sb[:, :], G["cT"][:, :], [], ["c_sb"])
        s.dma("sp", "ld2", bada[:, :], G["bada"][:, :], [], ["bada"])
        s.dma("sp", "ld3", G["gn"][:, :], G["gains"][:, :], [], ["gn"])
        s.dma("sp", "ld4", G["msk"][:, :], G["msk_d"][:, :], [], ["msk"])
        s.op("dve", [], ["zero"], lambda e: e.memset(G["zero"][:, :], 0.0))
        s.op("act", ["c_sb"], ["sig"], lambda e: e.activation(out=sig[:, :], in_=c_sb[:, :], func=AF.Sigmoid))
        s.op("dve", ["c_sb", "sig"], ["cact"], lambda e: e.tensor_tensor(out=cact[:, :], in0=c_sb[:, :], in1=sig[:, :], op=ALU.mult))
        ws = WStream(k, "wm", 3, 8192, 2)
        for l in range(DEPTH):
            wv = G["wada"][l].rearrange("(kc p) n -> p kc n", p=128)
            for gq in range(6):
                ws.add(lambda slot, wv=wv, gq=gq: (v3(slot, 16, 512), [(v3(slot, 16, 512), wv[:, :, gq * 512:(gq + 1) * 512])]))
        i = 0
        for l in range(DEPTH):
            for gq in range(6):
                w3, wres = ws.get(i)
                i += 1
                for nn in range(4):
                    col = l * 24 + gq * 4 + nn
                    for kc in range(NKC):
                        s.op("pe", ["cact", wres], ["mq_ps"],
                             lambda e, kc=kc, w3=w3, nn=nn, col=col: e.matmul(mq_ps[:, col:col + 1], w3[:, kc, nn * 128:(nn + 1) * 128], cact[:, kc:kc + 1],
                                                                              start=(kc == 0), stop=(kc == NKC - 1)), sig=(kc == NKC - 1))
        s.op("act", ["mq_ps"], ["mq"], lambda e: e.mul(out=mq[:, :], in_=mq_ps[:, 0:96], mul=1.0))
        for sl in range(4):
            s.op("dve", ["mq", "msk"], ["mq4"],
                 lambda e, sl=sl: e.tensor_scalar(out=mq4[:, sl, :], in0=mq[:, :], scalar1=G["msk"][:, sl:sl + 1], scalar2=None, op0=ALU.mult))
        s.dma("sp", "stm", G["MB"].ap().rearrange("(s p) n -> p s n", p=128), mq4[:, :, :], ["mq4"], ["MB"])
        s.collective("AllReduce", G["MB"], G["MG"], ["MB"], ["MG"])
        mods_v = G["mods"][:, :].rearrange("p (l q i) -> p l q i", q=4, i=24)
        for q in range(4):
            s.dma("sp", "ldm%d" % q, mods_v[:, :, q, :], G["MG"].ap()[q * 128:(q + 1) * 128, :].rearrange("p (l i) -> p l i", i=24), ["MG"], ["mods"])
        s.op("dve", ["mods", "bada"], ["mods"], lambda e: e.tensor_tensor(out=G["mods"][:, :], in0=G["mods"][:, :], in1=bada[:, :], op=ALU.add))
        zerob = k.sb("zerob", [128, 16, 2], BF16)
        s.op("dve", [], ["zerob"], lambda e: e.memset(zerob[:, :, :], 0.0))
        s.dma("sp", "stz", G["YS"][0].ap()[0:D, 0:2].rearrange("(kc p) c -> p kc c", p=128), zerob[:, :, :], ["zerob"], ["YSz"])
        s.dma("sp", "stx0", G["XS"].ap()[:, :], G["xT"][:, :], [], ["XS"])
        s.barrier()


def phase_a(k, G, TOK):
    s = k.s
    NT = TOK // 512
    with k.phase():
        xv = G["xT"].rearrange("(kc p) t -> p kc t", p=128)
        aeff = k.sb("aeff", [128, 16])
        ones_bf = k.sb("ones_bf", [128, 128], BF16)
        xts = [("x%d" % i, k.sb("xa%d" % i, [128, 16, 512])) for i in range(2)]
        sq3 = k.sb("sqa", [128, 16, 512], BF16)
        rstd = k.sb("rstd", [128, 512])
        tmps = Rot([("tmp%d" % i, k.sb("tmpa%d" % i, [128, 512])) for i in range(3)])
        ss_ps = k.ps("ss_ps")
        s.op("dve", [], ["ones"], lambda e: e.memset(ones_bf[:, :], 1.0))
        M = G["mods"]
        emit_aeff2(k, M[:, 16:32], G["gn"][:, 0:16], aeff, "aeff")
        hs = HSlots(k, G, M[:, 0:16], "a")
        for t in range(NT):
            xres, x3 = xts[t % 2]
            s.dma("sp", "ldx%d" % (t % 2), x3[:, :, :], xv[:, :, 2 + t * 512:2 + (t + 1) * 512], [], [xres])
            emit_norm(k, "n", x3, xres, 512, aeff, sq3, "sq", ones_bf, ss_ps, "ss_ps", rstd, "rstd", tmps,
                      lambda kc, tmp, tres, t=t: hs.sink(t, kc, tmp, tres))
        s.barrier()
    return lambda: emit_e1(k, G)


def phase_b(k, G, l, S, after_setup=None):
    s = k.s
    NT = S // 512
    NB = S // 128
    TOK = S // 4
    with k.phase():
        wv = G["win"][l].rearrange("(kc p) n -> p kc n", p=128)
        ysvs = [ys.ap().rearrange("(j g r p) t -> j g p r t", j=4, g=4, p=128) for ys in G["YS"]]
        nh = len(G["YS"])
        win = k.sb("win_sb", [128, 16, 770], BF16)
        wvz = k.sb("wvz", [128, 16, 384], BF16)
        pwm = k.sb("pwm_sb", [128, 384], BF16)
        pmat = k.sb("pmat_sb", [128, 384], BF16)
        cst = k.sb("cst_sb", [128, 640])
        pb = k.sb("pb_sb", [128, 16])
        bfs = k.sb("bf_sb", [1, 2])
        nbf = k.sb("nbf", [1, 2])
        cA = k.sb("cA", [128, 2])
        lt = k.sb("lt", [128, 2])
        ones_bf = k.sb("ones_bf", [128, 128], BF16)
        ones2 = k.sb("ones2", [128, 128], BF16)
        ones_row = k.sb("ones_row", [1, 512])
        hts = [("ht%d" % i, k.sb("ht%d" % i, [128, 16, 512], BF16)) for i in range(2)]
        kT = k.sb("kT", [128, 2, S], BF16)
        vtok = k.sb("vtok", [128, NB, 256], BF16)
        zptok = k.sb("zptok", [128, 8, 128], BF16)
        qT = k.sb("qT", [128, 2, 512], BF16)
        FQ = k.sb("FQ", [128, 2, 512], BF16)
        Gk = k.sb("Gk", [128, 2, NB])
        Gq = [k.sb("Gq%d" % h, [1, 512]) for h in range(2)]
        Gc = k.sb("Gc", [1, 2])
        sp1 = k.sb("sp1", [1, 512])
        lsp = k.sb("lsp", [1, 512])
        hib = k.sb("hib", [1, 512], BF16)
        hif = lsp
        lo = sp1
        gnb = [k.sb("gnb%d" % h, [128, 512]) for h in range(2)]
        nlo = k.sb("nlo", [1, 512], BF16)
        pts = Rot([("pt%d" % i, k.sb("pt%d" % i, [128, 512], BF16)) for i in range(4)])
        dgs = Rot([("dg%d" % i, k.sb("dg%d" % i, [128, 512])) for i in range(1)])
        rden = k.sb("rden", [128, 512])
        yb = k.sb("yb", [128, 4, 512], BF16)
        yres = "yb"
        yms = Rot([("ym%d" % i, k.sb("ym%d" % i, [128, 4, 512], BF16)) for i in range(2)])
        zxb = k.sb("zxb", [128, 516])
        zy = k.sb("zy", [128, 512])
        dT = k.sb("dT", [128, 512], BF16)
        xc = k.sb("xc", [128, 512])
        xcb = k.sb("xcb", [128, 512], BF16)
        gr = k.sb("gr", [128, 512])
        gi = k.sb("gi", [128, 512])
        av = k.sb("av", [128, 512])
        a2 = k.sb("a2", [128, 512])
        mm_ = k.sb("mm", [128, 512])
        inp = k.sb("inp", [128, 512])
        hh = k.sb("hh", [128, 512])
        hc = k.sb("hc", [128, 1])
        uu = k.sb("uu", [128, 512])
        sg = k.sb("sg", [128, 512])
        pj = Rot([("pj%d" % i, k.ps("pj%d" % i)) for i in range(2)])
        sts = Rot([("st%d" % i, k.ps("st%d" % i)) for i in range(4)])
        o_one = k.ps("o_ps")
        d_one = k.ps("d_ps")
        o_ps = [o_one, o_one]
        d_ps = [d_one, d_one]
        ident = cst[:, 0:128]
        trif = cst[:, 128:640]
        msk = G["msk"]

        for q4 in range(4):
            s.dma("pool", "ldw", win[:, 4 * q4:4 * q4 + 4, 0:768], wv[:, 4 * q4:4 * q4 + 4, 0:768], [], ["win"])
        s.dma("pool", "ldw", win[:, :, 768:770], wv[:, :, C_F:C_F + 2], [], ["win"])
        s.dma("pool", "ldw4", wvz[:, :, :], wv[:, :, C_V:C_V + 384], [], ["wvz"])
        s.dma("pool", "ldw2", pwm[:, :], G["pwm"][l], [], ["pwm"])
        s.dma("pool", "ldw3", pmat[:, :], G["pmat"][:, :], [], ["pmat"])
        s.dma("sp", "ldc", cst[:, :], G["cst"][:, :], [], ["cst"])
        s.dma("sp", "ldc2", pb[:, :], G["pbp"][l], [], ["pb"])
        s.dma("sp", "ldc3", bfs[:, :], G["bfp"][l], [], ["bf"])
        s.op("dve", [], ["ones"], lambda e: e.memset(ones_bf[:, :], 1.0))
        s.op("dve", [], ["ones2"], lambda e: e.memset(ones2[:, :], 0.0))
        s.op("dve", ["ones2"], ["ones2"], lambda e: e.memset(ones2[0:2, :], 1.0))
        s.op("dve", [], ["ones_row"], lambda e: e.memset(ones_row[:, :], 1.0))
        s.op("dve", [], ["FQ0", "FQ1"], lambda e: e.memset(FQ[:, :, :], 0.0))
        s.op("dve", [], ["zxb"], lambda e: e.memset(zxb[:, 0:4], 0.0))
        s.op("dve", [], ["hc"], lambda e: e.memset(hc[:, :], 0.0))
        s.op("dve", [], ["Gc0", "Gc1"], lambda e: e.memset(Gc[:, :], 0.0))
        s.op("dve", ["bf"], ["nbf"], lambda e: e.tensor_scalar(out=nbf[:, :], in0=bfs[:, :], scalar1=-1.0, scalar2=None, op0=ALU.mult))
        s.op("act", ["pb"], ["lt"], lambda e: e.activation(out=lt[:, 0:1], in_=pb[:, 8:9], func=AF.Exp, scale=-1.0))
        s.op("act", ["lt"], ["lt"], lambda e: e.activation(out=lt[:, 1:2], in_=lt[:, 0:1], func=AF.Ln, bias=1.0))
        s.op("dve", ["lt"], ["cA"], lambda e: e.tensor_scalar(out=cA[:, 0:1], in0=lt[:, 1:2], scalar1=-8.0, scalar2=None, op0=ALU.mult))
        s.op("dve", ["lt", "cA"], ["cA"], lambda e: e.tensor_scalar(out=cA[:, 1:2], in0=lt[:, 1:2], scalar1=-16.0, scalar2=None, op0=ALU.mult))

        if after_setup is not None:
            after_setup()

        QSCALE = float(128 ** -0.5)

        def fm_proj(col, ht3, hres):
            pres, ps = pj.next()
            for kc in range(NKC):
                s.op("pe", [hres, "win"], [pres],
                     lambda e, kc=kc, ps=ps: e.matmul(ps[:, :], win[:, kc, col:col + 128], ht3[:, kc, :], start=(kc == 0), stop=(kc == NKC - 1)),
                     sig=(kc == NKC - 1))
            return pres, ps

        tps = TOK // 512

        def load_h(T):
            hres, ht3 = hts[T % 2]
            sl_, tt = T // tps, T % tps
            hgi = sl_ * G["NH"] + (tt * 512) // G["PW"]
            hcol = (tt * 512) % G["PW"]
            hgv = G["HG"][hgi].ap().rearrange("(kc p) t -> p kc t", p=128)
            s.dma("sp", "ldh%d" % (T % 2), ht3[:, :, :], hgv[:, :, hcol:hcol + 512], ["HG%d" % hgi], [hres])

        load_h(0)
        for T in range(NT):
            t0 = T * 512
            hres, ht3 = hts[T % 2]
            if T + 1 < NT:
                load_h(T + 1)

            f_pss = []
            for h in range(2):
                pres, ps = sts.next()
                for kc in range(NKC):
                    s.op("pe", [hres, "win"], [pres],
                         lambda e, kc=kc, ps=ps, h=h: e.matmul(ps[0:1, :], win[:, kc, 768 + h:769 + h], ht3[:, kc, :],
                                                               start=(kc == 0), stop=(kc == NKC - 1)), sig=(kc == NKC - 1))
                f_pss.append((pres, ps))
            for h in range(2):
                pres, ps = f_pss[h]
                s.op("act", [pres, "nbf"], ["sp1"],
                     lambda e, ps=ps, h=h: e.activation(out=sp1[:, :], in_=ps[0:1, :], func=AF.Exp, bias=nbf[0:1, h:h + 1], scale=-1.0))
                s.op("act", ["sp1"], ["lsp"], lambda e: e.activation(out=lsp[:, :], in_=sp1[:, :], func=AF.Ln, bias=1.0))
                s.op("dve", ["lsp", "ones_row", "Gc%d" % h], ["Gq%d" % h],
                     lambda e, h=h: e.tensor_tensor_scan(out=Gq[h][:, :], data0=ones_row[:, :], data1=lsp[:, :], initial=Gc[0:1, h:h + 1],
                                                         op0=ALU.mult, op1=ALU.add))
                s.op("dve", ["Gq%d" % h], ["Gc%d" % h], lambda e, h=h: e.tensor_copy(out=Gc[0:1, h:h + 1], in_=Gq[h][:, 511:512]))
                s.op("dve", ["Gq%d" % h], ["hib"], lambda e, h=h: e.tensor_copy(out=hib[:, :], in_=Gq[h][:, :]))
                s.op("dve", ["hib", "lsp"], ["lsp"], lambda e: e.tensor_copy(out=hif[:, :], in_=hib[:, :]))
                s.op("dve", ["Gq%d" % h, "lsp", "sp1"], ["sp1"], lambda e, h=h: e.tensor_tensor(out=lo[:, :], in0=Gq[h][:, :], in1=hif[:, :], op=ALU.subtract))
                s.op("dve", ["lsp"], ["FQ%d" % h],
                     lambda e, h=h: e.tensor_scalar(out=FQ[0:1, h, :], in0=hif[:, :], scalar1=-1.0, scalar2=None, op0=ALU.mult))
                s.op("dve", ["sp1"], ["nlo"], lambda e: e.tensor_scalar(out=nlo[:, :], in0=lo[:, :], scalar1=-1.0, scalar2=None, op0=ALU.mult))
                s.dma("sp", "fq%d" % h, FQ[1:2, h, :], nlo[0:1, :], ["nlo"], ["FQ%d" % h])

            for h in range(2):
                pres, ps = fm_proj(C_Q + h * 128, ht3, hres)
                s.op("act", [pres], ["qT%d" % h], lambda e, ps=ps, h=h: e.mul(out=qT[:, h, :], in_=ps[:, :], mul=QSCALE))
            for h in range(2):
                pres, ps = fm_proj(C_K + h * 128, ht3, hres)
                s.op("dve", [pres], ["kT"], lambda e, ps=ps, h=h: e.tensor_copy(out=kT[:, h, t0:t0 + 512], in_=ps[:, :]))
            pres, ps = fm_proj(C_ZX, ht3, hres)
            s.op("act", [pres], ["zxb"], lambda e, ps=ps: e.mul(out=zxb[:, 4:516], in_=ps[:, :], mul=1.0))
            pres, ps = fm_proj(C_ZY, ht3, hres)
            s.op("act", [pres], ["zy"], lambda e, ps=ps: e.mul(out=zy[:, :], in_=ps[:, :], mul=1.0))
            for jb in range(4):
                blk = 4 * T + jb
                pres, ps = pj.next()
                for kc in range(NKC):
                    s.op("pe", [hres, "wvz"], [pres],
                         lambda e, kc=kc, ps=ps, jb=jb: e.matmul(ps[:, 0:384], ht3[:, kc, jb * 128:(jb + 1) * 128], wvz[:, kc, :],
                                                                 start=(kc == 0), stop=(kc == NKC - 1)), sig=(kc == NKC - 1))
                s.op("act", [pres], ["vtok"], lambda e, ps=ps, blk=blk: e.mul(out=vtok[:, blk, :], in_=ps[:, 0:256], mul=1.0))
                s.op("act", [pres], ["zptok"], lambda e, ps=ps, blk=blk: e.mul(out=zptok[:, blk % 8, :], in_=ps[:, 256:384], mul=1.0))

            tp_res, tp_ps = pj.next()
            for h in range(2):
                gres_, gps_ = sts.next()
                s.op("pe", ["ones2", "FQ%d" % h], [gres_], lambda e, h=h, gps_=gps_: e.matmul(gps_[:, :], ones2[:, :], FQ[:, h, :], start=True, stop=True))
                s.op("act", [gres_], ["gnb%d" % h], lambda e, h=h, gps_=gps_: e.mul(out=gnb[h][:, :], in_=gps_[:, :], mul=1.0))
                for jb in range(4):
                    s.op("pe", ["Gq%d" % h, "cst"], [tp_res],
                         lambda e, h=h, jb=jb: e.transpose(tp_ps[:, h * 4 + jb:h * 4 + jb + 1], Gq[h][0:1, jb * 128:(jb + 1) * 128], ident[0:1, 0:1]),
                         sig=(jb == 3))
            s.op("dve", [tp_res], ["Gk"], lambda e, T=T: e.tensor_copy(out=Gk[:, 0, 4 * T:4 * T + 4], in_=tp_ps[:, 0:4]))
            s.op("dve", [tp_res], ["Gk"], lambda e, T=T: e.tensor_copy(out=Gk[:, 1, 4 * T:4 * T + 4], in_=tp_ps[:, 4:8]))

            s.op("dve", ["zxb", "pb"], ["xc"],
                 lambda e: e.tensor_scalar(out=xc[:, :], in0=zxb[:, 1:513], scalar1=pb[:, 1:2], scalar2=pb[:, 5:6], op0=ALU.mult, op1=ALU.add))
            for kk in range(1, 4):
                s.op("dve", ["zxb", "pb", "xc"], ["xc"],
                     lambda e, kk=kk: e.scalar_tensor_tensor(out=xc[:, :], in0=zxb[:, 1 + kk:513 + kk], scalar=pb[:, 1 + kk:2 + kk], in1=xc[:, :],
                                                             op0=ALU.mult, op1=ALU.add))
            s.op("dve", ["zxb"], ["zxb"], lambda e: e.tensor_copy(out=zxb[:, 1:4], in_=zxb[:, 513:516]))
            s.op("act", ["xc"], ["xcb"], lambda e: e.mul(out=xcb[:, :], in_=xc[:, :], mul=1.0))

            nkb = 4 * T + 4
            blocks = [(h, kb) for h in range(2) for kb in range(nkb)]
            info = {}

            def emit_qk(i):
                h, kb = blocks[i]
                jj = kb - 4 * T
                c0 = jj * 128 if jj >= 0 else 0
                stres, st = sts.next()
                ptres, pt = pts.next()
                info[i] = (ptres, pt, c0)
                s.op("pe", ["kT", "qT%d" % h], [stres],
                     lambda e: e.matmul(st[:, c0:512], kT[:, h, kb * 128:(kb + 1) * 128], qT[:, h, c0:512], start=True, stop=True))
                s.op("dve", [stres, "gnb%d" % h], [stres],
                     lambda e: e.tensor_tensor(out=st[:, c0:512], in0=st[:, c0:512], in1=gnb[h][:, c0:512], op=ALU.add))
                if jj >= 0:
                    dgres, dg = dgs.next()
                    n = 512 - c0
                    s.op("dve", [stres, "cst"], [dgres],
                         lambda e: e.tensor_tensor(out=dg[:, 0:n], in0=st[:, c0:512], in1=trif[:, 0:n], op=ALU.add))
                    s.op("act", [dgres, "Gk"], [ptres],
                         lambda e: e.activation(out=pt[:, c0:512], in_=dg[:, 0:n], func=AF.Exp, bias=Gk[:, h, kb:kb + 1]))
                else:
                    s.op("act", [stres, "Gk"], [ptres],
                         lambda e: e.activation(out=pt[:, :], in_=st[:, :], func=AF.Exp, bias=Gk[:, h, kb:kb + 1]))

            def emit_pv(i):
                h, kb = blocks[i]
                ptres, pt, c0 = info.pop(i)
                last = (kb == nkb - 1)
                s.op("pe", ["vtok", ptres], ["o_ps"],
                     lambda e: e.matmul(o_ps[h][:, c0:512], vtok[:, kb, h * 128:(h + 1) * 128], pt[:, c0:512], start=(kb == 0), stop=last), sig=False)
                s.op("pe", ["ones", ptres], ["d_ps"],
                     lambda e: e.matmul(d_ps[h][:, c0:512], ones_bf[:, :], pt[:, c0:512], start=(kb == 0), stop=last))
                if last:
                    s.op("dve", ["d_ps"], ["rden"], lambda e: e.reciprocal(out=rden[:, :], in_=d_ps[h][:, :]))
                    s.op("dve", ["o_ps", "d_ps", "rden"], [yres],
                         lambda e: e.tensor_tensor(out=yb[:, 1 + h, :], in0=o_ps[h][:, :], in1=rden[:, :], op=ALU.mult))

            LA = 3
            for i in range(min(LA, len(blocks))):
                emit_qk(i)
            for i in range(len(blocks)):
                if i + LA < len(blocks):
                    emit_qk(i + LA)
                emit_pv(i)

            pres, ps = pj.next()
            for jb in range(4):
                blk = 4 * T + jb
                pc0 = 0 if blk == 0 else 128
                s.op("pe", ["zptok", "pmat"], [pres],
                     lambda e, ps=ps, jb=jb, blk=blk, pc0=pc0: e.matmul(ps[:, jb * 128:(jb + 1) * 128], zptok[:, blk % 8, :], pmat[:, pc0:pc0 + 128],
                                                                        start=True, stop=(blk == 0)), sig=(blk == 0))
                if blk > 0:
                    s.op("pe", ["zptok", "pmat"], [pres],
                         lambda e, ps=ps, jb=jb, blk=blk: e.matmul(ps[:, jb * 128:(jb + 1) * 128], zptok[:, (blk - 1) % 8, :], pmat[:, 256:384],
                                                                   start=False, stop=True), sig=True)
            s.op("act", [pres], ["dT"], lambda e, ps=ps: e.mul(out=dT[:, :], in_=ps[:, :], mul=1.0))
            pres2, ps2 = pj.next()
            s.op("pe", ["dT", "pwm"], [pres2], lambda e, ps2=ps2: e.matmul(ps2[:, :], pwm[:, 0:128], dT[:, :], start=True, stop=True))
            s.op("dve", [pres2, "pb"], [yres],
                 lambda e, ps2=ps2: e.tensor_scalar(out=yb[:, 0, :], in0=ps2[:, :], scalar1=pb[:, 0:1], scalar2=None, op0=ALU.mult))

            rres, rps = pj.next()
            s.op("pe", ["xcb", "pwm"], [rres], lambda e, rps=rps: e.matmul(rps[:, :], pwm[:, 128:256], xcb[:, :], start=True, stop=True))
            ires, ips = pj.next()
            s.op("pe", ["xcb", "pwm"], [ires], lambda e, ips=ips: e.matmul(ips[:, :], pwm[:, 256:384], xcb[:, :], start=True, stop=True))
            s.op("act", [rres, "pb"], ["gr"], lambda e, rps=rps: e.activation(out=gr[:, :], in_=rps[:, :], func=AF.Sigmoid, bias=pb[:, 6:7]))
            s.op("act", [ires, "pb"], ["gi"], lambda e, ips=ips: e.activation(out=gi[:, :], in_=ips[:, :], func=AF.Sigmoid, bias=pb[:, 7:8]))
            s.op("act", ["gr", "cA"], ["av"], lambda e: e.activation(out=av[:, :], in_=gr[:, :], func=AF.Exp, scale=cA[:, 0:1]))
            s.op("act", ["gr", "cA"], ["a2"], lambda e: e.activation(out=a2[:, :], in_=gr[:, :], func=AF.Exp, scale=cA[:, 1:2]))
            s.op("dve", ["a2"], ["mm"], lambda e: e.tensor_scalar(out=mm_[:, :], in0=a2[:, :], scalar1=-1.0, scalar2=1.0, op0=ALU.mult, op1=ALU.add))
            s.op("dve", ["mm"], ["mm"], lambda e: e.tensor_scalar(out=mm_[:, :], in0=mm_[:, :], scalar1=1e-30, scalar2=None, op0=ALU.max))
            s.op("act", ["mm"], ["mm"], lambda e: e.activation(out=mm_[:, :], in_=mm_[:, :], func=AF.Sqrt))
            s.op("dve", ["gi", "xc"], ["inp"], lambda e: e.tensor_tensor(out=inp[:, :], in0=gi[:, :], in1=xc[:, :], op=ALU.mult))
            s.op("dve", ["inp", "mm"], ["inp"], lambda e: e.tensor_tensor(out=inp[:, :], in0=inp[:, :], in1=mm_[:, :], op=ALU.mult))
            s.op("dve", ["av", "inp", "hc"], ["hh"],
                 lambda e: e.tensor_tensor_scan(out=hh[:, :], data0=av[:, :], data1=inp[:, :], initial=hc[:, 0:1], op0=ALU.mult, op1=ALU.add))
            s.op("dve", ["hh"], ["hc"], lambda e: e.tensor_copy(out=hc[:, :], in_=hh[:, 511:512]))
            s.op("dve", ["zy"], ["uu"], lambda e: e.tensor_tensor(out=uu[:, :], in0=zy[:, :], in1=zy[:, :], op=ALU.mult))
            s.op("dve", ["uu"], ["uu"], lambda e: e.tensor_scalar(out=uu[:, :], in0=uu[:, :], scalar1=0.044715, scalar2=1.0, op0=ALU.mult, op1=ALU.add))
            s.op("dve", ["uu", "zy"], ["uu"], lambda e: e.tensor_tensor(out=uu[:, :], in0=uu[:, :], in1=zy[:, :], op=ALU.mult))
            s.op("act", ["uu"], ["sg"], lambda e: e.activation(out=sg[:, :], in_=uu[:, :], func=AF.Sigmoid, scale=1.5957691216057308))
            s.op("dve", ["sg", "zy"], ["sg"], lambda e: e.tensor_tensor(out=sg[:, :], in0=sg[:, :], in1=zy[:, :], op=ALU.mult))
            s.op("dve", ["sg", "hh"], [yres], lambda e: e.tensor_tensor(out=yb[:, 3, :], in0=sg[:, :], in1=hh[:, :], op=ALU.mult))

            j = T // tps
            tt = T % tps
            if nh == 2 and tt >= tps // 2:
                half, cbase = 1, (tt - tps // 2) * 512
            else:
                half, cbase = 0, 2 + tt * 512
            for g2 in range(4):
                ymres, ym = yms.next()
                s.op("dve", [yres, "msk"], [ymres],
                     lambda e, ym=ym, g2=g2: e.tensor_scalar(out=ym[:, :, :], in0=yb[:, :, :], scalar1=msk[:, g2:g2 + 1], scalar2=None, op0=ALU.mult))
                s.dma("sp", "sty_" + ymres, ysvs[half][j, g2, :, :, cbase:cbase + 512], ym[:, :, :], [ymres], [], multi_w=["YS%d" % half])
                if tt == tps - 1 and j < 3:
                    s.dma("sp", "sty_" + ymres, ysvs[0][j + 1, g2, :, :, 0:2], ym[:, :, 510:512], [ymres], [], multi_w=["YS0"])
            if nh == 2 and T == 3 * tps + tps // 2 - 1:
                s.collective("ReduceScatter", G["YS"][0], G["YR"][0], ["YS0"], ["YR0"])
        if nh == 1:
            s.collective("ReduceScatter", G["YS"][0], G["YR"][0], ["YS0"], ["YR0"])
            s.barrier()
        else:
            s.barrier()
            s.collective("ReduceScatter", G["YS"][1], G["YR"][1], ["YS1"], ["YR1"])


def phase_c(k, G, l, TOK):
    s = k.s
    NT = TOK // 512
    final = (l == DEPTH - 1)
    with k.phase():
        xv = G["XS"].ap().rearrange("(kc p) t -> p kc t", p=128)
        yvs = [yr.ap().rearrange("(kc p) t -> p kc t", p=128) for yr in G["YR"]]
        nh = len(yvs)
        ov = G["out"].rearrange("(kc p) t -> p kc t", p=128)
        wov = G["w_out"][l].rearrange("(kc p) n -> p kc n", p=128)
        wgv = G["w_gate"][l].rearrange("(kc p) n -> p kc n", p=128)
        wuv = G["w_up"][l].rearrange("(kc p) n -> p kc n", p=128)
        w_down = G["w_down"][l]
        M = G["mods"]
        GN = G["gn"]
        msk = G["msk"]
        mo = l * 96
        gt1 = M[:, mo + 32:mo + 48]
        sh2 = M[:, mo + 48:mo + 64]
        sc2 = M[:, mo + 64:mo + 80]
        gt2 = M[:, mo + 80:mo + 96]
        if not final:
            g_n = GN[:, (l + 1) * 16:(l + 2) * 16]
            sh_n = M[:, mo + 96:mo + 112]
            sc_n = M[:, mo + 112:mo + 128]
        else:
            g_n = GN[:, 128:144]
            sh_n = G["zero"][:, 0:16]
            sc_n = G["zero"][:, 0:16]

        cv = k.sb("cv_sb", [128, NJ * 4])
        aeff2 = k.sb("aeff2", [128, 16])
        aeffn = k.sb("aeffn", [128, 16])
        ones_bf = k.sb("ones_bf", [128, 128], BF16)
        x3 = k.sb("x3", [128, 16, 512])
        y3 = k.sb("y3", [128, 16, 512], BF16)
        h23 = y3
        act3 = k.sb("act3", [128, NJ, 512], BF16)
        sq3 = act3
        xh = k.sb("xh", [128, 16, 2])
        yh = k.sb("yh", [128, 16, 2], BF16)
        h2h = k.sb("h2h", [128, 16, 2], BF16)
        gprev = k.sb("gprev", [128, NJ, 2])
        xl4 = k.sb("xl4", [128, 4, 16, 2])
        xg4 = k.sb("xg4", [128, 4, 16, 2])
        gbuf = [("gbuf%d" % i, k.sb("gbuf%d" % i, [128, 516])) for i in range(2)]
        acc = [("acc%d" % i, k.sb("acc%d" % i, [128, 512])) for i in range(2)]
        sil = [("sil%d" % i, k.sb("sil%d" % i, [128, 512])) for i in range(2)]
        rstd = k.sb("rstd", [128, 512])
        tmps = Rot([("tmp%d" % i, k.sb("tmp%d" % i, [128, 512])) for i in range(3)])
        g_ps = [("g_ps%d" % i, k.ps("g_ps%d" % i)) for i in range(2)]
        u_ps = [("u_ps%d" % i, k.ps("u_ps%d" % i)) for i in range(2)]
        m_ps = Rot([("m_ps%d" % i, k.ps("m_ps%d" % i)) for i in range(3)])
        ss_ps = k.ps("ss_ps")
        if final:
            hst = Rot([("hst%d" % i, k.sb("hst%d" % i, [128, 512])) for i in range(2)])
            hs = None
        else:
            hs = HSlots(k, G, sh_n, "c")

        split_e1 = (not final) and G["NH"] == 2 and NT == 4
        s.dma("sp", "ldp2", cv[:, :], G["cv"][l], [], ["cv"])
        s.op("dve", [], ["ones"], lambda e: e.memset(ones_bf[:, :], 1.0))
        emit_aeff2(k, sc2, GN[:, 64 + l * 16:80 + l * 16], aeff2, "aeff2")
        emit_aeff2(k, sc_n, g_n, aeffn, "aeffn")

        ws = WStream(k, "w", 4, 8192, 2)
        plan = {}

        def add_out(n):
            return ws.add(lambda slot: (v3(slot, 16, 512), [(v3(slot, 16, 512), wov[:, :, n * 512:(n + 1) * 512])]))

        def add_gu(wview, jg):
            return ws.add(lambda slot: (v3(slot, 16, 512), [(v3(slot, 16, 512), wview[:, :, jg * 512:(jg + 1) * 512])]))

        def add_down(m):
            return ws.add(lambda slot: (v3(slot, NJ, 128), [(slot[:, 0:NJ * 128], w_down[m, :, :])]))

        for t in range(NT):
            for n in range(4):
                plan[(t, "out", n)] = add_out(n)
            for jg in range(NJ // 4):
                plan[(t, "g", jg)] = add_gu(wgv, jg)
                plan[(t, "u", jg)] = add_gu(wuv, jg)
            for m in range(16):
                plan[(t, "d", m)] = add_down(m)

        def outproj(keyfn, segs):
            for n in range(4):
                w3, wres = ws.get(plan[keyfn(n)])
                for mm in range(4):
                    m = n * 4 + mm
                    for (yy3, yres, xx3, xres, W) in segs:
                        pres, ps = m_ps.next()
                        for kc in range(NKC):
                            s.op("pe", [yres, wres], [pres],
                                 lambda e, kc=kc, w3=w3, ps=ps, mm=mm, yy3=yy3, W=W: e.matmul(ps[:, 0:W], w3[:, kc, mm * 128:(mm + 1) * 128], yy3[:, kc, 0:W],
                                                                                              start=(kc == 0), stop=(kc == NKC - 1)), sig=(kc == NKC - 1))
                        s.op("dve", [pres, xres, "mods"], [xres],
                             lambda e, m=m, ps=ps, xx3=xx3, W=W: e.scalar_tensor_tensor(out=xx3[:, m, 0:W], in0=ps[:, 0:W], scalar=gt1[:, m:m + 1],
                                                                                        in1=xx3[:, m, 0:W], op0=ALU.mult, op1=ALU.add))

        s.dma("sp", "ldh", xh[:, :, :], xv[:, :, 0:2], ["XS"], ["xh"])
        s.dma("sp", "ldh2", yh[:, :, :], yvs[0][:, :, 0:2], ["YR0"], ["yh"])

        def sink_h(kc, tmp, tres):
            s.op("act", [tres, "mods"], ["h2h"],
                 lambda e: e.activation(out=h2h[:, kc, :], in_=tmp[:, 0:2], func=AF.Identity, bias=sh2[:, kc:kc + 1]))

        for t in range(NT):
            c0 = 2 + t * 512
            if nh == 2 and t >= NT // 2:
                yh_i, yc0 = 1, (t - NT // 2) * 512
            else:
                yh_i, yc0 = 0, 2 + t * 512
            s.dma("sp", "ldx", x3[:, :, :], xv[:, :, c0:c0 + 512], ["XS"], ["x3"])
            if t == 0:
                s.dma("sp", "ldy", y3[:, :, :], yvs[yh_i][:, :, yc0:yc0 + 512], ["YR%d" % yh_i], ["y3"])
            segs = [(y3, "y3", x3, "x3", 512)]
            if t == 0:
                segs = [(yh, "yh", xh, "xh", 2)] + segs
            outproj(lambda n, t=t: (t, "out", n), segs)
            if split_e1 and t == NT // 2:
                emit_e1(k, G, "early")
            if t == 0:
                emit_norm(k, "nh", xh, "xh", 2, aeff2, sq3, "act3", ones_bf, ss_ps, "ss_ps", rstd, "rstd", tmps, sink_h)

            def sink2(kc, tmp, tres):
                s.op("act", [tres, "mods"], ["y3"],
                     lambda e: e.activation(out=h23[:, kc, :], in_=tmp[:, :], func=AF.Identity, bias=sh2[:, kc:kc + 1]))

            emit_norm(k, "n2", x3, "x3", 512, aeff2, sq3, "act3", ones_bf, ss_ps, "ss_ps", rstd, "rstd", tmps, sink2)

            for jg in range(NJ // 4):
                wg3, wgres = ws.get(plan[(t, "g", jg)])
                wu3, wures = ws.get(plan[(t, "u", jg)])
                for jj in range(4):
                    j = jg * 4 + jj
                    gres, gp = g_ps[j % 2]
                    ures, up = u_ps[j % 2]
                    bres, gb = gbuf[j % 2]
                    ares, ac = acc[j % 2]
                    sres, sl = sil[j % 2]
                    if t == 0:
                        pres, ps = m_ps.next()
                        for kc in range(NKC):
                            s.op("pe", ["h2h", wgres], [pres],
                                 lambda e, kc=kc, wg3=wg3, ps=ps, jj=jj: e.matmul(ps[:, 0:2], wg3[:, kc, jj * 128:(jj + 1) * 128], h2h[:, kc, :],
                                                                                  start=(kc == 0), stop=(kc == NKC - 1)), sig=(kc == NKC - 1))
                        s.op("dve", [pres, "msk"], ["gprev"],
                             lambda e, j=j, ps=ps: e.tensor_scalar(out=gprev[:, j, :], in0=ps[:, 0:2], scalar1=msk[:, 8:9], scalar2=None, op0=ALU.mult))
                    for kc in range(NKC):
                        s.op("pe", ["y3", wgres], [gres],
                             lambda e, kc=kc, gp=gp, jj=jj, wg3=wg3: e.matmul(gp[:, :], wg3[:, kc, jj * 128:(jj + 1) * 128], h23[:, kc, :],
                                                                              start=(kc == 0), stop=(kc == NKC - 1)), sig=(kc == NKC - 1))
                    for kc in range(NKC):
                        s.op("pe", ["y3", wures], [ures],
                             lambda e, kc=kc, up=up, jj=jj, wu3=wu3: e.matmul(up[:, :], wu3[:, kc, jj * 128:(jj + 1) * 128], h23[:, kc, :],
                                                                              start=(kc == 0), stop=(kc == NKC - 1)), sig=(kc == NKC - 1))
                    s.op("act", [gres], [bres], lambda e, gb=gb, gp=gp: e.mul(out=gb[:, 4:516], in_=gp[:, :], mul=1.0))
                    s.op("dve", ["gprev"], [bres], lambda e, gb=gb, j=j: e.tensor_copy(out=gb[:, 2:4], in_=gprev[:, j, :]))
                    s.op("dve", [bres], ["gprev"], lambda e, gb=gb, j=j: e.tensor_copy(out=gprev[:, j, :], in_=gb[:, 514:516]))
                    s.op("dve", [bres, "cv"], [ares],
                         lambda e, gb=gb, ac=ac, j=j: e.tensor_scalar(out=ac[:, :], in0=gb[:, 2:514], scalar1=cv[:, 4 * j:4 * j + 1],
                                                                      scalar2=cv[:, 4 * j + 3:4 * j + 4], op0=ALU.mult, op1=ALU.add))
                    s.op("dve", [bres, "cv", ares], [ares],
                         lambda e, gb=gb, ac=ac, j=j: e.scalar_tensor_tensor(out=ac[:, :], in0=gb[:, 3:515], scalar=cv[:, 4 * j + 1:4 * j + 2],
                                                                             in1=ac[:, :], op0=ALU.mult, op1=ALU.add))
                    s.op("dve", [bres, "cv", ares], [ares],
                         lambda e, gb=gb, ac=ac, j=j: e.scalar_tensor_tensor(out=ac[:, :], in0=gb[:, 4:516], scalar=cv[:, 4 * j + 2:4 * j + 3],
                                                                             in1=ac[:, :], op0=ALU.mult, op1=ALU.add))
                    s.op("act", [ares], [sres], lambda e, ac=ac, sl=sl: e.activation(out=sl[:, :], in_=ac[:, :], func=AF.Silu))
                    s.op("dve", [sres, ures], ["act3"],
                         lambda e, sl=sl, up=up, j=j: e.tensor_tensor(out=act3[:, j, :], in0=sl[:, :], in1=up[:, :], op=ALU.mult))

            if t + 1 < NT:
                if nh == 2 and t + 1 >= NT // 2:
                    nyh, nyc = 1, (t + 1 - NT // 2) * 512
                else:
                    nyh, nyc = 0, 2 + (t + 1) * 512
                s.dma("sp", "ldy", y3[:, :, :], yvs[nyh][:, :, nyc:nyc + 512], ["YR%d" % nyh], ["y3"])
            for m in range(16):
                wd3, wdres = ws.get(plan[(t, "d", m)])
                pres, ps = m_ps.next()
                for j in range(NJ):
                    s.op("pe", ["act3", wdres], [pres],
                         lambda e, j=j, wd3=wd3, ps=ps: e.matmul(ps[:, :], wd3[:, j, :], act3[:, j, :], start=(j == 0), stop=(j == NJ - 1)),
                         sig=(j == NJ - 1))
                s.op("dve", [pres, "x3", "mods"], ["x3"],
                     lambda e, m=m, ps=ps: e.scalar_tensor_tensor(out=x3[:, m, :], in0=ps[:, :], scalar=gt2[:, m:m + 1],
                                                                  in1=x3[:, m, :], op0=ALU.mult, op1=ALU.add))
            if not final:
                s.dma("sp", "stx", xv[:, :, c0:c0 + 512], x3[:, :, :], ["x3"], ["XS"])
                if t == NT - 1:
                    for sl_ in range(4):
                        s.op("dve", ["x3", "msk"], ["xl4"],
                             lambda e, sl_=sl_: e.tensor_scalar(out=xl4[:, sl_, :, :], in0=x3[:, :, 510:512], scalar1=msk[:, 4 + sl_:5 + sl_], scalar2=None,
                                                                op0=ALU.mult))
                    s.dma("sp", "stxl", G["XL"].ap().rearrange("(s kc p) c -> p s kc c", s=4, p=128), xl4[:, :, :, :], ["xl4"], ["XL"])
                    s.collective("AllReduce", G["XL"], G["XG"], ["XL"], ["XG"])

            if final:
                def sinkn(kc, tmp, tres, t=t):
                    hres, hsb_ = hst.next()
                    s.op("act", [tres, "zero"], [hres],
                         lambda e: e.activation(out=hsb_[:, :], in_=tmp[:, :], func=AF.Identity, bias=sh_n[:, kc:kc + 1]))
                    s.dma("sp", "sth_" + hres, ov[:, kc, t * 512:(t + 1) * 512], hsb_[:, :], [hres], [], is_out=True)
            else:
                def sinkn(kc, tmp, tres, t=t):
                    hs.sink(t, kc, tmp, tres)

            emit_norm(k, "nn", x3, "x3", 512, aeffn, sq3, "act3", ones_bf, ss_ps, "ss_ps", rstd, "rstd", tmps, sinkn)

        if not final:
            s.dma("sp", "ldxg", xg4[:, :, :, :], G["XG"].ap().rearrange("(s kc p) c -> p s kc c", s=4, p=128), ["XG"], ["xg4"])
            s.op("dve", ["xg4", "msk"], ["xh"],
                 lambda e: e.tensor_scalar(out=xh[:, :, :], in0=xg4[:, 0, :, :], scalar1=msk[:, 0:1], scalar2=None, op0=ALU.mult))
            for sl_ in range(1, 4):
                s.op("dve", ["xg4", "msk", "xh"], ["xh"],
                     lambda e, sl_=sl_: e.scalar_tensor_tensor(out=xh[:, :, :], in0=xg4[:, sl_, :, :], scalar=msk[:, sl_:sl_ + 1], in1=xh[:, :, :],
                                                               op0=ALU.mult, op1=ALU.add))
            s.dma("sp", "stxh", xv[:, :, 0:2], xh[:, :, :], ["xh"], ["XS"])
        s.barrier()
    if not final:
        return lambda: emit_e1(k, G, "late" if split_e1 else "all")
    return None


def build_fused(S):
    k = KB()
    TOK = S // 4
    G = {}
    G["xT"] = k.din("xT", [D, TOK + 2])
    G["cT"] = k.din("cT", [128, 16])
    G["wada"] = k.din("wada", [DEPTH, D, 3072])
    G["bada"] = k.din("bada", [128, 384])
    G["gains"] = k.din("gains", [128, 144])
    G["msk_d"] = k.din("msk", [128, 9])
    G["win"] = k.din("win", [DEPTH, D, NIN_G])
    G["pwm"] = k.din("pwm", [DEPTH, 128, 384])
    G["pmat"] = k.din("pmat", [128, 384])
    G["cst"] = k.din("cst", [128, 640])
    G["pbp"] = k.din("pbp", [DEPTH, 128, 16])
    G["bfp"] = k.din("bfp", [DEPTH, 1, 2])
    G["w_out"] = k.din("w_out", [DEPTH, D, D])
    G["w_gate"] = k.din("w_gate", [DEPTH, D, DFF])
    G["w_up"] = k.din("w_up", [DEPTH, D, DFF])
    G["w_down"] = k.din("w_down", [DEPTH, 16, 128, NJ * 128])
    G["cv"] = k.din("cv", [DEPTH, 128, NJ * 4])
    G["out"] = k.dout("out", [D, TOK])
    G["XS"] = k.dscr("XS", [D, TOK + 2], F32)
    G["NH"] = max(1, TOK // 1024)
    G["PW"] = TOK // G["NH"]
    G["HB"] = [k.dscr("HB%d" % i, [D, G["PW"]], BF16) for i in range(4 * G["NH"])]
    G["HG"] = [k.dscr("HG%d" % i, [D, G["PW"]], BF16) for i in range(4 * G["NH"])]
    if TOK // 512 >= 2:
        G["YS"] = [k.dscr("YSa", [4 * D, TOK // 2 + 2], BF16), k.dscr("YSb", [4 * D, TOK // 2], BF16)]
        G["YR"] = [k.dscr("YRa", [D, TOK // 2 + 2], BF16), k.dscr("YRb", [D, TOK // 2], BF16)]
    else:
        G["YS"] = [k.dscr("YS", [4 * D, TOK + 2], BF16)]
        G["YR"] = [k.dscr("YR", [D, TOK + 2], BF16)]
    G["MB"] = k.dscr("MB", [512, 96], F32)
    G["MG"] = k.dscr("MG", [512, 96], F32)
    G["XL"] = k.dscr("XL", [4 * D, 2], F32)
    G["XG"] = k.dscr("XG", [4 * D, 2], F32)
    G["mods"] = k.sb("mods", [128, 384])
    G["gn"] = k.sb("gn", [128, 144])
    G["msk"] = k.sb("msk_sb", [128, 9])
    G["zero"] = k.sb("zero", [128, 32])
    phase_m(k, G)
    pending = phase_a(k, G, TOK)
    for l in range(DEPTH):
        phase_b(k, G, l, S, after_setup=pending)
        pending = phase_c(k, G, l, TOK)
    k.s.finish("sp")
    global _LAST_SCHED
    _LAST_SCHED = k.s
    return k.close()


def _fm(v):
    v = np.asarray(v, np.float32)
    return np.ascontiguousarray(v.reshape(-1, 128).T)


def _pool_mats(win):
    s_ = np.arange(128)[:, None]
    t_ = np.arange(128)[None, :]
    band = ((t_ - s_) >= 0) & ((t_ - s_) < win)
    eye = np.eye(128, dtype=np.float32)
    pd = band.astype(np.float32) / win - eye
    pd0 = band.astype(np.float32) / np.minimum(t_ + 1, win).astype(np.float32) - eye
    pp = (((t_ + 128 - s_) < win)).astype(np.float32) / win
    return np.ascontiguousarray(np.concatenate([pd0, pd, pp], axis=1).astype(np.float32))


_PROGS = {}
_LAST_SCHED = None


def kernel(x, c, w_ada, b_ada, g_mix, w_in, b_f, pool_w, pool_scale, lru_conv_w, lru_conv_b,
           lru_wa, lru_ba, lru_wi, lru_bi, lru_lambda, w_out, g_ffn, w_ffn_gate, w_ffn_up,
           ffn_conv_w, ffn_conv_b, w_ffn_down, final_g):
    f32 = np.float32
    A = lambda v: np.asarray(v, f32)
    x = A(x)
    B, S, _ = x.shape
    TOK = S // 4
    c, w_ada, b_ada, g_mix, w_in, b_f = A(c), A(w_ada), A(b_ada), A(g_mix), A(w_in), A(b_f)
    pool_w, pool_scale, lru_conv_w, lru_conv_b = A(pool_w), A(pool_scale), A(lru_conv_w), A(lru_conv_b)
    lru_wa, lru_ba, lru_wi, lru_bi, lru_lambda = A(lru_wa), A(lru_ba), A(lru_wi), A(lru_bi), A(lru_lambda)
    w_out, g_ffn, w_ffn_gate, w_ffn_up = A(w_out), A(g_ffn), A(w_ffn_gate), A(w_ffn_up)
    ffn_conv_w, ffn_conv_b, w_ffn_down, final_g = A(ffn_conv_w), A(ffn_conv_b), A(w_ffn_down), A(final_g)

    if S not in _PROGS:
        _PROGS[S] = build_fused(S)
    nc = _PROGS[S]

    bada = np.ascontiguousarray(np.concatenate([_fm(b_ada[l]) for l in range(DEPTH)], axis=1))
    gains = np.ascontiguousarray(np.concatenate([_fm(g_mix[l]) for l in range(DEPTH)] + [_fm(g_ffn[l]) for l in range(DEPTH)]
                                                + [_fm(final_g)], axis=1))
    cst = np.ascontiguousarray(np.concatenate([np.eye(128, dtype=f32),
                                               np.where(np.arange(128)[None, :] >= np.arange(128)[:, None], 0.0, MASKNEG).astype(f32),
                                               np.zeros((128, 384), f32)], axis=1))
    perm = []
    for g in range(4):
        perm += list(range(g * 128, (g + 1) * 128))
        perm += list(range(512 + 2 * g * 128, 512 + (2 * g + 2) * 128))
        perm += list(range(1536 + g * 128, 1536 + (g + 1) * 128))
    w_out_p = np.ascontiguousarray(w_out[:, perm, :])
    wd_l = np.ascontiguousarray(w_ffn_down.reshape(DEPTH, NJ, 128, 16, 128).transpose(0, 3, 2, 1, 4).reshape(DEPTH, 16, 128, NJ * 128))
    cvv = np.zeros((DEPTH, 128, NJ * 4), f32)
    for l in range(DEPTH):
        for kk in range(3):
            cvv[l][:, kk::4] = _fm(ffn_conv_w[l][kk])
        cvv[l][:, 3::4] = _fm(ffn_conv_b[l])
    per_g = []
    for g in range(4):
        cols = []
        for h in range(2):
            cols += list(range(512 + (2 * g + h) * 128, 512 + (2 * g + h + 1) * 128))
        for h in range(2):
            cols += list(range(1536 + (2 * g + h) * 128, 1536 + (2 * g + h + 1) * 128))
        cols += list(range(3592 + g * 128, 3592 + (g + 1) * 128))
        cols += list(range(4104 + g * 128, 4104 + (g + 1) * 128))
        for h in range(2):
            cols += list(range(2560 + (2 * g + h) * 128, 2560 + (2 * g + h + 1) * 128))
        cols += list(range(g * 128, (g + 1) * 128))
        cols += [3584 + 2 * g, 3584 + 2 * g + 1]
        gs = slice(g * 128, (g + 1) * 128)
        pbv = np.zeros((DEPTH, 128, 16), f32)
        for l in range(DEPTH):
            pbv[l][:, 0] = pool_scale[l][gs]
            for kk in range(4):
                pbv[l][:, 1 + kk] = lru_conv_w[l][kk][gs]
            pbv[l][:, 5] = lru_conv_b[l][gs]
            pbv[l][:, 6] = lru_ba[l][gs]
            pbv[l][:, 7] = lru_bi[l][gs]
            pbv[l][:, 8] = lru_lambda[l][gs]
        per_g.append({
            "win": np.ascontiguousarray(w_in[:, :, cols]),
            "pwm": np.ascontiguousarray(np.concatenate([pool_w[:, g], lru_wa[:, g], lru_wi[:, g]], axis=2)),
            "pmat": _pool_mats(POOL_WINDOWS[g]),
            "pbp": pbv,
            "bfp": np.ascontiguousarray(b_f[:, None, 2 * g:2 * g + 2]),
            "wada": np.ascontiguousarray(w_ada[:, :, g * 3072:(g + 1) * 3072]),
        })
    maps = []
    for cid in range(NCORE):
        b, j = cid // 4, cid % 4
        xin = np.zeros((D, TOK + 2), f32)
        if j == 0:
            xin[:, 2:] = x[b, 0:TOK, :].T
        else:
            xin[:, :] = x[b, j * TOK - 2:(j + 1) * TOK, :].T
        msk = np.zeros((128, 9), f32)
        msk[:, j] = 1.0
        if j + 1 < 4:
            msk[:, 4 + j + 1] = 1.0
        msk[:, 8] = 0.0 if j == 0 else 1.0
        m = {"xT": xin, "cT": _fm(c[b]), "bada": bada, "gains": gains, "msk": msk, "cst": cst,
             "w_out": w_out_p, "w_gate": w_ffn_gate, "w_up": w_ffn_up, "w_down": wd_l, "cv": cvv}
        m.update(per_g[j])
        maps.append(m)
    res = run_bass_kernel_spmd(nc, maps, core_ids=list(range(NCORE)))
    r = res.results
    out = np.empty((B, S, D), f32)
    for cid in range(NCORE):
        b, j = cid // 4, cid % 4
        out[b, j * TOK:(j + 1) * TOK, :] = r[cid]["out"].T
    return out
```

```python
import math
from contextlib import ExitStack, contextmanager

import numpy as np
import ml_dtypes

import concourse.bass as bass
import concourse.mybir as mybir
from concourse.bass_utils import run_bass_kernel_spmd

F32 = mybir.dt.float32
BF16 = mybir.dt.bfloat16
AF = mybir.ActivationFunctionType
ALU = mybir.AluOpType

D = 2048
NKC = 16
DFF = 5632
NJ = 44
DEPTH = 4
NCORE = 8
EPS = 1e-6
POOL_WINDOWS = (2, 4, 8, 16)
NIN = 4616
C_Q, C_K, C_ZX, C_ZY, C_V, C_ZP, C_F = 0, 256, 512, 640, 768, 1024, 1152
NIN_G = 1154
MASKNEG = -30000.0


class Sched:
    EPOCH = 30000

    def __init__(self, nc, es):
        self.nc = nc
        self.es = es
        self.engs = {"pe": nc.tensor, "act": nc.scalar, "dve": nc.vector, "pool": nc.gpsimd, "sp": nc.sync}
        self.sems = {}
        self.count = {}
        self.step = {}
        self.waited = {e: {} for e in self.engs}
        self.lastw = {}
        self.readers = {}
        self.ninstr = 0
        self.outs = {}
        self.multiw = {}
        self.log = {e: [] for e in self.engs}

    def _sem(self, chan, ep):
        key = (chan, ep)
        if key not in self.sems:
            self.sems[key] = self.es.enter_context(self.nc.semaphore("s_%s_%d" % (chan, ep)))
        return self.sems[key]

    def _chan(self, chan, step):
        if chan not in self.count:
            self.count[chan] = 0
            self.step[chan] = step

    def _wait(self, eng, chan, total):
        if total <= self.waited[eng].get(chan, 0):
            return
        self.waited[eng][chan] = total
        esz = self.EPOCH * self.step[chan]
        ep = (total - 1) // esz
        self.engs[eng].wait_ge(self._sem(chan, ep), total - ep * esz)
        self.log[eng].append(("wait", (chan, ep), total - ep * esz))
        self.ninstr += 1

    def _deps(self, eng, me, reads, writes):
        for r in reads:
            lw = self.lastw.get(r)
            if lw is not None:
                self._wait(eng, lw[0], lw[1])
        for w in writes:
            lw = self.lastw.get(w)
            if lw is not None and lw[0] != me:
                self._wait(eng, lw[0], lw[1])
            for ch, tot in self.readers.get(w, {}).items():
                if ch != me or me not in ("pe",):
                    self._wait(eng, ch, tot)

    def _record(self, me, total, reads, writes):
        for w in writes:
            self.lastw[w] = (me, total)
            self.readers[w] = {}
        for r in reads:
            d = self.readers.setdefault(r, {})
            d[me] = max(d.get(me, 0), total)

    def op(self, eng, reads, writes, fn, sig=True):
        self._chan(eng, 1)
        for r in reads:
            lw = self.lastw.get(r)
            if lw is not None:
                self._wait(eng, lw[0], lw[1])
        for w in writes:
            lw = self.lastw.get(w)
            if lw is not None and not (eng == "pe" and lw[0] == "pe"):
                self._wait(eng, lw[0], lw[1])
            for ch, tot in self.readers.get(w, {}).items():
                if ch != eng:
                    self._wait(eng, ch, tot)
        ins = fn(self.engs[eng])
        self.ninstr += 1
        total = self.count[eng] + 1
        if sig:
            esz = self.EPOCH
            ep = (total - 1) // esz
            ins.then_inc(self._sem(eng, ep), 1)
            self.log[eng].append(("inc", (eng, ep), 1))
            self.count[eng] = total
        else:
            self.log[eng].append(("nop", None, 0))
        self._record(eng, total, reads, writes)
        return ins

    def dma(self, queue, chan, out, in_, reads, writes, is_out=False, multi_w=()):
        self._chan(chan, 16)
        for r in reads:
            lw = self.lastw.get(r)
            if lw is not None:
                self._wait(queue, lw[0], lw[1])
        for w in writes:
            lw = self.lastw.get(w)
            if lw is not None and lw[0] != chan:
                self._wait(queue, lw[0], lw[1])
            for ch, tot in self.readers.get(w, {}).items():
                self._wait(queue, ch, tot)
        ins = self.engs[queue].dma_start(out=out, in_=in_)
        self.ninstr += 1
        self.count[chan] += 16
        total = self.count[chan]
        assert total < self.EPOCH * 16
        ins.then_inc(self._sem(chan, 0), 16)
        self.log[queue].append(("inc", (chan, 0), 16))
        self._record(chan, total, reads, writes)
        if is_out:
            self.outs[chan] = total
        for r in multi_w:
            self.multiw.setdefault(r, {})[chan] = total
        return ins

    def simulate(self):
        sem = {}
        pos = {e: 0 for e in self.engs}
        progress = True
        while progress:
            progress = False
            for e, lg in self.log.items():
                while pos[e] < len(lg):
                    kind, key, val = lg[pos[e]]
                    if kind == "wait":
                        if sem.get(key, 0) < val:
                            break
                    elif kind == "inc":
                        sem[key] = sem.get(key, 0) + val
                    pos[e] += 1
                    progress = True
        stuck = {e: (pos[e], len(lg), lg[pos[e]] if pos[e] < len(lg) else None) for e, lg in self.log.items() if pos[e] < len(lg)}
        return stuck

    def barrier(self):
        for e in self.engs:
            for ch, tot in self.count.items():
                if ch != e and tot > 0:
                    self._wait(e, ch, tot)
        self.lastw.clear()
        self.readers.clear()
        self.multiw.clear()

    def collective(self, kind, in_t, out_t, reads, writes):
        self._chan("cc", 1)
        for r in reads:
            lw = self.lastw.get(r)
            if lw is not None:
                self._wait("pool", lw[0], lw[1])
            for ch, tot in self.multiw.get(r, {}).items():
                self._wait("pool", ch, tot)
        for w in writes:
            lw = self.lastw.get(w)
            if lw is not None:
                self._wait("pool", lw[0], lw[1])
            for ch, tot in self.readers.get(w, {}).items():
                self._wait("pool", ch, tot)
        ins = self.engs["pool"].collective_compute(kind, ALU.add, replica_groups=[[0, 1, 2, 3], [4, 5, 6, 7]],
                                                   ins=[in_t.ap().opt()], outs=[out_t.ap().opt()], dma_qos="P2")
        self.ninstr += 1
        self.count["cc"] += 1
        ins.then_inc(self._sem("cc", 0))
        self.log["pool"].append(("inc", ("cc", 0), 1))
        self._record("cc", self.count["cc"], reads, writes)
        return ins

    def finish(self, eng):
        for ch, tot in self.outs.items():
            self._wait(eng, ch, tot)


class KB:
    def __init__(self):
        self.nc = bass.Bass("TRN2", target_bir_lowering=False)
        self.es = ExitStack()
        self.s = Sched(self.nc, self.es)
        self._n = 0

    def din(self, name, shape, dt=F32):
        return self.nc.dram_tensor(name, list(shape), dt, kind="ExternalInput").ap()

    def dout(self, name, shape, dt=F32):
        return self.nc.dram_tensor(name, list(shape), dt, kind="ExternalOutput").ap()

    def sb(self, name, shape, dt=F32):
        self._n += 1
        return self.es.enter_context(self.nc.sbuf_tensor("%s_u%d" % (name, self._n), list(shape), dt))

    def ps(self, name, shape=(128, 512), dt=F32):
        self._n += 1
        return self.es.enter_context(self.nc.psum_tensor("%s_u%d" % (name, self._n), list(shape), dt))

    @contextmanager
    def phase(self):
        old = self.es
        self.es = ExitStack()
        try:
            yield
        finally:
            self.es.close()
            self.es = old

    def dscr(self, name, shape, dt=F32):
        return self.nc.dram_tensor(name, list(shape), dt)

    def close(self):
        self.es.close()
        return self.nc


class Rot:
    def __init__(self, items):
        self.items = items
        self.i = 0

    def next(self):
        it = self.items[self.i % len(self.items)]
        self.i += 1
        return it


class WStream:
    def __init__(self, k, name, nslot, slot_elems, prefetch):
        self.k = k
        self.slots = [k.sb("%s_slot%d" % (name, i), [128, slot_elems], BF16) for i in range(nslot)]
        self.name = name
        self.nslot = nslot
        self.pf = prefetch
        self.loads = []
        self.issued = 0

    def add(self, fn):
        self.loads.append(fn)
        return len(self.loads) - 1

    def res(self, i):
        return "%s_w%d" % (self.name, i % self.nslot)

    def _issue(self, i):
        slot = self.slots[i % self.nslot]
        view, pairs = self.loads[i](slot)
        for (o, a) in pairs:
            self.k.s.dma("pool", "%s_c%d" % (self.name, i % self.nslot), o, a, [], [self.res(i)])
        return view

    def get(self, i):
        while self.issued < min(len(self.loads), i + 1 + self.pf):
            self._issue(self.issued)
            self.issued += 1
        slot = self.slots[i % self.nslot]
        view, _ = self.loads[i](slot)
        return view, self.res(i)


def v3(t, a, b):
    return t[:, 0:a * b].rearrange("p (a b) -> p a b", b=b)


def emit_norm(k, tag, x3, xres, W, aeff, sq3, sqres, ones_bf, ss_ps, ss_res, rstd, rstd_res, tmps, sink):
    s = k.s
    s.op("act", [xres], [sqres], lambda e: e.activation(out=sq3[:, 0:NKC, 0:W], in_=x3[:, 0:NKC, 0:W], func=AF.Square))
    for kc in range(NKC):
        s.op("pe", [sqres], [ss_res],
             lambda e, kc=kc: e.matmul(ss_ps[:, 0:W], ones_bf[:, :], sq3[:, kc, 0:W], start=(kc == 0), stop=(kc == NKC - 1)),
             sig=(kc == NKC - 1))
    s.op("act", [ss_res], [rstd_res],
         lambda e: e.activation(out=rstd[:, 0:W], in_=ss_ps[:, 0:W], func=AF.Sqrt, bias=float(D * EPS)))
    s.op("dve", [rstd_res], [rstd_res], lambda e: e.reciprocal(out=rstd[:, 0:W], in_=rstd[:, 0:W]))
    for kc in range(NKC):
        tres, tmp = tmps.next()
        s.op("dve", [xres, rstd_res], [tres],
             lambda e, kc=kc, tmp=tmp: e.scalar_tensor_tensor(out=tmp[:, 0:W], in0=x3[:, kc, 0:W], scalar=aeff[:, kc:kc + 1],
                                                              in1=rstd[:, 0:W], op0=ALU.mult, op1=ALU.mult))
        sink(kc, tmp, tres)


def emit_aeff(k, prm, cg, csc, aeff, res_in, res_out):
    s = k.s
    s.op("dve", [res_in], [res_out],
         lambda e: e.tensor_scalar(out=aeff[:, :], in0=prm[:, csc:csc + 16], scalar1=1.0, scalar2=float(math.sqrt(D)),
                                   op0=ALU.add, op1=ALU.mult))
    s.op("dve", [res_in, res_out], [res_out],
         lambda e: e.tensor_tensor(out=aeff[:, :], in0=aeff[:, :], in1=prm[:, cg:cg + 16], op=ALU.mult))


def emit_aeff2(k, sc_ap, g_ap, aeff, res_out):
    s = k.s
    s.op("dve", ["mods"], [res_out],
         lambda e: e.tensor_scalar(out=aeff[:, :], in0=sc_ap, scalar1=1.0, scalar2=float(math.sqrt(D)), op0=ALU.add, op1=ALU.mult))
    s.op("dve", ["gn", res_out], [res_out], lambda e: e.tensor_tensor(out=aeff[:, :], in0=aeff[:, :], in1=g_ap, op=ALU.mult))


class HSlots:
    def __init__(self, k, G, sh_ap, tag):
        self.k = k
        self.G = G
        s = k.s
        self.shm = k.sb("shm_" + tag, [128, 4, 16])
        for sl in range(4):
            s.op("dve", ["mods", "msk"], ["shm"],
                 lambda e, sl=sl: e.tensor_scalar(out=self.shm[:, sl, :], in0=sh_ap, scalar1=G["msk"][:, sl:sl + 1], scalar2=None, op0=ALU.mult))
        self.stg = Rot([("stg%d" % i, k.sb("stg%d_%s" % (i, tag), [128, 4, 2, 512], BF16)) for i in range(2)])
        self.cur = None

    def sink(self, t, kc, tmp, tres):
        s = self.k.s
        G = self.G
        if kc % 2 == 0:
            self.cur = self.stg.next()
        sres, st = self.cur
        for sl in range(4):
            if sl < 2:
                s.op("act", [tres, "shm", "msk"], [sres],
                     lambda e, sl=sl, st=st: e.activation(out=st[:, sl, kc % 2, :], in_=tmp[:, :], func=AF.Identity,
                                                          bias=self.shm[:, sl, kc:kc + 1], scale=G["msk"][:, sl:sl + 1]))
            else:
                s.op("dve", [tres, "shm", "msk"], [sres],
                     lambda e, sl=sl, st=st: e.tensor_scalar(out=st[:, sl, kc % 2, :], in0=tmp[:, :], scalar1=G["msk"][:, sl:sl + 1],
                                                             scalar2=self.shm[:, sl, kc:kc + 1], op0=ALU.mult, op1=ALU.add))
        if kc % 2 == 1:
            hf, col = (t * 512) // G["PW"], (t * 512) % G["PW"]
            for sl in range(4):
                hi_ = sl * G["NH"] + hf
                hbv = G["HB"][hi_].ap().rearrange("(kc p) t -> p kc t", p=128)
                s.dma("sp", "sthb_" + sres, hbv[:, kc - 1:kc + 1, col:col + 512], st[:, sl, :, :], [sres], [], multi_w=["HB%d" % hi_])


def emit_e1(k, G, which="all"):
    for i in range(4 * G["NH"]):
        early = (G["NH"] == 2 and i % 2 == 0)
        if which == "all" or (which == "early") == early:
            k.s.collective("AllReduce", G["HB"][i], G["HG"][i], ["HB%d" % i], ["HG%d" % i])


def phase_m(k, G):
    s = k.s
    with k.phase():
        c_sb = k.sb("c_sb", [128, 16])
        sig = k.sb("sig", [128, 16])
        cact = k.sb("cact", [128, 16], BF16)
        mq = k.sb("mq", [128, 96])
        mq4 = k.sb("mq4", [128, 4, 96])
        bada = k.sb("bada_sb", [128, 384])
        mq_ps = k.ps("mq_ps")
        s.dma("sp", "ld", c_sb[:, :], G["cT"][:, :], [], ["c_sb"])
        s.dma("sp", "ld2", bada[:, :], G["bada"][:, :], [], ["bada"])
        s.dma("sp", "ld3", G["gn"][:, :], G["gains"][:, :], [], ["gn"])
        s.dma("sp", "ld4", G["msk"][:, :], G["msk_d"][:, :], [], ["msk"])
        s.op("dve", [], ["zero"], lambda e: e.memset(G["zero"][:, :], 0.0))
        s.op("act", ["c_sb"], ["sig"], lambda e: e.activation(out=sig[:, :], in_=c_sb[:, :], func=AF.Sigmoid))
        s.op("dve", ["c_sb", "sig"], ["cact"], lambda e: e.tensor_tensor(out=cact[:, :], in0=c_sb[:, :], in1=sig[:, :], op=ALU.mult))
        ws = WStream(k, "wm", 3, 8192, 2)
        for l in range(DEPTH):
            wv = G["wada"][l].rearrange("(kc p) n -> p kc n", p=128)
            for gq in range(6):
                ws.add(lambda slot, wv=wv, gq=gq: (v3(slot, 16, 512), [(v3(slot, 16, 512), wv[:, :, gq * 512:(gq + 1) * 512])]))
        i = 0
        for l in range(DEPTH):
            for gq in range(6):
                w3, wres = ws.get(i)
                i += 1
                for nn in range(4):
                    col = l * 24 + gq * 4 + nn
                    for kc in range(NKC):
                        s.op("pe", ["cact", wres], ["mq_ps"],
                             lambda e, kc=kc, w3=w3, nn=nn, col=col: e.matmul(mq_ps[:, col:col + 1], w3[:, kc, nn * 128:(nn + 1) * 128], cact[:, kc:kc + 1],
                                                                              start=(kc == 0), stop=(kc == NKC - 1)), sig=(kc == NKC - 1))
        s.op("act", ["mq_ps"], ["mq"], lambda e: e.mul(out=mq[:, :], in_=mq_ps[:, 0:96], mul=1.0))
        for sl in range(4):
            s.op("dve", ["mq", "msk"], ["mq4"],
                 lambda e, sl=sl: e.tensor_scalar(out=mq4[:, sl, :], in0=mq[:, :], scalar1=G["msk"][:, sl:sl + 1], scalar2=None, op0=ALU.mult))
        s.dma("sp", "stm", G["MB"].ap().rearrange("(s p) n -> p s n", p=128), mq4[:, :, :], ["mq4"], ["MB"])
        s.collective("AllReduce", G["MB"], G["MG"], ["MB"], ["MG"])
        mods_v = G["mods"][:, :].rearrange("p (l q i) -> p l q i", q=4, i=24)
        for q in range(4):
            s.dma("sp", "ldm%d" % q, mods_v[:, :, q, :], G["MG"].ap()[q * 128:(q + 1) * 128, :].rearrange("p (l i) -> p l i", i=24), ["MG"], ["mods"])
        s.op("dve", ["mods", "bada"], ["mods"], lambda e: e.tensor_tensor(out=G["mods"][:, :], in0=G["mods"][:, :], in1=bada[:, :], op=ALU.add))
        zerob = k.sb("zerob", [128, 16, 2], BF16)
        s.op("dve", [], ["zerob"], lambda e: e.memset(zerob[:, :, :], 0.0))
        s.dma("sp", "stz", G["YS"][0].ap()[0:D, 0:2].rearrange("(kc p) c -> p kc c", p=128), zerob[:, :, :], ["zerob"], ["YSz"])
        s.dma("sp", "stx0", G["XS"].ap()[:, :], G["xT"][:, :], [], ["XS"])
        s.barrier()


def phase_a(k, G, TOK):
    s = k.s
    NT = TOK // 512
    with k.phase():
        xv = G["xT"].rearrange("(kc p) t -> p kc t", p=128)
        aeff = k.sb("aeff", [128, 16])
        ones_bf = k.sb("ones_bf", [128, 128], BF16)
        xts = [("x%d" % i, k.sb("xa%d" % i, [128, 16, 512])) for i in range(2)]
        sq3 = k.sb("sqa", [128, 16, 512], BF16)
        rstd = k.sb("rstd", [128, 512])
        tmps = Rot([("tmp%d" % i, k.sb("tmpa%d" % i, [128, 512])) for i in range(3)])
        ss_ps = k.ps("ss_ps")
        s.op("dve", [], ["ones"], lambda e: e.memset(ones_bf[:, :], 1.0))
        M = G["mods"]
        emit_aeff2(k, M[:, 16:32], G["gn"][:, 0:16], aeff, "aeff")
        hs = HSlots(k, G, M[:, 0:16], "a")
        for t in range(NT):
            xres, x3 = xts[t % 2]
            s.dma("sp", "ldx%d" % (t % 2), x3[:, :, :], xv[:, :, 2 + t * 512:2 + (t + 1) * 512], [], [xres])
            emit_norm(k, "n", x3, xres, 512, aeff, sq3, "sq", ones_bf, ss_ps, "ss_ps", rstd, "rstd", tmps,
                      lambda kc, tmp, tres, t=t: hs.sink(t, kc, tmp, tres))
        s.barrier()
    return lambda: emit_e1(k, G)


def phase_b(k, G, l, S, after_setup=None):
    s = k.s
    NT = S // 512
    NB = S // 128
    TOK = S // 4
    with k.phase():
        wv = G["win"][l].rearrange("(kc p) n -> p kc n", p=128)
        ysvs = [ys.ap().rearrange("(j g r p) t -> j g p r t", j=4, g=4, p=128) for ys in G["YS"]]
        nh = len(G["YS"])
        win = k.sb("win_sb", [128, 16, 770], BF16)
        wvz = k.sb("wvz", [128, 16, 384], BF16)
        pwm = k.sb("pwm_sb", [128, 384], BF16)
        pmat = k.sb("pmat_sb", [128, 384], BF16)
        cst = k.sb("cst_sb", [128, 640])
        pb = k.sb("pb_sb", [128, 16])
        bfs = k.sb("bf_sb", [1, 2])
        nbf = k.sb("nbf", [1, 2])
        cA = k.sb("cA", [128, 2])
        lt = k.sb("lt", [128, 2])
        ones_bf = k.sb("ones_bf", [128, 128], BF16)
        ones2 = k.sb("ones2", [128, 128], BF16)
        ones_row = k.sb("ones_row", [1, 512])
        hts = [("ht%d" % i, k.sb("ht%d" % i, [128, 16, 512], BF16)) for i in range(2)]
        kT = k.sb("kT", [128, 2, S], BF16)
        vtok = k.sb("vtok", [128, NB, 256], BF16)
        zptok = k.sb("zptok", [128, 8, 128], BF16)
        qT = k.sb("qT", [128, 2, 512], BF16)
        FQ = k.sb("FQ", [128, 2, 512], BF16)
        Gk = k.sb("Gk", [128, 2, NB])
        Gq = [k.sb("Gq%d" % h, [1, 512]) for h in range(2)]
        Gc = k.sb("Gc", [1, 2])
        sp1 = k.sb("sp1", [1, 512])
        lsp = k.sb("lsp", [1, 512])
        hib = k.sb("hib", [1, 512], BF16)
        hif = lsp
        lo = sp1
        gnb = [k.sb("gnb%d" % h, [128, 512]) for h in range(2)]
        nlo = k.sb("nlo", [1, 512], BF16)
        pts = Rot([("pt%d" % i, k.sb("pt%d" % i, [128, 512], BF16)) for i in range(4)])
        dgs = Rot([("dg%d" % i, k.sb("dg%d" % i, [128, 512])) for i in range(1)])
        rden = k.sb("rden", [128, 512])
        yb = k.sb("yb", [128, 4, 512], BF16)
        yres = "yb"
        yms = Rot([("ym%d" % i, k.sb("ym%d" % i, [128, 4, 512], BF16)) for i in range(2)])
        zxb = k.sb("zxb", [128, 516])
        zy = k.sb("zy", [128, 512])
        dT = k.sb("dT", [128, 512], BF16)
        xc = k.sb("xc", [128, 512])
        xcb = k.sb("xcb", [128, 512], BF16)
        gr = k.sb("gr", [128, 512])
        gi = k.sb("gi", [128, 512])
        av = k.sb("av", [128, 512])
        a2 = k.sb("a2", [128, 512])
        mm_ = k.sb("mm", [128, 512])
        inp = k.sb("inp", [128, 512])
        hh = k.sb("hh", [128, 512])
        hc = k.sb("hc", [128, 1])
        uu = k.sb("uu", [128, 512])
        sg = k.sb("sg", [128, 512])
        pj = Rot([("pj%d" % i, k.ps("pj%d" % i)) for i in range(2)])
        sts = Rot([("st%d" % i, k.ps("st%d" % i)) for i in range(4)])
        o_one = k.ps("o_ps")
        d_one = k.ps("d_ps")
        o_ps = [o_one, o_one]
        d_ps = [d_one, d_one]
        ident = cst[:, 0:128]
        trif = cst[:, 128:640]
        msk = G["msk"]

        for q4 in range(4):
            s.dma("pool", "ldw", win[:, 4 * q4:4 * q4 + 4, 0:768], wv[:, 4 * q4:4 * q4 + 4, 0:768], [], ["win"])
        s.dma("pool", "ldw", win[:, :, 768:770], wv[:, :, C_F:C_F + 2], [], ["win"])
        s.dma("pool", "ldw4", wvz[:, :, :], wv[:, :, C_V:C_V + 384], [], ["wvz"])
        s.dma("pool", "ldw2", pwm[:, :], G["pwm"][l], [], ["pwm"])
        s.dma("pool", "ldw3", pmat[:, :], G["pmat"][:, :], [], ["pmat"])
        s.dma("sp", "ldc", cst[:, :], G["cst"][:, :], [], ["cst"])
        s.dma("sp", "ldc2", pb[:, :], G["pbp"][l], [], ["pb"])
        s.dma("sp", "ldc3", bfs[:, :], G["bfp"][l], [], ["bf"])
        s.op("dve", [], ["ones"], lambda e: e.memset(ones_bf[:, :], 1.0))
        s.op("dve", [], ["ones2"], lambda e: e.memset(ones2[:, :], 0.0))
        s.op("dve", ["ones2"], ["ones2"], lambda e: e.memset(ones2[0:2, :], 1.0))
        s.op("dve", [], ["ones_row"], lambda e: e.memset(ones_row[:, :], 1.0))
        s.op("dve", [], ["FQ0", "FQ1"], lambda e: e.memset(FQ[:, :, :], 0.0))
        s.op("dve", [], ["zxb"], lambda e: e.memset(zxb[:, 0:4], 0.0))
        s.op("dve", [], ["hc"], lambda e: e.memset(hc[:, :], 0.0))
        s.op("dve", [], ["Gc0", "Gc1"], lambda e: e.memset(Gc[:, :], 0.0))
        s.op("dve", ["bf"], ["nbf"], lambda e: e.tensor_scalar(out=nbf[:, :], in0=bfs[:, :], scalar1=-1.0, scalar2=None, op0=ALU.mult))
        s.op("act", ["pb"], ["lt"], lambda e: e.activation(out=lt[:, 0:1], in_=pb[:, 8:9], func=AF.Exp, scale=-1.0))
        s.op("act", ["lt"], ["lt"], lambda e: e.activation(out=lt[:, 1:2], in_=lt[:, 0:1], func=AF.Ln, bias=1.0))
        s.op("dve", ["lt"], ["cA"], lambda e: e.tensor_scalar(out=cA[:, 0:1], in0=lt[:, 1:2], scalar1=-8.0, scalar2=None, op0=ALU.mult))
        s.op("dve", ["lt", "cA"], ["cA"], lambda e: e.tensor_scalar(out=cA[:, 1:2], in0=lt[:, 1:2], scalar1=-16.0, scalar2=None, op0=ALU.mult))

        if after_setup is not None:
            after_setup()

        QSCALE = float(128 ** -0.5)

        def fm_proj(col, ht3, hres):
            pres, ps = pj.next()
            for kc in range(NKC):
                s.op("pe", [hres, "win"], [pres],
                     lambda e, kc=kc, ps=ps: e.matmul(ps[:, :], win[:, kc, col:col + 128], ht3[:, kc, :], start=(kc == 0), stop=(kc == NKC - 1)),
                     sig=(kc == NKC - 1))
            return pres, ps

        tps = TOK // 512

        def load_h(T):
            hres, ht3 = hts[T % 2]
            sl_, tt = T // tps, T % tps
            hgi = sl_ * G["NH"] + (tt * 512) // G["PW"]
            hcol = (tt * 512) % G["PW"]
            hgv = G["HG"][hgi].ap().rearrange("(kc p) t -> p kc t", p=128)
            s.dma("sp", "ldh%d" % (T % 2), ht3[:, :, :], hgv[:, :, hcol:hcol + 512], ["HG%d" % hgi], [hres])

        load_h(0)
        for T in range(NT):
            t0 = T * 512
            hres, ht3 = hts[T % 2]
            if T + 1 < NT:
                load_h(T + 1)

            f_pss = []
            for h in range(2):
                pres, ps = sts.next()
                for kc in range(NKC):
                    s.op("pe", [hres, "win"], [pres],
                         lambda e, kc=kc, ps=ps, h=h: e.matmul(ps[0:1, :], win[:, kc, 768 + h:769 + h], ht3[:, kc, :],
                                                               start=(kc == 0), stop=(kc == NKC - 1)), sig=(kc == NKC - 1))
                f_pss.append((pres, ps))
            for h in range(2):
                pres, ps = f_pss[h]
                s.op("act", [pres, "nbf"], ["sp1"],
                     lambda e, ps=ps, h=h: e.activation(out=sp1[:, :], in_=ps[0:1, :], func=AF.Exp, bias=nbf[0:1, h:h + 1], scale=-1.0))
                s.op("act", ["sp1"], ["lsp"], lambda e: e.activation(out=lsp[:, :], in_=sp1[:, :], func=AF.Ln, bias=1.0))
                s.op("dve", ["lsp", "ones_row", "Gc%d" % h], ["Gq%d" % h],
                     lambda e, h=h: e.tensor_tensor_scan(out=Gq[h][:, :], data0=ones_row[:, :], data1=lsp[:, :], initial=Gc[0:1, h:h + 1],
                                                         op0=ALU.mult, op1=ALU.add))
                s.op("dve", ["Gq%d" % h], ["Gc%d" % h], lambda e, h=h: e.tensor_copy(out=Gc[0:1, h:h + 1], in_=Gq[h][:, 511:512]))
                s.op("dve", ["Gq%d" % h], ["hib"], lambda e, h=h: e.tensor_copy(out=hib[:, :], in_=Gq[h][:, :]))
                s.op("dve", ["hib", "lsp"], ["lsp"], lambda e: e.tensor_copy(out=hif[:, :], in_=hib[:, :]))
                s.op("dve", ["Gq%d" % h, "lsp", "sp1"], ["sp1"], lambda e, h=h: e.tensor_tensor(out=lo[:, :], in0=Gq[h][:, :], in1=hif[:, :], op=ALU.subtract))
                s.op("dve", ["lsp"], ["FQ%d" % h],
                     lambda e, h=h: e.tensor_scalar(out=FQ[0:1, h, :], in0=hif[:, :], scalar1=-1.0, scalar2=None, op0=ALU.mult))
                s.op("dve", ["sp1"], ["nlo"], lambda e: e.tensor_scalar(out=nlo[:, :], in0=lo[:, :], scalar1=-1.0, scalar2=None, op0=ALU.mult))
                s.dma("sp", "fq%d" % h, FQ[1:2, h, :], nlo[0:1, :], ["nlo"], ["FQ%d" % h])

            for h in range(2):
                pres, ps = fm_proj(C_Q + h * 128, ht3, hres)
                s.op("act", [pres], ["qT%d" % h], lambda e, ps=ps, h=h: e.mul(out=qT[:, h, :], in_=ps[:, :], mul=QSCALE))
            for h in range(2):
                pres, ps = fm_proj(C_K + h * 128, ht3, hres)
                s.op("dve", [pres], ["kT"], lambda e, ps=ps, h=h: e.tensor_copy(out=kT[:, h, t0:t0 + 512], in_=ps[:, :]))
            pres, ps = fm_proj(C_ZX, ht3, hres)
            s.op("act", [pres], ["zxb"], lambda e, ps=ps: e.mul(out=zxb[:, 4:516], in_=ps[:, :], mul=1.0))
            pres, ps = fm_proj(C_ZY, ht3, hres)
            s.op("act", [pres], ["zy"], lambda e, ps=ps: e.mul(out=zy[:, :], in_=ps[:, :], mul=1.0))
            for jb in range(4):
                blk = 4 * T + jb
                pres, ps = pj.next()
                for kc in range(NKC):
                    s.op("pe", [hres, "wvz"], [pres],
                         lambda e, kc=kc, ps=ps, jb=jb: e.matmul(ps[:, 0:384], ht3[:, kc, jb * 128:(jb + 1) * 128], wvz[:, kc, :],
                                                                 start=(kc == 0), stop=(kc == NKC - 1)), sig=(kc == NKC - 1))
                s.op("act", [pres], ["vtok"], lambda e, ps=ps, blk=blk: e.mul(out=vtok[:, blk, :], in_=ps[:, 0:256], mul=1.0))
                s.op("act", [pres], ["zptok"], lambda e, ps=ps, blk=blk: e.mul(out=zptok[:, blk % 8, :], in_=ps[:, 256:384], mul=1.0))

            tp_res, tp_ps = pj.next()
            for h in range(2):
                gres_, gps_ = sts.next()
                s.op("pe", ["ones2", "FQ%d" % h], [gres_], lambda e, h=h, gps_=gps_: e.matmul(gps_[:, :], ones2[:, :], FQ[:, h, :], start=True, stop=True))
                s.op("act", [gres_], ["gnb%d" % h], lambda e, h=h, gps_=gps_: e.mul(out=gnb[h][:, :], in_=gps_[:, :], mul=1.0))
                for jb in range(4):
                    s.op("pe", ["Gq%d" % h, "cst"], [tp_res],
                         lambda e, h=h, jb=jb: e.transpose(tp_ps[:, h * 4 + jb:h * 4 + jb + 1], Gq[h][0:1, jb * 128:(jb + 1) * 128], ident[0:1, 0:1]),
                         sig=(jb == 3))
            s.op("dve", [tp_res], ["Gk"], lambda e, T=T: e.tensor_copy(out=Gk[:, 0, 4 * T:4 * T + 4], in_=tp_ps[:, 0:4]))
            s.op("dve", [tp_res], ["Gk"], lambda e, T=T: e.tensor_copy(out=Gk[:, 1, 4 * T:4 * T + 4], in_=tp_ps[:, 4:8]))

            s.op("dve", ["zxb", "pb"], ["xc"],
                 lambda e: e.tensor_scalar(out=xc[:, :], in0=zxb[:, 1:513], scalar1=pb[:, 1:2], scalar2=pb[:, 5:6], op0=ALU.mult, op1=ALU.add))
            for kk in range(1, 4):
                s.op("dve", ["zxb", "pb", "xc"], ["xc"],
                     lambda e, kk=kk: e.scalar_tensor_tensor(out=xc[:, :], in0=zxb[:, 1 + kk:513 + kk], scalar=pb[:, 1 + kk:2 + kk], in1=xc[:, :],
                                                             op0=ALU.mult, op1=ALU.add))
            s.op("dve", ["zxb"], ["zxb"], lambda e: e.tensor_copy(out=zxb[:, 1:4], in_=zxb[:, 513:516]))
            s.op("act", ["xc"], ["xcb"], lambda e: e.mul(out=xcb[:, :], in_=xc[:, :], mul=1.0))

            nkb = 4 * T + 4
            blocks = [(h, kb) for h in range(2) for kb in range(nkb)]
            info = {}

            def emit_qk(i):
                h, kb = blocks[i]
                jj = kb - 4 * T
                c0 = jj * 128 if jj >= 0 else 0
                stres, st = sts.next()
                ptres, pt = pts.next()
                info[i] = (ptres, pt, c0)
                s.op("pe", ["kT", "qT%d" % h], [stres],
                     lambda e: e.matmul(st[:, c0:512], kT[:, h, kb * 128:(kb + 1) * 128], qT[:, h, c0:512], start=True, stop=True))
                s.op("dve", [stres, "gnb%d" % h], [stres],
                     lambda e: e.tensor_tensor(out=st[:, c0:512], in0=st[:, c0:512], in1=gnb[h][:, c0:512], op=ALU.add))
                if jj >= 0:
                    dgres, dg = dgs.next()
                    n = 512 - c0
                    s.op("dve", [stres, "cst"], [dgres],
                         lambda e: e.tensor_tensor(out=dg[:, 0:n], in0=st[:, c0:512], in1=trif[:, 0:n], op=ALU.add))
                    s.op("act", [dgres, "Gk"], [ptres],
                         lambda e: e.activation(out=pt[:, c0:512], in_=dg[:, 0:n], func=AF.Exp, bias=Gk[:, h, kb:kb + 1]))
                else:
                    s.op("act", [stres, "Gk"], [ptres],
                         lambda e: e.activation(out=pt[:, :], in_=st[:, :], func=AF.Exp, bias=Gk[:, h, kb:kb + 1]))

            def emit_pv(i):
                h, kb = blocks[i]
                ptres, pt, c0 = info.pop(i)
                last = (kb == nkb - 1)
                s.op("pe", ["vtok", ptres], ["o_ps"],
                     lambda e: e.matmul(o_ps[h][:, c0:512], vtok[:, kb, h * 128:(h + 1) * 128], pt[:, c0:512], start=(kb == 0), stop=last), sig=False)
                s.op("pe", ["ones", ptres], ["d_ps"],
                     lambda e: e.matmul(d_ps[h][:, c0:512], ones_bf[:, :], pt[:, c0:512], start=(kb == 0), stop=last))
                if last:
                    s.op("dve", ["d_ps"], ["rden"], lambda e: e.reciprocal(out=rden[:, :], in_=d_ps[h][:, :]))
                    s.op("dve", ["o_ps", "d_ps", "rden"], [yres],
                         lambda e: e.tensor_tensor(out=yb[:, 1 + h, :], in0=o_ps[h][:, :], in1=rden[:, :], op=ALU.mult))

            LA = 3
            for i in range(min(LA, len(blocks))):
                emit_qk(i)
            for i in range(len(blocks)):
                if i + LA < len(blocks):
                    emit_qk(i + LA)
                emit_pv(i)

            pres, ps = pj.next()
            for jb in range(4):
                blk = 4 * T + jb
                pc0 = 0 if blk == 0 else 128
                s.op("pe", ["zptok", "pmat"], [pres],
                     lambda e, ps=ps, jb=jb, blk=blk, pc0=pc0: e.matmul(ps[:, jb * 128:(jb + 1) * 128], zptok[:, blk % 8, :], pmat[:, pc0:pc0 + 128],
                                                                        start=True, stop=(blk == 0)), sig=(blk == 0))
                if blk > 0:
                    s.op("pe", ["zptok", "pmat"], [pres],
                         lambda e, ps=ps, jb=jb, blk=blk: e.matmul(ps[:, jb * 128:(jb + 1) * 128], zptok[:, (blk - 1) % 8, :], pmat[:, 256:384],
                                                                   start=False, stop=True), sig=True)
            s.op("act", [pres], ["dT"], lambda e, ps=ps: e.mul(out=dT[:, :], in_=ps[:, :], mul=1.0))
            pres2, ps2 = pj.next()
            s.op("pe", ["dT", "pwm"], [pres2], lambda e, ps2=ps2: e.matmul(ps2[:, :], pwm[:, 0:128], dT[:, :], start=True, stop=True))
            s.op("dve", [pres2, "pb"], [yres],
                 lambda e, ps2=ps2: e.tensor_scalar(out=yb[:, 0, :], in0=ps2[:, :], scalar1=pb[:, 0:1], scalar2=None, op0=ALU.mult))

            rres, rps = pj.next()
            s.op("pe", ["xcb", "pwm"], [rres], lambda e, rps=rps: e.matmul(rps[:, :], pwm[:, 128:256], xcb[:, :], start=True, stop=True))
            ires, ips = pj.next()
            s.op("pe", ["xcb", "pwm"], [ires], lambda e, ips=ips: e.matmul(ips[:, :], pwm[:, 256:384], xcb[:, :], start=True, stop=True))
            s.op("act", [rres, "pb"], ["gr"], lambda e, rps=rps: e.activation(out=gr[:, :], in_=rps[:, :], func=AF.Sigmoid, bias=pb[:, 6:7]))
            s.op("act", [ires, "pb"], ["gi"], lambda e, ips=ips: e.activation(out=gi[:, :], in_=ips[:, :], func=AF.Sigmoid, bias=pb[:, 7:8]))
            s.op("act", ["gr", "cA"], ["av"], lambda e: e.activation(out=av[:, :], in_=gr[:, :], func=AF.Exp, scale=cA[:, 0:1]))
            s.op("act", ["gr", "cA"], ["a2"], lambda e: e.activation(out=a2[:, :], in_=gr[:, :], func=AF.Exp, scale=cA[:, 1:2]))
            s.op("dve", ["a2"], ["mm"], lambda e: e.tensor_scalar(out=mm_[:, :], in0=a2[:, :], scalar1=-1.0, scalar2=1.0, op0=ALU.mult, op1=ALU.add))
            s.op("dve", ["mm"], ["mm"], lambda e: e.tensor_scalar(out=mm_[:, :], in0=mm_[:, :], scalar1=1e-30, scalar2=None, op0=ALU.max))
            s.op("act", ["mm"], ["mm"], lambda e: e.activation(out=mm_[:, :], in_=mm_[:, :], func=AF.Sqrt))
            s.op("dve", ["gi", "xc"], ["inp"], lambda e: e.tensor_tensor(out=inp[:, :], in0=gi[:, :], in1=xc[:, :], op=ALU.mult))
            s.op("dve", ["inp", "mm"], ["inp"], lambda e: e.tensor_tensor(out=inp[:, :], in0=inp[:, :], in1=mm_[:, :], op=ALU.mult))
            s.op("dve", ["av", "inp", "hc"], ["hh"],
                 lambda e: e.tensor_tensor_scan(out=hh[:, :], data0=av[:, :], data1=inp[:, :], initial=hc[:, 0:1], op0=ALU.mult, op1=ALU.add))
            s.op("dve", ["hh"], ["hc"], lambda e: e.tensor_copy(out=hc[:, :], in_=hh[:, 511:512]))
            s.op("dve", ["zy"], ["uu"], lambda e: e.tensor_tensor(out=uu[:, :], in0=zy[:, :], in1=zy[:, :], op=ALU.mult))
            s.op("dve", ["uu"], ["uu"], lambda e: e.tensor_scalar(out=uu[:, :], in0=uu[:, :], scalar1=0.044715, scalar2=1.0, op0=ALU.mult, op1=ALU.add))
            s.op("dve", ["uu", "zy"], ["uu"], lambda e: e.tensor_tensor(out=uu[:, :], in0=uu[:, :], in1=zy[:, :], op=ALU.mult))
            s.op("act", ["uu"], ["sg"], lambda e: e.activation(out=sg[:, :], in_=uu[:, :], func=AF.Sigmoid, scale=1.5957691216057308))
            s.op("dve", ["sg", "zy"], ["sg"], lambda e: e.tensor_tensor(out=sg[:, :], in0=sg[:, :], in1=zy[:, :], op=ALU.mult))
            s.op("dve", ["sg", "hh"], [yres], lambda e: e.tensor_tensor(out=yb[:, 3, :], in0=sg[:, :], in1=hh[:, :], op=ALU.mult))

            j = T // tps
            tt = T % tps
            if nh == 2 and tt >= tps // 2:
                half, cbase = 1, (tt - tps // 2) * 512
            else:
                half, cbase = 0, 2 + tt * 512
            for g2 in range(4):
                ymres, ym = yms.next()
                s.op("dve", [yres, "msk"], [ymres],
                     lambda e, ym=ym, g2=g2: e.tensor_scalar(out=ym[:, :, :], in0=yb[:, :, :], scalar1=msk[:, g2:g2 + 1], scalar2=None, op0=ALU.mult))
                s.dma("sp", "sty_" + ymres, ysvs[half][j, g2, :, :, cbase:cbase + 512], ym[:, :, :], [ymres], [], multi_w=["YS%d" % half])
                if tt == tps - 1 and j < 3:
                    s.dma("sp", "sty_" + ymres, ysvs[0][j + 1, g2, :, :, 0:2], ym[:, :, 510:512], [ymres], [], multi_w=["YS0"])
            if nh == 2 and T == 3 * tps + tps // 2 - 1:
                s.collective("ReduceScatter", G["YS"][0], G["YR"][0], ["YS0"], ["YR0"])
        if nh == 1:
            s.collective("ReduceScatter", G["YS"][0], G["YR"][0], ["YS0"], ["YR0"])
            s.barrier()
        else:
            s.barrier()
            G["pending_rs"] = lambda: s.collective("ReduceScatter", G["YS"][1], G["YR"][1], ["YS1"], ["YR1"])


def phase_c(k, G, l, TOK):
    s = k.s
    NT = TOK // 512
    final = (l == DEPTH - 1)
    with k.phase():
        xv = G["XS"].ap().rearrange("(kc p) t -> p kc t", p=128)
        yvs = [yr.ap().rearrange("(kc p) t -> p kc t", p=128) for yr in G["YR"]]
        nh = len(yvs)
        ov = G["out"].rearrange("(kc p) t -> p kc t", p=128)
        wov = G["w_out"][l].rearrange("(kc p) n -> p kc n", p=128)
        wgv = G["w_gate"][l].rearrange("(kc p) n -> p kc n", p=128)
        wuv = G["w_up"][l].rearrange("(kc p) n -> p kc n", p=128)
        w_down = G["w_down"][l]
        M = G["mods"]
        GN = G["gn"]
        msk = G["msk"]
        mo = l * 96
        gt1 = M[:, mo + 32:mo + 48]
        sh2 = M[:, mo + 48:mo + 64]
        sc2 = M[:, mo + 64:mo + 80]
        gt2 = M[:, mo + 80:mo + 96]
        if not final:
            g_n = GN[:, (l + 1) * 16:(l + 2) * 16]
            sh_n = M[:, mo + 96:mo + 112]
            sc_n = M[:, mo + 112:mo + 128]
        else:
            g_n = GN[:, 128:144]
            sh_n = G["zero"][:, 0:16]
            sc_n = G["zero"][:, 0:16]

        cv = k.sb("cv_sb", [128, NJ * 4])
        aeff2 = k.sb("aeff2", [128, 16])
        aeffn = k.sb("aeffn", [128, 16])
        ones_bf = k.sb("ones_bf", [128, 128], BF16)
        x3 = k.sb("x3", [128, 16, 512])
        y3 = k.sb("y3", [128, 16, 512], BF16)
        h23 = y3
        act3 = k.sb("act3", [128, NJ, 512], BF16)
        sq3 = act3
        xh = k.sb("xh", [128, 16, 2])
        yh = k.sb("yh", [128, 16, 2], BF16)
        h2h = k.sb("h2h", [128, 16, 2], BF16)
        gprev = k.sb("gprev", [128, NJ, 2])
        xl4 = k.sb("xl4", [128, 4, 16, 2])
        xg4 = k.sb("xg4", [128, 4, 16, 2])
        gbuf = [("gbuf%d" % i, k.sb("gbuf%d" % i, [128, 516])) for i in range(2)]
        acc = [("acc%d" % i, k.sb("acc%d" % i, [128, 512])) for i in range(2)]
        sil = [("sil%d" % i, k.sb("sil%d" % i, [128, 512])) for i in range(2)]
        rstd = k.sb("rstd", [128, 512])
        tmps = Rot([("tmp%d" % i, k.sb("tmp%d" % i, [128, 512])) for i in range(3)])
        g_ps = [("g_ps%d" % i, k.ps("g_ps%d" % i)) for i in range(2)]
        u_ps = [("u_ps%d" % i, k.ps("u_ps%d" % i)) for i in range(2)]
        m_ps = Rot([("m_ps%d" % i, k.ps("m_ps%d" % i)) for i in range(3)])
        ss_ps = k.ps("ss_ps")
        if final:
            hst = Rot([("hst%d" % i, k.sb("hst%d" % i, [128, 512])) for i in range(2)])
            hs = None
        else:
            hs = HSlots(k, G, sh_n, "c")

        split_e1 = (not final) and G["NH"] == 2 and NT == 4
        s.dma("sp", "ldp2", cv[:, :], G["cv"][l], [], ["cv"])
        s.op("dve", [], ["ones"], lambda e: e.memset(ones_bf[:, :], 1.0))
        emit_aeff2(k, sc2, GN[:, 64 + l * 16:80 + l * 16], aeff2, "aeff2")
        emit_aeff2(k, sc_n, g_n, aeffn, "aeffn")

        ws = WStream(k, "w", 4, 8192, 2)
        plan = {}

        def add_out(n):
            return ws.add(lambda slot: (v3(slot, 16, 512), [(v3(slot, 16, 512), wov[:, :, n * 512:(n + 1) * 512])]))

        def add_gu(wview, jg):
            return ws.add(lambda slot: (v3(slot, 16, 512), [(v3(slot, 16, 512), wview[:, :, jg * 512:(jg + 1) * 512])]))

        def add_down(m):
            return ws.add(lambda slot: (v3(slot, NJ, 128), [(slot[:, 0:NJ * 128], w_down[m, :, :])]))

        for t in range(NT):
            for n in range(4):
                plan[(t, "out", n)] = add_out(n)
            for jg in range(NJ // 4):
                plan[(t, "g", jg)] = add_gu(wgv, jg)
                plan[(t, "u", jg)] = add_gu(wuv, jg)
            for m in range(16):
                plan[(t, "d", m)] = add_down(m)

        def outproj(keyfn, segs):
            for n in range(4):
                w3, wres = ws.get(plan[keyfn(n)])
                for mm in range(4):
                    m = n * 4 + mm
                    for (yy3, yres, xx3, xres, W) in segs:
                        pres, ps = m_ps.next()
                        for kc in range(NKC):
                            s.op("pe", [yres, wres], [pres],
                                 lambda e, kc=kc, w3=w3, ps=ps, mm=mm, yy3=yy3, W=W: e.matmul(ps[:, 0:W], w3[:, kc, mm * 128:(mm + 1) * 128], yy3[:, kc, 0:W],
                                                                                              start=(kc == 0), stop=(kc == NKC - 1)), sig=(kc == NKC - 1))
                        s.op("dve", [pres, xres, "mods"], [xres],
                             lambda e, m=m, ps=ps, xx3=xx3, W=W: e.scalar_tensor_tensor(out=xx3[:, m, 0:W], in0=ps[:, 0:W], scalar=gt1[:, m:m + 1],
                                                                                        in1=xx3[:, m, 0:W], op0=ALU.mult, op1=ALU.add))

        s.dma("sp", "ldh", xh[:, :, :], xv[:, :, 0:2], ["XS"], ["xh"])
        s.dma("sp", "ldh2", yh[:, :, :], yvs[0][:, :, 0:2], ["YR0"], ["yh"])

        def sink_h(kc, tmp, tres):
            s.op("act", [tres, "mods"], ["h2h"],
                 lambda e: e.activation(out=h2h[:, kc, :], in_=tmp[:, 0:2], func=AF.Identity, bias=sh2[:, kc:kc + 1]))

        for t in range(NT):
            c0 = 2 + t * 512
            if nh == 2 and t >= NT // 2:
                yh_i, yc0 = 1, (t - NT // 2) * 512
            else:
                yh_i, yc0 = 0, 2 + t * 512
            s.dma("sp", "ldx", x3[:, :, :], xv[:, :, c0:c0 + 512], ["XS"], ["x3"])
            if t == 0:
                s.dma("sp", "ldy", y3[:, :, :], yvs[yh_i][:, :, yc0:yc0 + 512], ["YR%d" % yh_i], ["y3"])
            segs = [(y3, "y3", x3, "x3", 512)]
            if t == 0:
                segs = [(yh, "yh", xh, "xh", 2)] + segs
            if t == 0 and G.get("pending_rs") is not None:
                ws.get(plan[(0, "out", 0)])
                G["pending_rs"]()
                G["pending_rs"] = None
            outproj(lambda n, t=t: (t, "out", n), segs)
            if split_e1 and t == NT // 2:
                emit_e1(k, G, "early")
            if t == 0:
                emit_norm(k, "nh", xh, "xh", 2, aeff2, sq3, "act3", ones_bf, ss_ps, "ss_ps", rstd, "rstd", tmps, sink_h)

            def sink2(kc, tmp, tres):
                s.op("act", [tres, "mods"], ["y3"],
                     lambda e: e.activation(out=h23[:, kc, :], in_=tmp[:, :], func=AF.Identity, bias=sh2[:, kc:kc + 1]))

            emit_norm(k, "n2", x3, "x3", 512, aeff2, sq3, "act3", ones_bf, ss_ps, "ss_ps", rstd, "rstd", tmps, sink2)

            for jg in range(NJ // 4):
                wg3, wgres = ws.get(plan[(t, "g", jg)])
                wu3, wures = ws.get(plan[(t, "u", jg)])
                for jj in range(4):
                    j = jg * 4 + jj
                    gres, gp = g_ps[j % 2]
                    ures, up = u_ps[j % 2]
                    bres, gb = gbuf[j % 2]
                    ares, ac = acc[j % 2]
                    sres, sl = sil[j % 2]
                    if t == 0:
                        pres, ps = m_ps.next()
                        for kc in range(NKC):
                            s.op("pe", ["h2h", wgres], [pres],
                                 lambda e, kc=kc, wg3=wg3, ps=ps, jj=jj: e.matmul(ps[:, 0:2], wg3[:, kc, jj * 128:(jj + 1) * 128], h2h[:, kc, :],
                                                                                  start=(kc == 0), stop=(kc == NKC - 1)), sig=(kc == NKC - 1))
                        s.op("dve", [pres, "msk"], ["gprev"],
                             lambda e, j=j, ps=ps: e.tensor_scalar(out=gprev[:, j, :], in0=ps[:, 0:2], scalar1=msk[:, 8:9], scalar2=None, op0=ALU.mult))
                    for kc in range(NKC):
                        s.op("pe", ["y3", wgres], [gres],
                             lambda e, kc=kc, gp=gp, jj=jj, wg3=wg3: e.matmul(gp[:, :], wg3[:, kc, jj * 128:(jj + 1) * 128], h23[:, kc, :],
                                                                              start=(kc == 0), stop=(kc == NKC - 1)), sig=(kc == NKC - 1))
                    for kc in range(NKC):
                        s.op("pe", ["y3", wures], [ures],
                             lambda e, kc=kc, up=up, jj=jj, wu3=wu3: e.matmul(up[:, :], wu3[:, kc, jj * 128:(jj + 1) * 128], h23[:, kc, :],
                                                                              start=(kc == 0), stop=(kc == NKC - 1)), sig=(kc == NKC - 1))
                    s.op("act", [gres], [bres], lambda e, gb=gb, gp=gp: e.mul(out=gb[:, 4:516], in_=gp[:, :], mul=1.0))
                    s.op("dve", ["gprev"], [bres], lambda e, gb=gb, j=j: e.tensor_copy(out=gb[:, 2:4], in_=gprev[:, j, :]))
                    s.op("dve", [bres], ["gprev"], lambda e, gb=gb, j=j: e.tensor_copy(out=gprev[:, j, :], in_=gb[:, 514:516]))
                    s.op("dve", [bres, "cv"], [ares],
                         lambda e, gb=gb, ac=ac, j=j: e.tensor_scalar(out=ac[:, :], in0=gb[:, 2:514], scalar1=cv[:, 4 * j:4 * j + 1],
                                                                      scalar2=cv[:, 4 * j + 3:4 * j + 4], op0=ALU.mult, op1=ALU.add))
                    s.op("dve", [bres, "cv", ares], [ares],
                         lambda e, gb=gb, ac=ac, j=j: e.scalar_tensor_tensor(out=ac[:, :], in0=gb[:, 3:515], scalar=cv[:, 4 * j + 1:4 * j + 2],
                                                                             in1=ac[:, :], op0=ALU.mult, op1=ALU.add))
                    s.op("dve", [bres, "cv", ares], [ares],
                         lambda e, gb=gb, ac=ac, j=j: e.scalar_tensor_tensor(out=ac[:, :], in0=gb[:, 4:516], scalar=cv[:, 4 * j + 2:4 * j + 3],
                                                                             in1=ac[:, :], op0=ALU.mult, op1=ALU.add))
                    s.op("act", [ares], [sres], lambda e, ac=ac, sl=sl: e.activation(out=sl[:, :], in_=ac[:, :], func=AF.Silu))
                    s.op("dve", [sres, ures], ["act3"],
                         lambda e, sl=sl, up=up, j=j: e.tensor_tensor(out=act3[:, j, :], in0=sl[:, :], in1=up[:, :], op=ALU.mult))

            if t + 1 < NT:
                if nh == 2 and t + 1 >= NT // 2:
                    nyh, nyc = 1, (t + 1 - NT // 2) * 512
                else:
                    nyh, nyc = 0, 2 + (t + 1) * 512
                s.dma("sp", "ldy", y3[:, :, :], yvs[nyh][:, :, nyc:nyc + 512], ["YR%d" % nyh], ["y3"])
            for m in range(16):
                wd3, wdres = ws.get(plan[(t, "d", m)])
                pres, ps = m_ps.next()
                for j in range(NJ):
                    s.op("pe", ["act3", wdres], [pres],
                         lambda e, j=j, wd3=wd3, ps=ps: e.matmul(ps[:, :], wd3[:, j, :], act3[:, j, :], start=(j == 0), stop=(j == NJ - 1)),
                         sig=(j == NJ - 1))
                s.op("dve", [pres, "x3", "mods"], ["x3"],
                     lambda e, m=m, ps=ps: e.scalar_tensor_tensor(out=x3[:, m, :], in0=ps[:, :], scalar=gt2[:, m:m + 1],
                                                                  in1=x3[:, m, :], op0=ALU.mult, op1=ALU.add))
            if not final:
                s.dma("sp", "stx", xv[:, :, c0:c0 + 512], x3[:, :, :], ["x3"], ["XS"])
                if t == NT - 1:
                    for sl_ in range(4):
                        s.op("dve", ["x3", "msk"], ["xl4"],
                             lambda e, sl_=sl_: e.tensor_scalar(out=xl4[:, sl_, :, :], in0=x3[:, :, 510:512], scalar1=msk[:, 4 + sl_:5 + sl_], scalar2=None,
                                                                op0=ALU.mult))
                    s.dma("sp", "stxl", G["XL"].ap().rearrange("(s kc p) c -> p s kc c", s=4, p=128), xl4[:, :, :, :], ["xl4"], ["XL"])
                    s.collective("AllReduce", G["XL"], G["XG"], ["XL"], ["XG"])

            if final:
                def sinkn(kc, tmp, tres, t=t):
                    hres, hsb_ = hst.next()
                    s.op("act", [tres, "zero"], [hres],
                         lambda e: e.activation(out=hsb_[:, :], in_=tmp[:, :], func=AF.Identity, bias=sh_n[:, kc:kc + 1]))
                    s.dma("sp", "sth_" + hres, ov[:, kc, t * 512:(t + 1) * 512], hsb_[:, :], [hres], [], is_out=True)
            else:
                def sinkn(kc, tmp, tres, t=t):
                    hs.sink(t, kc, tmp, tres)

            emit_norm(k, "nn", x3, "x3", 512, aeffn, sq3, "act3", ones_bf, ss_ps, "ss_ps", rstd, "rstd", tmps, sinkn)

        if not final:
            s.dma("sp", "ldxg", xg4[:, :, :, :], G["XG"].ap().rearrange("(s kc p) c -> p s kc c", s=4, p=128), ["XG"], ["xg4"])
            s.op("dve", ["xg4", "msk"], ["xh"],
                 lambda e: e.tensor_scalar(out=xh[:, :, :], in0=xg4[:, 0, :, :], scalar1=msk[:, 0:1], scalar2=None, op0=ALU.mult))
            for sl_ in range(1, 4):
                s.op("dve", ["xg4", "msk", "xh"], ["xh"],
                     lambda e, sl_=sl_: e.scalar_tensor_tensor(out=xh[:, :, :], in0=xg4[:, sl_, :, :], scalar=msk[:, sl_:sl_ + 1], in1=xh[:, :, :],
                                                               op0=ALU.mult, op1=ALU.add))
            s.dma("sp", "stxh", xv[:, :, 0:2], xh[:, :, :], ["xh"], ["XS"])
        s.barrier()
    if not final:
        return lambda: emit_e1(k, G, "late" if split_e1 else "all")
    return None


def build_fused(S):
    k = KB()
    TOK = S // 4
    G = {}
    G["xT"] = k.din("xT", [D, TOK + 2])
    G["cT"] = k.din("cT", [128, 16])
    G["wada"] = k.din("wada", [DEPTH, D, 3072])
    G["bada"] = k.din("bada", [128, 384])
    G["gains"] = k.din("gains", [128, 144])
    G["msk_d"] = k.din("msk", [128, 9])
    G["win"] = k.din("win", [DEPTH, D, NIN_G])
    G["pwm"] = k.din("pwm", [DEPTH, 128, 384])
    G["pmat"] = k.din("pmat", [128, 384])
    G["cst"] = k.din("cst", [128, 640])
    G["pbp"] = k.din("pbp", [DEPTH, 128, 16])
    G["bfp"] = k.din("bfp", [DEPTH, 1, 2])
    G["w_out"] = k.din("w_out", [DEPTH, D, D])
    G["w_gate"] = k.din("w_gate", [DEPTH, D, DFF])
    G["w_up"] = k.din("w_up", [DEPTH, D, DFF])
    G["w_down"] = k.din("w_down", [DEPTH, 16, 128, NJ * 128])
    G["cv"] = k.din("cv", [DEPTH, 128, NJ * 4])
    G["out"] = k.dout("out", [D, TOK])
    G["XS"] = k.dscr("XS", [D, TOK + 2], F32)
    G["NH"] = max(1, TOK // 1024)
    G["PW"] = TOK // G["NH"]
    G["HB"] = [k.dscr("HB%d" % i, [D, G["PW"]], BF16) for i in range(4 * G["NH"])]
    G["HG"] = [k.dscr("HG%d" % i, [D, G["PW"]], BF16) for i in range(4 * G["NH"])]
    if TOK // 512 >= 2:
        G["YS"] = [k.dscr("YSa", [4 * D, TOK // 2 + 2], BF16), k.dscr("YSb", [4 * D, TOK // 2], BF16)]
        G["YR"] = [k.dscr("YRa", [D, TOK // 2 + 2], BF16), k.dscr("YRb", [D, TOK // 2], BF16)]
    else:
        G["YS"] = [k.dscr("YS", [4 * D, TOK + 2], BF16)]
        G["YR"] = [k.dscr("YR", [D, TOK + 2], BF16)]
    G["MB"] = k.dscr("MB", [512, 96], F32)
    G["MG"] = k.dscr("MG", [512, 96], F32)
    G["XL"] = k.dscr("XL", [4 * D, 2], F32)
    G["XG"] = k.dscr("XG", [4 * D, 2], F32)
    G["mods"] = k.sb("mods", [128, 384])
    G["gn"] = k.sb("gn", [128, 144])
    G["msk"] = k.sb("msk_sb", [128, 9])
    G["zero"] = k.sb("zero", [128, 32])
    phase_m(k, G)
    pending = phase_a(k, G, TOK)
    for l in range(DEPTH):
        phase_b(k, G, l, S, after_setup=pending)
        pending = phase_c(k, G, l, TOK)
    k.s.finish("sp")
    global _LAST_SCHED
    _LAST_SCHED = k.s
    return k.close()


def _fm(v):
    v = np.asarray(v, np.float32)
    return np.ascontiguousarray(v.reshape(-1, 128).T)


def _pool_mats(win):
    s_ = np.arange(128)[:, None]
    t_ = np.arange(128)[None, :]
    band = ((t_ - s_) >= 0) & ((t_ - s_) < win)
    eye = np.eye(128, dtype=np.float32)
    pd = band.astype(np.float32) / win - eye
    pd0 = band.astype(np.float32) / np.minimum(t_ + 1, win).astype(np.float32) - eye
    pp = (((t_ + 128 - s_) < win)).astype(np.float32) / win
    return np.ascontiguousarray(np.concatenate([pd0, pd, pp], axis=1).astype(np.float32))


_PROGS = {}
_LAST_SCHED = None


def kernel(x, c, w_ada, b_ada, g_mix, w_in, b_f, pool_w, pool_scale, lru_conv_w, lru_conv_b,
           lru_wa, lru_ba, lru_wi, lru_bi, lru_lambda, w_out, g_ffn, w_ffn_gate, w_ffn_up,
           ffn_conv_w, ffn_conv_b, w_ffn_down, final_g):
    f32 = np.float32
    A = lambda v: np.asarray(v, f32)
    x = A(x)
    B, S, _ = x.shape
    TOK = S // 4
    c, w_ada, b_ada, g_mix, w_in, b_f = A(c), A(w_ada), A(b_ada), A(g_mix), A(w_in), A(b_f)
    pool_w, pool_scale, lru_conv_w, lru_conv_b = A(pool_w), A(pool_scale), A(lru_conv_w), A(lru_conv_b)
    lru_wa, lru_ba, lru_wi, lru_bi, lru_lambda = A(lru_wa), A(lru_ba), A(lru_wi), A(lru_bi), A(lru_lambda)
    w_out, g_ffn, w_ffn_gate, w_ffn_up = A(w_out), A(g_ffn), A(w_ffn_gate), A(w_ffn_up)
    ffn_conv_w, ffn_conv_b, w_ffn_down, final_g = A(ffn_conv_w), A(ffn_conv_b), A(w_ffn_down), A(final_g)

    if S not in _PROGS:
        _PROGS[S] = build_fused(S)
    nc = _PROGS[S]

    bada = np.ascontiguousarray(np.concatenate([_fm(b_ada[l]) for l in range(DEPTH)], axis=1))
    gains = np.ascontiguousarray(np.concatenate([_fm(g_mix[l]) for l in range(DEPTH)] + [_fm(g_ffn[l]) for l in range(DEPTH)]
                                                + [_fm(final_g)], axis=1))
    cst = np.ascontiguousarray(np.concatenate([np.eye(128, dtype=f32),
                                               np.where(np.arange(128)[None, :] >= np.arange(128)[:, None], 0.0, MASKNEG).astype(f32),
                                               np.zeros((128, 384), f32)], axis=1))
    perm = []
    for g in range(4):
        perm += list(range(g * 128, (g + 1) * 128))
        perm += list(range(512 + 2 * g * 128, 512 + (2 * g + 2) * 128))
        perm += list(range(1536 + g * 128, 1536 + (g + 1) * 128))
    w_out_p = np.ascontiguousarray(w_out[:, perm, :])
    wd_l = np.ascontiguousarray(w_ffn_down.reshape(DEPTH, NJ, 128, 16, 128).transpose(0, 3, 2, 1, 4).reshape(DEPTH, 16, 128, NJ * 128))
    cvv = np.zeros((DEPTH, 128, NJ * 4), f32)
    for l in range(DEPTH):
        for kk in range(3):
            cvv[l][:, kk::4] = _fm(ffn_conv_w[l][kk])
        cvv[l][:, 3::4] = _fm(ffn_conv_b[l])
    per_g = []
    for g in range(4):
        cols = []
        for h in range(2):
            cols += list(range(512 + (2 * g + h) * 128, 512 + (2 * g + h + 1) * 128))
        for h in range(2):
            cols += list(range(1536 + (2 * g + h) * 128, 1536 + (2 * g + h + 1) * 128))
        cols += list(range(3592 + g * 128, 3592 + (g + 1) * 128))
        cols += list(range(4104 + g * 128, 4104 + (g + 1) * 128))
        for h in range(2):
            cols += list(range(2560 + (2 * g + h) * 128, 2560 + (2 * g + h + 1) * 128))
        cols += list(range(g * 128, (g + 1) * 128))
        cols += [3584 + 2 * g, 3584 + 2 * g + 1]
        gs = slice(g * 128, (g + 1) * 128)
        pbv = np.zeros((DEPTH, 128, 16), f32)
        for l in range(DEPTH):
            pbv[l][:, 0] = pool_scale[l][gs]
            for kk in range(4):
                pbv[l][:, 1 + kk] = lru_conv_w[l][kk][gs]
            pbv[l][:, 5] = lru_conv_b[l][gs]
            pbv[l][:, 6] = lru_ba[l][gs]
            pbv[l][:, 7] = lru_bi[l][gs]
            pbv[l][:, 8] = lru_lambda[l][gs]
        per_g.append({
            "win": np.ascontiguousarray(w_in[:, :, cols]),
            "pwm": np.ascontiguousarray(np.concatenate([pool_w[:, g], lru_wa[:, g], lru_wi[:, g]], axis=2)),
            "pmat": _pool_mats(POOL_WINDOWS[g]),
            "pbp": pbv,
            "bfp": np.ascontiguousarray(b_f[:, None, 2 * g:2 * g + 2]),
            "wada": np.ascontiguousarray(w_ada[:, :, g * 3072:(g + 1) * 3072]),
        })
    maps = []
    for cid in range(NCORE):
        b, j = cid // 4, cid % 4
        xin = np.zeros((D, TOK + 2), f32)
        if j == 0:
            xin[:, 2:] = x[b, 0:TOK, :].T
        else:
            xin[:, :] = x[b, j * TOK - 2:(j + 1) * TOK, :].T
        msk = np.zeros((128, 9), f32)
        msk[:, j] = 1.0
        if j + 1 < 4:
            msk[:, 4 + j + 1] = 1.0
        msk[:, 8] = 0.0 if j == 0 else 1.0
        m = {"xT": xin, "cT": _fm(c[b]), "bada": bada, "gains": gains, "msk": msk, "cst": cst,
             "w_out": w_out_p, "w_gate": w_ffn_gate, "w_up": w_ffn_up, "w_down": wd_l, "cv": cvv}
        m.update(per_g[j])
        maps.append(m)
    res = run_bass_kernel_spmd(nc, maps, core_ids=list(range(NCORE)))
    r = res.results
    out = np.empty((B, S, D), f32)
    for cid in range(NCORE):
        b, j = cid // 4, cid % 4
        out[b, j * TOK:(j + 1) * TOK, :] = r[cid]["out"].T
    return out
```

```python
import math
from contextlib import ExitStack, contextmanager

import numpy as np
import ml_dtypes

import concourse.bass as bass
import concourse.mybir as mybir
from concourse.bass_utils import run_bass_kernel_spmd

F32 = mybir.dt.float32
BF16 = mybir.dt.bfloat16
AF = mybir.ActivationFunctionType
ALU = mybir.AluOpType

D = 2048
NKC = 16
DFF = 5632
NJ = 44
DEPTH = 4
NCORE = 8
EPS = 1e-6
POOL_WINDOWS = (2, 4, 8, 16)
NIN = 4616
C_Q, C_K, C_ZX, C_ZY, C_V, C_ZP, C_F = 0, 256, 512, 640, 768, 1024, 1152
NIN_G = 1154
MASKNEG = -30000.0


class Sched:
    EPOCH = 30000

    def __init__(self, nc, es):
        self.nc = nc
        self.es = es
        self.engs = {"pe": nc.tensor, "act": nc.scalar, "dve": nc.vector, "pool": nc.gpsimd, "sp": nc.sync}
        self.sems = {}
        self.count = {}
        self.step = {}
        self.waited = {e: {} for e in self.engs}
        self.lastw = {}
        self.readers = {}
        self.ninstr = 0
        self.outs = {}
        self.multiw = {}
        self.log = {e: [] for e in self.engs}

    def _sem(self, chan, ep):
        key = (chan, ep)
        if key not in self.sems:
            self.sems[key] = self.es.enter_context(self.nc.semaphore("s_%s_%d" % (chan, ep)))
        return self.sems[key]

    def _chan(self, chan, step):
        if chan not in self.count:
            self.count[chan] = 0
            self.step[chan] = step

    def _wait(self, eng, chan, total):
        if total <= self.waited[eng].get(chan, 0):
            return
        self.waited[eng][chan] = total
        esz = self.EPOCH * self.step[chan]
        ep = (total - 1) // esz
        self.engs[eng].wait_ge(self._sem(chan, ep), total - ep * esz)
        self.log[eng].append(("wait", (chan, ep), total - ep * esz))
        self.ninstr += 1

    def _deps(self, eng, me, reads, writes):
        for r in reads:
            lw = self.lastw.get(r)
            if lw is not None:
                self._wait(eng, lw[0], lw[1])
        for w in writes:
            lw = self.lastw.get(w)
            if lw is not None and lw[0] != me:
                self._wait(eng, lw[0], lw[1])
            for ch, tot in self.readers.get(w, {}).items():
                if ch != me or me not in ("pe",):
                    self._wait(eng, ch, tot)

    def _record(self, me, total, reads, writes):
        for w in writes:
            self.lastw[w] = (me, total)
            self.readers[w] = {}
        for r in reads:
            d = self.readers.setdefault(r, {})
            d[me] = max(d.get(me, 0), total)

    def op(self, eng, reads, writes, fn, sig=True):
        self._chan(eng, 1)
        for r in reads:
            lw = self.lastw.get(r)
            if lw is not None:
                self._wait(eng, lw[0], lw[1])
        for w in writes:
            lw = self.lastw.get(w)
            if lw is not None and not (eng == "pe" and lw[0] == "pe"):
                self._wait(eng, lw[0], lw[1])
            for ch, tot in self.readers.get(w, {}).items():
                if ch != eng:
                    self._wait(eng, ch, tot)
        ins = fn(self.engs[eng])
        self.ninstr += 1
        total = self.count[eng] + 1
        if sig:
            esz = self.EPOCH
            ep = (total - 1) // esz
            ins.then_inc(self._sem(eng, ep), 1)
            self.log[eng].append(("inc", (eng, ep), 1))
            self.count[eng] = total
        else:
            self.log[eng].append(("nop", None, 0))
        self._record(eng, total, reads, writes)
        return ins

    def dma(self, queue, chan, out, in_, reads, writes, is_out=False, multi_w=()):
        self._chan(chan, 16)
        for r in reads:
            lw = self.lastw.get(r)
            if lw is not None:
                self._wait(queue, lw[0], lw[1])
        for w in writes:
            lw = self.lastw.get(w)
            if lw is not None and lw[0] != chan:
                self._wait(queue, lw[0], lw[1])
            for ch, tot in self.readers.get(w, {}).items():
                self._wait(queue, ch, tot)
        ins = self.engs[queue].dma_start(out=out, in_=in_)
        self.ninstr += 1
        self.count[chan] += 16
        total = self.count[chan]
        assert total < self.EPOCH * 16
        ins.then_inc(self._sem(chan, 0), 16)
        self.log[queue].append(("inc", (chan, 0), 16))
        self._record(chan, total, reads, writes)
        if is_out:
            self.outs[chan] = total
        for r in multi_w:
            self.multiw.setdefault(r, {})[chan] = total
        return ins

    def simulate(self):
        sem = {}
        pos = {e: 0 for e in self.engs}
        progress = True
        while progress:
            progress = False
            for e, lg in self.log.items():
                while pos[e] < len(lg):
                    kind, key, val = lg[pos[e]]
                    if kind == "wait":
                        if sem.get(key, 0) < val:
                            break
                    elif kind == "inc":
                        sem[key] = sem.get(key, 0) + val
                    pos[e] += 1
                    progress = True
        stuck = {e: (pos[e], len(lg), lg[pos[e]] if pos[e] < len(lg) else None) for e, lg in self.log.items() if pos[e] < len(lg)}
        return stuck

    def barrier(self):
        for e in self.engs:
            for ch, tot in self.count.items():
                if ch != e and tot > 0:
                    self._wait(e, ch, tot)
        self.lastw.clear()
        self.readers.clear()
        self.multiw.clear()

    def collective(self, kind, in_t, out_t, reads, writes):
        self._chan("cc", 1)
        for r in reads:
            lw = self.lastw.get(r)
            if lw is not None:
                self._wait("pool", lw[0], lw[1])
            for ch, tot in self.multiw.get(r, {}).items():
                self._wait("pool", ch, tot)
        for w in writes:
            lw = self.lastw.get(w)
            if lw is not None:
                self._wait("pool", lw[0], lw[1])
            for ch, tot in self.readers.get(w, {}).items():
                self._wait("pool", ch, tot)
        ins = self.engs["pool"].collective_compute(kind, ALU.add, replica_groups=[[0, 1, 2, 3], [4, 5, 6, 7]],
                                                   ins=[in_t.ap().opt()], outs=[out_t.ap().opt()], dma_qos="P2")
        self.ninstr += 1
        self.count["cc"] += 1
        ins.then_inc(self._sem("cc", 0))
        self.log["pool"].append(("inc", ("cc", 0), 1))
        self._record("cc", self.count["cc"], reads, writes)
        return ins

    def finish(self, eng):
        for ch, tot in self.outs.items():
            self._wait(eng, ch, tot)


class KB:
    def __init__(self):
        self.nc = bass.Bass("TRN2", target_bir_lowering=False)
        self.es = ExitStack()
        self.s = Sched(self.nc, self.es)
        self._n = 0

    def din(self, name, shape, dt=F32):
        return self.nc.dram_tensor(name, list(shape), dt, kind="ExternalInput").ap()

    def dout(self, name, shape, dt=F32):
        return self.nc.dram_tensor(name, list(shape), dt, kind="ExternalOutput").ap()

    def sb(self, name, shape, dt=F32):
        self._n += 1
        return self.es.enter_context(self.nc.sbuf_tensor("%s_u%d" % (name, self._n), list(shape), dt))

    def ps(self, name, shape=(128, 512), dt=F32):
        self._n += 1
        return self.es.enter_context(self.nc.psum_tensor("%s_u%d" % (name, self._n), list(shape), dt))

    @contextmanager
    def phase(self):
        old = self.es
        self.es = ExitStack()
        try:
            yield
        finally:
            self.es.close()
            self.es = old

    def dscr(self, name, shape, dt=F32):
        return self.nc.dram_tensor(name, list(shape), dt)

    def close(self):
        self.es.close()
        return self.nc


class Rot:
    def __init__(self, items):
        self.items = items
        self.i = 0

    def next(self):
        it = self.items[self.i % len(self.items)]
        self.i += 1
        return it


class WStream:
    def __init__(self, k, name, nslot, slot_elems, prefetch):
        self.k = k
        self.slots = [k.sb("%s_slot%d" % (name, i), [128, slot_elems], BF16) for i in range(nslot)]
        self.name = name
        self.nslot = nslot
        self.pf = prefetch
        self.loads = []
        self.issued = 0

    def add(self, fn):
        self.loads.append(fn)
        return len(self.loads) - 1

    def res(self, i):
        return "%s_w%d" % (self.name, i % self.nslot)

    def _issue(self, i):
        slot = self.slots[i % self.nslot]
        view, pairs = self.loads[i](slot)
        for (o, a) in pairs:
            self.k.s.dma("pool", "%s_c%d" % (self.name, i % self.nslot), o, a, [], [self.res(i)])
        return view

    def get(self, i):
        while self.issued < min(len(self.loads), i + 1 + self.pf):
            self._issue(self.issued)
            self.issued += 1
        slot = self.slots[i % self.nslot]
        view, _ = self.loads[i](slot)
        return view, self.res(i)


def v3(t, a, b):
    return t[:, 0:a * b].rearrange("p (a b) -> p a b", b=b)


def emit_norm(k, tag, x3, xres, W, aeff, sq3, sqres, ones_bf, ss_ps, ss_res, rstd, rstd_res, tmps, sink):
    s = k.s
    xr = xres if isinstance(xres, list) else [xres] * NKC
    s.op("act", sorted(set(xr)), [sqres], lambda e: e.activation(out=sq3[:, 0:NKC, 0:W], in_=x3[:, 0:NKC, 0:W], func=AF.Square))
    for kc in range(NKC):
        s.op("pe", [sqres], [ss_res],
             lambda e, kc=kc: e.matmul(ss_ps[:, 0:W], ones_bf[:, :], sq3[:, kc, 0:W], start=(kc == 0), stop=(kc == NKC - 1)),
             sig=(kc == NKC - 1))
    s.op("act", [ss_res], [rstd_res],
         lambda e: e.activation(out=rstd[:, 0:W], in_=ss_ps[:, 0:W], func=AF.Sqrt, bias=float(D * EPS)))
    s.op("dve", [rstd_res], [rstd_res], lambda e: e.reciprocal(out=rstd[:, 0:W], in_=rstd[:, 0:W]))
    for kc in range(NKC):
        tres, tmp = tmps.next()
        s.op("dve", [xr[kc], rstd_res], [tres],
             lambda e, kc=kc, tmp=tmp: e.scalar_tensor_tensor(out=tmp[:, 0:W], in0=x3[:, kc, 0:W], scalar=aeff[:, kc:kc + 1],
                                                              in1=rstd[:, 0:W], op0=ALU.mult, op1=ALU.mult))
        sink(kc, tmp, tres)


def emit_aeff(k, prm, cg, csc, aeff, res_in, res_out):
    s = k.s
    s.op("dve", [res_in], [res_out],
         lambda e: e.tensor_scalar(out=aeff[:, :], in0=prm[:, csc:csc + 16], scalar1=1.0, scalar2=float(math.sqrt(D)),
                                   op0=ALU.add, op1=ALU.mult))
    s.op("dve", [res_in, res_out], [res_out],
         lambda e: e.tensor_tensor(out=aeff[:, :], in0=aeff[:, :], in1=prm[:, cg:cg + 16], op=ALU.mult))


def emit_aeff2(k, sc_ap, g_ap, aeff, res_out):
    s = k.s
    s.op("dve", ["mods"], [res_out],
         lambda e: e.tensor_scalar(out=aeff[:, :], in0=sc_ap, scalar1=1.0, scalar2=float(math.sqrt(D)), op0=ALU.add, op1=ALU.mult))
    s.op("dve", ["gn", res_out], [res_out], lambda e: e.tensor_tensor(out=aeff[:, :], in0=aeff[:, :], in1=g_ap, op=ALU.mult))


class HSlots:
    def __init__(self, k, G, sh_ap, tag):
        self.k = k
        self.G = G
        s = k.s
        self.shm = k.sb("shm_" + tag, [128, 4, 16])
        for sl in range(4):
            s.op("dve", ["mods", "msk"], ["shm"],
                 lambda e, sl=sl: e.tensor_scalar(out=self.shm[:, sl, :], in0=sh_ap, scalar1=G["msk"][:, sl:sl + 1], scalar2=None, op0=ALU.mult))
        self.stg = Rot([("stg%d" % i, k.sb("stg%d_%s" % (i, tag), [128, 4, 2, 512], BF16)) for i in range(2)])
        self.cur = None

    def sink(self, t, kc, tmp, tres):
        s = self.k.s
        G = self.G
        if kc % 2 == 0:
            self.cur = self.stg.next()
        sres, st = self.cur
        for sl in range(4):
            if sl < 2:
                s.op("act", [tres, "shm", "msk"], [sres],
                     lambda e, sl=sl, st=st: e.activation(out=st[:, sl, kc % 2, :], in_=tmp[:, :], func=AF.Identity,
                                                          bias=self.shm[:, sl, kc:kc + 1], scale=G["msk"][:, sl:sl + 1]))
            else:
                s.op("dve", [tres, "shm", "msk"], [sres],
                     lambda e, sl=sl, st=st: e.tensor_scalar(out=st[:, sl, kc % 2, :], in0=tmp[:, :], scalar1=G["msk"][:, sl:sl + 1],
                                                             scalar2=self.shm[:, sl, kc:kc + 1], op0=ALU.mult, op1=ALU.add))
        if kc % 2 == 1:
            hf, col = (t * 512) // G["PW"], (t * 512) % G["PW"]
            for sl in range(4):
                hi_ = sl * G["NH"] + hf
                hbv = G["HB"][hi_].ap().rearrange("(kc p) t -> p kc t", p=128)
                s.dma("sp", "sthb_" + sres, hbv[:, kc - 1:kc + 1, col:col + 512], st[:, sl, :, :], [sres], [], multi_w=["HB%d" % hi_])


def emit_e1(k, G, which="all"):
    for i in range(4 * G["NH"]):
        early = (G["NH"] == 2 and i % 2 == 0)
        if which == "all" or (which == "early") == early:
            k.s.collective("AllReduce", G["HB"][i], G["HG"][i], ["HB%d" % i], ["HG%d" % i])


def phase_m(k, G):
    s = k.s
    with k.phase():
        c_sb = k.sb("c_sb", [128, 16])
        sig = k.sb("sig", [128, 16])
        cact = k.sb("cact", [128, 16], BF16)
        mq = k.sb("mq", [128, 96])
        mq4 = k.sb("mq4", [128, 4, 96])
        bada = k.sb("bada_sb", [128, 384])
        mq_ps = k.ps("mq_ps")
        s.dma("sp", "ld", c_sb[:, :], G["cT"][:, :], [], ["c_sb"])
        s.dma("sp", "ld2", bada[:, :], G["bada"][:, :], [], ["bada"])
        s.dma("sp", "ld3", G["gn"][:, :], G["gains"][:, :], [], ["gn"])
        s.dma("sp", "ld4", G["msk"][:, :], G["msk_d"][:, :], [], ["msk"])
        s.op("dve", [], ["zero"], lambda e: e.memset(G["zero"][:, :], 0.0))
        s.op("act", ["c_sb"], ["sig"], lambda e: e.activation(out=sig[:, :], in_=c_sb[:, :], func=AF.Sigmoid))
        s.op("dve", ["c_sb", "sig"], ["cact"], lambda e: e.tensor_tensor(out=cact[:, :], in0=c_sb[:, :], in1=sig[:, :], op=ALU.mult))
        ws = WStream(k, "wm", 3, 8192, 2)
        for l in range(DEPTH):
            wv = G["wada"][l].rearrange("(kc p) n -> p kc n", p=128)
            for gq in range(6):
                ws.add(lambda slot, wv=wv, gq=gq: (v3(slot, 16, 512), [(v3(slot, 16, 512), wv[:, :, gq * 512:(gq + 1) * 512])]))
        i = 0
        for l in range(DEPTH):
            for gq in range(6):
                w3, wres = ws.get(i)
                i += 1
                for nn in range(4):
                    col = l * 24 + gq * 4 + nn
                    for kc in range(NKC):
                        s.op("pe", ["cact", wres], ["mq_ps"],
                             lambda e, kc=kc, w3=w3, nn=nn, col=col: e.matmul(mq_ps[:, col:col + 1], w3[:, kc, nn * 128:(nn + 1) * 128], cact[:, kc:kc + 1],
                                                                              start=(kc == 0), stop=(kc == NKC - 1)), sig=(kc == NKC - 1))
        s.op("act", ["mq_ps"], ["mq"], lambda e: e.mul(out=mq[:, :], in_=mq_ps[:, 0:96], mul=1.0))
        for sl in range(4):
            s.op("dve", ["mq", "msk"], ["mq4"],
                 lambda e, sl=sl: e.tensor_scalar(out=mq4[:, sl, :], in0=mq[:, :], scalar1=G["msk"][:, sl:sl + 1], scalar2=None, op0=ALU.mult))
        s.dma("sp", "stm", G["MB"].ap().rearrange("(s p) n -> p s n", p=128), mq4[:, :, :], ["mq4"], ["MB"])
        s.collective("AllReduce", G["MB"], G["MG"], ["MB"], ["MG"])
        mods_v = G["mods"][:, :].rearrange("p (l q i) -> p l q i", q=4, i=24)
        for q in range(4):
            s.dma("sp", "ldm%d" % q, mods_v[:, :, q, :], G["MG"].ap()[q * 128:(q + 1) * 128, :].rearrange("p (l i) -> p l i", i=24), ["MG"], ["mods"])
        s.op("dve", ["mods", "bada"], ["mods"], lambda e: e.tensor_tensor(out=G["mods"][:, :], in0=G["mods"][:, :], in1=bada[:, :], op=ALU.add))
        zerob = k.sb("zerob", [128, 16, 2], BF16)
        s.op("dve", [], ["zerob"], lambda e: e.memset(zerob[:, :, :], 0.0))
        s.dma("sp", "stz", G["YS"][0].ap()[0:D, 0:2].rearrange("(kc p) c -> p kc c", p=128), zerob[:, :, :], ["zerob"], ["YSz"])
        s.dma("sp", "stx0", G["XS"].ap()[:, :], G["xT"][:, :], [], ["XS"])
        s.barrier()


def phase_a(k, G, TOK):
    s = k.s
    NT = TOK // 512
    with k.phase():
        xv = G["xT"].rearrange("(kc p) t -> p kc t", p=128)
        aeff = k.sb("aeff", [128, 16])
        ones_bf = k.sb("ones_bf", [128, 128], BF16)
        xts = [("x%d" % i, k.sb("xa%d" % i, [128, 16, 512])) for i in range(2)]
        sq3 = k.sb("sqa", [128, 16, 512], BF16)
        rstd = k.sb("rstd", [128, 512])
        tmps = Rot([("tmp%d" % i, k.sb("tmpa%d" % i, [128, 512])) for i in range(3)])
        ss_ps = k.ps("ss_ps")
        s.op("dve", [], ["ones"], lambda e: e.memset(ones_bf[:, :], 1.0))
        M = G["mods"]
        emit_aeff2(k, M[:, 16:32], G["gn"][:, 0:16], aeff, "aeff")
        hs = HSlots(k, G, M[:, 0:16], "a")
        for t in range(NT):
            xres, x3 = xts[t % 2]
            s.dma("sp", "ldx%d" % (t % 2), x3[:, :, :], xv[:, :, 2 + t * 512:2 + (t + 1) * 512], [], [xres])
            emit_norm(k, "n", x3, xres, 512, aeff, sq3, "sq", ones_bf, ss_ps, "ss_ps", rstd, "rstd", tmps,
                      lambda kc, tmp, tres, t=t: hs.sink(t, kc, tmp, tres))
        s.barrier()
    return lambda: emit_e1(k, G)


def phase_b(k, G, l, S, after_setup=None):
    s = k.s
    NT = S // 512
    NB = S // 128
    TOK = S // 4
    with k.phase():
        wv = G["win"][l].rearrange("(kc p) n -> p kc n", p=128)
        ysvs = [ys.ap().rearrange("(j g r p) t -> j g p r t", j=4, g=4, p=128) for ys in G["YS"]]
        nh = len(G["YS"])
        win = k.sb("win_sb", [128, 16, 770], BF16)
        wvz = k.sb("wvz", [128, 16, 384], BF16)
        pwm = k.sb("pwm_sb", [128, 384], BF16)
        pmat = k.sb("pmat_sb", [128, 384], BF16)
        cst = k.sb("cst_sb", [128, 640])
        pb = k.sb("pb_sb", [128, 16])
        bfs = k.sb("bf_sb", [1, 2])
        nbf = k.sb("nbf", [1, 2])
        cA = k.sb("cA", [128, 2])
        lt = k.sb("lt", [128, 2])
        ones_bf = k.sb("ones_bf", [128, 128], BF16)
        ones2 = k.sb("ones2", [128, 128], BF16)
        ones_row = k.sb("ones_row", [1, 512])
        hts = [("ht%d" % i, k.sb("ht%d" % i, [128, 16, 512], BF16)) for i in range(2)]
        kT = k.sb("kT", [128, 2, S], BF16)
        vtok = k.sb("vtok", [128, NB, 256], BF16)
        zptok = k.sb("zptok", [128, 8, 128], BF16)
        qT = k.sb("qT", [128, 2, 512], BF16)
        FQ = k.sb("FQ", [128, 2, 512], BF16)
        Gk = k.sb("Gk", [128, 2, NB])
        Gq = [k.sb("Gq%d" % h, [1, 512]) for h in range(2)]
        Gc = k.sb("Gc", [1, 2])
        sp1 = k.sb("sp1", [1, 512])
        lsp = k.sb("lsp", [1, 512])
        hib = k.sb("hib", [1, 512], BF16)
        hif = lsp
        lo = sp1
        gnb = [k.sb("gnb%d" % h, [128, 512]) for h in range(2)]
        nlo = k.sb("nlo", [1, 512], BF16)
        pts = Rot([("pt%d" % i, k.sb("pt%d" % i, [128, 512], BF16)) for i in range(4)])
        dgs = Rot([("dg%d" % i, k.sb("dg%d" % i, [128, 512])) for i in range(1)])
        rden = k.sb("rden", [128, 512])
        yb = k.sb("yb", [128, 4, 512], BF16)
        yres = "yb"
        yms = Rot([("ym%d" % i, k.sb("ym%d" % i, [128, 4, 512], BF16)) for i in range(2)])
        zxb = k.sb("zxb", [128, 516])
        zy = k.sb("zy", [128, 512])
        dT = k.sb("dT", [128, 512], BF16)
        xc = k.sb("xc", [128, 512])
        xcb = k.sb("xcb", [128, 512], BF16)
        gr = k.sb("gr", [128, 512])
        gi = k.sb("gi", [128, 512])
        av = k.sb("av", [128, 512])
        a2 = k.sb("a2", [128, 512])
        mm_ = k.sb("mm", [128, 512])
        inp = k.sb("inp", [128, 512])
        hh = k.sb("hh", [128, 512])
        hc = k.sb("hc", [128, 1])
        uu = k.sb("uu", [128, 512])
        sg = k.sb("sg", [128, 512])
        pj = Rot([("pj%d" % i, k.ps("pj%d" % i)) for i in range(2)])
        sts = Rot([("st%d" % i, k.ps("st%d" % i)) for i in range(4)])
        o_one = k.ps("o_ps")
        d_one = k.ps("d_ps")
        o_ps = [o_one, o_one]
        d_ps = [d_one, d_one]
        ident = cst[:, 0:128]
        trif = cst[:, 128:640]
        msk = G["msk"]

        for q4 in range(4):
            s.dma("pool", "ldw", win[:, 4 * q4:4 * q4 + 4, 0:768], wv[:, 4 * q4:4 * q4 + 4, 0:768], [], ["win"])
        s.dma("pool", "ldw", win[:, :, 768:770], wv[:, :, C_F:C_F + 2], [], ["win"])
        s.dma("pool", "ldw4", wvz[:, :, :], wv[:, :, C_V:C_V + 384], [], ["wvz"])
        s.dma("pool", "ldw2", pwm[:, :], G["pwm"][l], [], ["pwm"])
        s.dma("pool", "ldw3", pmat[:, :], G["pmat"][:, :], [], ["pmat"])
        s.dma("sp", "ldc", cst[:, :], G["cst"][:, :], [], ["cst"])
        s.dma("sp", "ldc2", pb[:, :], G["pbp"][l], [], ["pb"])
        s.dma("sp", "ldc3", bfs[:, :], G["bfp"][l], [], ["bf"])
        s.op("dve", [], ["ones"], lambda e: e.memset(ones_bf[:, :], 1.0))
        s.op("dve", [], ["ones2"], lambda e: e.memset(ones2[:, :], 0.0))
        s.op("dve", ["ones2"], ["ones2"], lambda e: e.memset(ones2[0:2, :], 1.0))
        s.op("dve", [], ["ones_row"], lambda e: e.memset(ones_row[:, :], 1.0))
        s.op("dve", [], ["FQ0", "FQ1"], lambda e: e.memset(FQ[:, :, :], 0.0))
        s.op("dve", [], ["zxb"], lambda e: e.memset(zxb[:, 0:4], 0.0))
        s.op("dve", [], ["hc"], lambda e: e.memset(hc[:, :], 0.0))
        s.op("dve", [], ["Gc0", "Gc1"], lambda e: e.memset(Gc[:, :], 0.0))
        s.op("dve", ["bf"], ["nbf"], lambda e: e.tensor_scalar(out=nbf[:, :], in0=bfs[:, :], scalar1=-1.0, scalar2=None, op0=ALU.mult))
        s.op("act", ["pb"], ["lt"], lambda e: e.activation(out=lt[:, 0:1], in_=pb[:, 8:9], func=AF.Exp, scale=-1.0))
        s.op("act", ["lt"], ["lt"], lambda e: e.activation(out=lt[:, 1:2], in_=lt[:, 0:1], func=AF.Ln, bias=1.0))
        s.op("dve", ["lt"], ["cA"], lambda e: e.tensor_scalar(out=cA[:, 0:1], in0=lt[:, 1:2], scalar1=-8.0, scalar2=None, op0=ALU.mult))
        s.op("dve", ["lt", "cA"], ["cA"], lambda e: e.tensor_scalar(out=cA[:, 1:2], in0=lt[:, 1:2], scalar1=-16.0, scalar2=None, op0=ALU.mult))

        if after_setup is not None:
            after_setup()

        QSCALE = float(128 ** -0.5)

        def fm_proj(col, ht3, hres):
            pres, ps = pj.next()
            for kc in range(NKC):
                s.op("pe", [hres, "win"], [pres],
                     lambda e, kc=kc, ps=ps: e.matmul(ps[:, :], win[:, kc, col:col + 128], ht3[:, kc, :], start=(kc == 0), stop=(kc == NKC - 1)),
                     sig=(kc == NKC - 1))
            return pres, ps

        tps = TOK // 512

        def load_h(T):
            hres, ht3 = hts[T % 2]
            sl_, tt = T // tps, T % tps
            hgi = sl_ * G["NH"] + (tt * 512) // G["PW"]
            hcol = (tt * 512) % G["PW"]
            hgv = G["HG"][hgi].ap().rearrange("(kc p) t -> p kc t", p=128)
            s.dma("sp", "ldh%d" % (T % 2), ht3[:, :, :], hgv[:, :, hcol:hcol + 512], ["HG%d" % hgi], [hres])

        load_h(0)
        for T in range(NT):
            t0 = T * 512
            hres, ht3 = hts[T % 2]
            if T + 1 < NT:
                load_h(T + 1)

            f_pss = []
            for h in range(2):
                pres, ps = sts.next()
                for kc in range(NKC):
                    s.op("pe", [hres, "win"], [pres],
                         lambda e, kc=kc, ps=ps, h=h: e.matmul(ps[0:1, :], win[:, kc, 768 + h:769 + h], ht3[:, kc, :],
                                                               start=(kc == 0), stop=(kc == NKC - 1)), sig=(kc == NKC - 1))
                f_pss.append((pres, ps))
            for h in range(2):
                pres, ps = f_pss[h]
                s.op("act", [pres, "nbf"], ["sp1"],
                     lambda e, ps=ps, h=h: e.activation(out=sp1[:, :], in_=ps[0:1, :], func=AF.Exp, bias=nbf[0:1, h:h + 1], scale=-1.0))
                s.op("act", ["sp1"], ["lsp"], lambda e: e.activation(out=lsp[:, :], in_=sp1[:, :], func=AF.Ln, bias=1.0))
                s.op("dve", ["lsp", "ones_row", "Gc%d" % h], ["Gq%d" % h],
                     lambda e, h=h: e.tensor_tensor_scan(out=Gq[h][:, :], data0=ones_row[:, :], data1=lsp[:, :], initial=Gc[0:1, h:h + 1],
                                                         op0=ALU.mult, op1=ALU.add))
                s.op("dve", ["Gq%d" % h], ["Gc%d" % h], lambda e, h=h: e.tensor_copy(out=Gc[0:1, h:h + 1], in_=Gq[h][:, 511:512]))
                s.op("dve", ["Gq%d" % h], ["hib"], lambda e, h=h: e.tensor_copy(out=hib[:, :], in_=Gq[h][:, :]))
                s.op("dve", ["hib", "lsp"], ["lsp"], lambda e: e.tensor_copy(out=hif[:, :], in_=hib[:, :]))
                s.op("dve", ["Gq%d" % h, "lsp", "sp1"], ["sp1"], lambda e, h=h: e.tensor_tensor(out=lo[:, :], in0=Gq[h][:, :], in1=hif[:, :], op=ALU.subtract))
                s.op("dve", ["lsp"], ["FQ%d" % h],
                     lambda e, h=h: e.tensor_scalar(out=FQ[0:1, h, :], in0=hif[:, :], scalar1=-1.0, scalar2=None, op0=ALU.mult))
                s.op("dve", ["sp1"], ["nlo"], lambda e: e.tensor_scalar(out=nlo[:, :], in0=lo[:, :], scalar1=-1.0, scalar2=None, op0=ALU.mult))
                s.dma("sp", "fq%d" % h, FQ[1:2, h, :], nlo[0:1, :], ["nlo"], ["FQ%d" % h])

            for h in range(2):
                pres, ps = fm_proj(C_Q + h * 128, ht3, hres)
                s.op("act", [pres], ["qT%d" % h], lambda e, ps=ps, h=h: e.mul(out=qT[:, h, :], in_=ps[:, :], mul=QSCALE))
            for h in range(2):
                pres, ps = fm_proj(C_K + h * 128, ht3, hres)
                s.op("dve", [pres], ["kT"], lambda e, ps=ps, h=h: e.tensor_copy(out=kT[:, h, t0:t0 + 512], in_=ps[:, :]))
            pres, ps = fm_proj(C_ZX, ht3, hres)
            s.op("act", [pres], ["zxb"], lambda e, ps=ps: e.mul(out=zxb[:, 4:516], in_=ps[:, :], mul=1.0))
            pres, ps = fm_proj(C_ZY, ht3, hres)
            s.op("act", [pres], ["zy"], lambda e, ps=ps: e.mul(out=zy[:, :], in_=ps[:, :], mul=1.0))
            for jb in range(4):
                blk = 4 * T + jb
                pres, ps = pj.next()
                for kc in range(NKC):
                    s.op("pe", [hres, "wvz"], [pres],
                         lambda e, kc=kc, ps=ps, jb=jb: e.matmul(ps[:, 0:384], ht3[:, kc, jb * 128:(jb + 1) * 128], wvz[:, kc, :],
                                                                 start=(kc == 0), stop=(kc == NKC - 1)), sig=(kc == NKC - 1))
                s.op("act", [pres], ["vtok"], lambda e, ps=ps, blk=blk: e.mul(out=vtok[:, blk, :], in_=ps[:, 0:256], mul=1.0))
                s.op("act", [pres], ["zptok"], lambda e, ps=ps, blk=blk: e.mul(out=zptok[:, blk % 8, :], in_=ps[:, 256:384], mul=1.0))

            tp_res, tp_ps = pj.next()
            for h in range(2):
                gres_, gps_ = sts.next()
                s.op("pe", ["ones2", "FQ%d" % h], [gres_], lambda e, h=h, gps_=gps_: e.matmul(gps_[:, :], ones2[:, :], FQ[:, h, :], start=True, stop=True))
                s.op("act", [gres_], ["gnb%d" % h], lambda e, h=h, gps_=gps_: e.mul(out=gnb[h][:, :], in_=gps_[:, :], mul=1.0))
                for jb in range(4):
                    s.op("pe", ["Gq%d" % h, "cst"], [tp_res],
                         lambda e, h=h, jb=jb: e.transpose(tp_ps[:, h * 4 + jb:h * 4 + jb + 1], Gq[h][0:1, jb * 128:(jb + 1) * 128], ident[0:1, 0:1]),
                         sig=(jb == 3))
            s.op("dve", [tp_res], ["Gk"], lambda e, T=T: e.tensor_copy(out=Gk[:, 0, 4 * T:4 * T + 4], in_=tp_ps[:, 0:4]))
            s.op("dve", [tp_res], ["Gk"], lambda e, T=T: e.tensor_copy(out=Gk[:, 1, 4 * T:4 * T + 4], in_=tp_ps[:, 4:8]))

            s.op("dve", ["zxb", "pb"], ["xc"],
                 lambda e: e.tensor_scalar(out=xc[:, :], in0=zxb[:, 1:513], scalar1=pb[:, 1:2], scalar2=pb[:, 5:6], op0=ALU.mult, op1=ALU.add))
            for kk in range(1, 4):
                s.op("dve", ["zxb", "pb", "xc"], ["xc"],
                     lambda e, kk=kk: e.scalar_tensor_tensor(out=xc[:, :], in0=zxb[:, 1 + kk:513 + kk], scalar=pb[:, 1 + kk:2 + kk], in1=xc[:, :],
                                                             op0=ALU.mult, op1=ALU.add))
            s.op("dve", ["zxb"], ["zxb"], lambda e: e.tensor_copy(out=zxb[:, 1:4], in_=zxb[:, 513:516]))
            s.op("act", ["xc"], ["xcb"], lambda e: e.mul(out=xcb[:, :], in_=xc[:, :], mul=1.0))

            nkb = 4 * T + 4
            blocks = [(h, kb) for h in range(2) for kb in range(nkb)]
            info = {}

            def emit_qk(i):
                h, kb = blocks[i]
                jj = kb - 4 * T
                c0 = jj * 128 if jj >= 0 else 0
                stres, st = sts.next()
                ptres, pt = pts.next()
                info[i] = (ptres, pt, c0)
                s.op("pe", ["kT", "qT%d" % h], [stres],
                     lambda e: e.matmul(st[:, c0:512], kT[:, h, kb * 128:(kb + 1) * 128], qT[:, h, c0:512], start=True, stop=True))
                s.op("dve", [stres, "gnb%d" % h], [stres],
                     lambda e: e.tensor_tensor(out=st[:, c0:512], in0=st[:, c0:512], in1=gnb[h][:, c0:512], op=ALU.add))
                if jj >= 0:
                    dgres, dg = dgs.next()
                    n = 512 - c0
                    s.op("dve", [stres, "cst"], [dgres],
                         lambda e: e.tensor_tensor(out=dg[:, 0:n], in0=st[:, c0:512], in1=trif[:, 0:n], op=ALU.add))
                    s.op("act", [dgres, "Gk"], [ptres],
                         lambda e: e.activation(out=pt[:, c0:512], in_=dg[:, 0:n], func=AF.Exp, bias=Gk[:, h, kb:kb + 1]))
                else:
                    s.op("act", [stres, "Gk"], [ptres],
                         lambda e: e.activation(out=pt[:, :], in_=st[:, :], func=AF.Exp, bias=Gk[:, h, kb:kb + 1]))

            def emit_pv(i):
                h, kb = blocks[i]
                ptres, pt, c0 = info.pop(i)
                last = (kb == nkb - 1)
                s.op("pe", ["vtok", ptres], ["o_ps"],
                     lambda e: e.matmul(o_ps[h][:, c0:512], vtok[:, kb, h * 128:(h + 1) * 128], pt[:, c0:512], start=(kb == 0), stop=last), sig=False)
                s.op("pe", ["ones", ptres], ["d_ps"],
                     lambda e: e.matmul(d_ps[h][:, c0:512], ones_bf[:, :], pt[:, c0:512], start=(kb == 0), stop=last))
                if last:
                    s.op("dve", ["d_ps"], ["rden"], lambda e: e.reciprocal(out=rden[:, :], in_=d_ps[h][:, :]))
                    s.op("dve", ["o_ps", "d_ps", "rden"], [yres],
                         lambda e: e.tensor_tensor(out=yb[:, 1 + h, :], in0=o_ps[h][:, :], in1=rden[:, :], op=ALU.mult))

            LA = 3
            for i in range(min(LA, len(blocks))):
                emit_qk(i)
            for i in range(len(blocks)):
                if i + LA < len(blocks):
                    emit_qk(i + LA)
                emit_pv(i)

            pres, ps = pj.next()
            for jb in range(4):
                blk = 4 * T + jb
                pc0 = 0 if blk == 0 else 128
                s.op("pe", ["zptok", "pmat"], [pres],
                     lambda e, ps=ps, jb=jb, blk=blk, pc0=pc0: e.matmul(ps[:, jb * 128:(jb + 1) * 128], zptok[:, blk % 8, :], pmat[:, pc0:pc0 + 128],
                                                                        start=True, stop=(blk == 0)), sig=(blk == 0))
                if blk > 0:
                    s.op("pe", ["zptok", "pmat"], [pres],
                         lambda e, ps=ps, jb=jb, blk=blk: e.matmul(ps[:, jb * 128:(jb + 1) * 128], zptok[:, (blk - 1) % 8, :], pmat[:, 256:384],
                                                                   start=False, stop=True), sig=True)
            s.op("act", [pres], ["dT"], lambda e, ps=ps: e.mul(out=dT[:, :], in_=ps[:, :], mul=1.0))
            pres2, ps2 = pj.next()
            s.op("pe", ["dT", "pwm"], [pres2], lambda e, ps2=ps2: e.matmul(ps2[:, :], pwm[:, 0:128], dT[:, :], start=True, stop=True))
            s.op("dve", [pres2, "pb"], [yres],
                 lambda e, ps2=ps2: e.tensor_scalar(out=yb[:, 0, :], in0=ps2[:, :], scalar1=pb[:, 0:1], scalar2=None, op0=ALU.mult))

            rres, rps = pj.next()
            s.op("pe", ["xcb", "pwm"], [rres], lambda e, rps=rps: e.matmul(rps[:, :], pwm[:, 128:256], xcb[:, :], start=True, stop=True))
            ires, ips = pj.next()
            s.op("pe", ["xcb", "pwm"], [ires], lambda e, ips=ips: e.matmul(ips[:, :], pwm[:, 256:384], xcb[:, :], start=True, stop=True))
            s.op("act", [rres, "pb"], ["gr"], lambda e, rps=rps: e.activation(out=gr[:, :], in_=rps[:, :], func=AF.Sigmoid, bias=pb[:, 6:7]))
            s.op("act", [ires, "pb"], ["gi"], lambda e, ips=ips: e.activation(out=gi[:, :], in_=ips[:, :], func=AF.Sigmoid, bias=pb[:, 7:8]))
            s.op("act", ["gr", "cA"], ["av"], lambda e: e.activation(out=av[:, :], in_=gr[:, :], func=AF.Exp, scale=cA[:, 0:1]))
            s.op("act", ["gr", "cA"], ["a2"], lambda e: e.activation(out=a2[:, :], in_=gr[:, :], func=AF.Exp, scale=cA[:, 1:2]))
            s.op("dve", ["a2"], ["mm"], lambda e: e.tensor_scalar(out=mm_[:, :], in0=a2[:, :], scalar1=-1.0, scalar2=1.0, op0=ALU.mult, op1=ALU.add))
            s.op("dve", ["mm"], ["mm"], lambda e: e.tensor_scalar(out=mm_[:, :], in0=mm_[:, :], scalar1=1e-30, scalar2=None, op0=ALU.max))
            s.op("act", ["mm"], ["mm"], lambda e: e.activation(out=mm_[:, :], in_=mm_[:, :], func=AF.Sqrt))
            s.op("dve", ["gi", "xc"], ["inp"], lambda e: e.tensor_tensor(out=inp[:, :], in0=gi[:, :], in1=xc[:, :], op=ALU.mult))
            s.op("dve", ["inp", "mm"], ["inp"], lambda e: e.tensor_tensor(out=inp[:, :], in0=inp[:, :], in1=mm_[:, :], op=ALU.mult))
            s.op("dve", ["av", "inp", "hc"], ["hh"],
                 lambda e: e.tensor_tensor_scan(out=hh[:, :], data0=av[:, :], data1=inp[:, :], initial=hc[:, 0:1], op0=ALU.mult, op1=ALU.add))
            s.op("dve", ["hh"], ["hc"], lambda e: e.tensor_copy(out=hc[:, :], in_=hh[:, 511:512]))
            s.op("dve", ["zy"], ["uu"], lambda e: e.tensor_tensor(out=uu[:, :], in0=zy[:, :], in1=zy[:, :], op=ALU.mult))
            s.op("dve", ["uu"], ["uu"], lambda e: e.tensor_scalar(out=uu[:, :], in0=uu[:, :], scalar1=0.044715, scalar2=1.0, op0=ALU.mult, op1=ALU.add))
            s.op("dve", ["uu", "zy"], ["uu"], lambda e: e.tensor_tensor(out=uu[:, :], in0=uu[:, :], in1=zy[:, :], op=ALU.mult))
            s.op("act", ["uu"], ["sg"], lambda e: e.activation(out=sg[:, :], in_=uu[:, :], func=AF.Sigmoid, scale=1.5957691216057308))
            s.op("dve", ["sg", "zy"], ["sg"], lambda e: e.tensor_tensor(out=sg[:, :], in0=sg[:, :], in1=zy[:, :], op=ALU.mult))
            s.op("dve", ["sg", "hh"], [yres], lambda e: e.tensor_tensor(out=yb[:, 3, :], in0=sg[:, :], in1=hh[:, :], op=ALU.mult))

            j = T // tps
            tt = T % tps
            if nh == 2 and tt >= tps // 2:
                half, cbase = 1, (tt - tps // 2) * 512
            else:
                half, cbase = 0, 2 + tt * 512
            for g2 in range(4):
                ymres, ym = yms.next()
                s.op("dve", [yres, "msk"], [ymres],
                     lambda e, ym=ym, g2=g2: e.tensor_scalar(out=ym[:, :, :], in0=yb[:, :, :], scalar1=msk[:, g2:g2 + 1], scalar2=None, op0=ALU.mult))
                s.dma("sp", "sty_" + ymres, ysvs[half][j, g2, :, :, cbase:cbase + 512], ym[:, :, :], [ymres], [], multi_w=["YS%d" % half])
                if tt == tps - 1 and j < 3:
                    s.dma("sp", "sty_" + ymres, ysvs[0][j + 1, g2, :, :, 0:2], ym[:, :, 510:512], [ymres], [], multi_w=["YS0"])
            if nh == 2 and T == 3 * tps + tps // 2 - 1:
                s.collective("ReduceScatter", G["YS"][0], G["YR"][0], ["YS0"], ["YR0"])
        if nh == 1:
            s.collective("ReduceScatter", G["YS"][0], G["YR"][0], ["YS0"], ["YR0"])
            s.barrier()
        else:
            s.barrier()
            G["pending_rs"] = lambda: s.collective("ReduceScatter", G["YS"][1], G["YR"][1], ["YS1"], ["YR1"])


def phase_c(k, G, l, TOK):
    s = k.s
    NT = TOK // 512
    final = (l == DEPTH - 1)
    with k.phase():
        xv = G["XS"].ap().rearrange("(kc p) t -> p kc t", p=128)
        yvs = [yr.ap().rearrange("(kc p) t -> p kc t", p=128) for yr in G["YR"]]
        nh = len(yvs)
        ov = G["out"].rearrange("(kc p) t -> p kc t", p=128)
        wov = G["w_out"][l].rearrange("(kc p) n -> p kc n", p=128)
        wgv = G["w_gate"][l].rearrange("(kc p) n -> p kc n", p=128)
        wuv = G["w_up"][l].rearrange("(kc p) n -> p kc n", p=128)
        w_down = G["w_down"][l]
        M = G["mods"]
        GN = G["gn"]
        msk = G["msk"]
        mo = l * 96
        gt1 = M[:, mo + 32:mo + 48]
        sh2 = M[:, mo + 48:mo + 64]
        sc2 = M[:, mo + 64:mo + 80]
        gt2 = M[:, mo + 80:mo + 96]
        if not final:
            g_n = GN[:, (l + 1) * 16:(l + 2) * 16]
            sh_n = M[:, mo + 96:mo + 112]
            sc_n = M[:, mo + 112:mo + 128]
        else:
            g_n = GN[:, 128:144]
            sh_n = G["zero"][:, 0:16]
            sc_n = G["zero"][:, 0:16]

        cv = k.sb("cv_sb", [128, NJ * 4])
        aeff2 = k.sb("aeff2", [128, 16])
        aeffn = k.sb("aeffn", [128, 16])
        ones_bf = k.sb("ones_bf", [128, 128], BF16)
        x3 = k.sb("x3", [128, 16, 512])
        y3 = k.sb("y3", [128, 16, 512], BF16)
        h23 = y3
        act3 = k.sb("act3", [128, NJ, 512], BF16)
        sq3 = act3
        xh = k.sb("xh", [128, 16, 2])
        yh = k.sb("yh", [128, 16, 2], BF16)
        h2h = k.sb("h2h", [128, 16, 2], BF16)
        gprev = k.sb("gprev", [128, NJ, 2])
        xl4 = k.sb("xl4", [128, 4, 16, 2])
        xg4 = k.sb("xg4", [128, 4, 16, 2])
        gbuf = [("gbuf%d" % i, k.sb("gbuf%d" % i, [128, 516])) for i in range(2)]
        acc = [("acc%d" % i, k.sb("acc%d" % i, [128, 512])) for i in range(2)]
        sil = [("sil%d" % i, k.sb("sil%d" % i, [128, 512])) for i in range(2)]
        rstd = k.sb("rstd", [128, 512])
        tmps = Rot([("tmp%d" % i, k.sb("tmp%d" % i, [128, 512])) for i in range(3)])
        g_ps = [("g_ps%d" % i, k.ps("g_ps%d" % i)) for i in range(2)]
        u_ps = [("u_ps%d" % i, k.ps("u_ps%d" % i)) for i in range(2)]
        m_ps = Rot([("m_ps%d" % i, k.ps("m_ps%d" % i)) for i in range(3)])
        ss_ps = k.ps("ss_ps")
        if final:
            hst = Rot([("hst%d" % i, k.sb("hst%d" % i, [128, 512])) for i in range(2)])
            hs = None
        else:
            hs = HSlots(k, G, sh_n, "c")

        XR = ["x3c%d" % m for m in range(NKC)]
        split_e1 = (not final) and G["NH"] == 2 and NT == 4
        s.dma("sp", "ldp2", cv[:, :], G["cv"][l], [], ["cv"])
        s.op("dve", [], ["ones"], lambda e: e.memset(ones_bf[:, :], 1.0))
        emit_aeff2(k, sc2, GN[:, 64 + l * 16:80 + l * 16], aeff2, "aeff2")
        emit_aeff2(k, sc_n, g_n, aeffn, "aeffn")

        ws = WStream(k, "w", 4, 8192, 2)
        plan = {}

        def add_out(n):
            return ws.add(lambda slot: (v3(slot, 16, 512), [(v3(slot, 16, 512), wov[:, :, n * 512:(n + 1) * 512])]))

        def add_gu(wview, jg):
            return ws.add(lambda slot: (v3(slot, 16, 512), [(v3(slot, 16, 512), wview[:, :, jg * 512:(jg + 1) * 512])]))

        def add_down(m):
            return ws.add(lambda slot: (v3(slot, NJ, 128), [(slot[:, 0:NJ * 128], w_down[m, :, :])]))

        for t in range(NT):
            for n in range(4):
                plan[(t, "out", n)] = add_out(n)
            for jg in range(NJ // 4):
                plan[(t, "g", jg)] = add_gu(wgv, jg)
                plan[(t, "u", jg)] = add_gu(wuv, jg)
            for m in range(16):
                plan[(t, "d", m)] = add_down(m)

        def outproj(keyfn, segs):
            for n in range(4):
                w3, wres = ws.get(plan[keyfn(n)])
                for mm in range(4):
                    m = n * 4 + mm
                    for (yy3, yres, xx3, xres, W) in segs:
                        pres, ps = m_ps.next()
                        for kc in range(NKC):
                            s.op("pe", [yres, wres], [pres],
                                 lambda e, kc=kc, w3=w3, ps=ps, mm=mm, yy3=yy3, W=W: e.matmul(ps[:, 0:W], w3[:, kc, mm * 128:(mm + 1) * 128], yy3[:, kc, 0:W],
                                                                                              start=(kc == 0), stop=(kc == NKC - 1)), sig=(kc == NKC - 1))
                        xrm = xres[m] if isinstance(xres, list) else xres
                        s.op("dve", [pres, xrm, "mods"], [xrm],
                             lambda e, m=m, ps=ps, xx3=xx3, W=W: e.scalar_tensor_tensor(out=xx3[:, m, 0:W], in0=ps[:, 0:W], scalar=gt1[:, m:m + 1],
                                                                                        in1=xx3[:, m, 0:W], op0=ALU.mult, op1=ALU.add))

        s.dma("sp", "ldh", xh[:, :, :], xv[:, :, 0:2], ["XS"], ["xh"])
        s.dma("sp", "ldh2", yh[:, :, :], yvs[0][:, :, 0:2], ["YR0"], ["yh"])

        def sink_h(kc, tmp, tres):
            s.op("act", [tres, "mods"], ["h2h"],
                 lambda e: e.activation(out=h2h[:, kc, :], in_=tmp[:, 0:2], func=AF.Identity, bias=sh2[:, kc:kc + 1]))

        for t in range(NT):
            c0 = 2 + t * 512
            if nh == 2 and t >= NT // 2:
                yh_i, yc0 = 1, (t - NT // 2) * 512
            else:
                yh_i, yc0 = 0, 2 + t * 512
            if t == 0:
                for m in range(NKC):
                    s.dma("sp", "ldx%d" % m, x3[:, m, :], xv[:, m, c0:c0 + 512], ["XS"], [XR[m]])
            if t == 0:
                s.dma("sp", "ldy", y3[:, :, :], yvs[yh_i][:, :, yc0:yc0 + 512], ["YR%d" % yh_i], ["y3"])
            segs = [(y3, "y3", x3, XR, 512)]
            if t == 0:
                segs = [(yh, "yh", xh, "xh", 2)] + segs
            if t == 0 and G.get("pending_rs") is not None:
                ws.get(plan[(0, "out", 0)])
                G["pending_rs"]()
                G["pending_rs"] = None
            outproj(lambda n, t=t: (t, "out", n), segs)
            if split_e1 and t == NT // 2:
                emit_e1(k, G, "early")
            if t == 0:
                emit_norm(k, "nh", xh, "xh", 2, aeff2, sq3, "act3", ones_bf, ss_ps, "ss_ps", rstd, "rstd", tmps, sink_h)

            def sink2(kc, tmp, tres):
                s.op("act", [tres, "mods"], ["y3"],
                     lambda e: e.activation(out=h23[:, kc, :], in_=tmp[:, :], func=AF.Identity, bias=sh2[:, kc:kc + 1]))

            emit_norm(k, "n2", x3, XR, 512, aeff2, sq3, "act3", ones_bf, ss_ps, "ss_ps", rstd, "rstd", tmps, sink2)

            for jg in range(NJ // 4):
                wg3, wgres = ws.get(plan[(t, "g", jg)])
                wu3, wures = ws.get(plan[(t, "u", jg)])
                for jj in range(4):
                    j = jg * 4 + jj
                    gres, gp = g_ps[j % 2]
                    ures, up = u_ps[j % 2]
                    bres, gb = gbuf[j % 2]
                    ares, ac = acc[j % 2]
                    sres, sl = sil[j % 2]
                    if t == 0:
                        pres, ps = m_ps.next()
                        for kc in range(NKC):
                            s.op("pe", ["h2h", wgres], [pres],
                                 lambda e, kc=kc, wg3=wg3, ps=ps, jj=jj: e.matmul(ps[:, 0:2], wg3[:, kc, jj * 128:(jj + 1) * 128], h2h[:, kc, :],
                                                                                  start=(kc == 0), stop=(kc == NKC - 1)), sig=(kc == NKC - 1))
                        s.op("dve", [pres, "msk"], ["gprev"],
                             lambda e, j=j, ps=ps: e.tensor_scalar(out=gprev[:, j, :], in0=ps[:, 0:2], scalar1=msk[:, 8:9], scalar2=None, op0=ALU.mult))
                    for kc in range(NKC):
                        s.op("pe", ["y3", wgres], [gres],
                             lambda e, kc=kc, gp=gp, jj=jj, wg3=wg3: e.matmul(gp[:, :], wg3[:, kc, jj * 128:(jj + 1) * 128], h23[:, kc, :],
                                                                              start=(kc == 0), stop=(kc == NKC - 1)), sig=(kc == NKC - 1))
                    for kc in range(NKC):
                        s.op("pe", ["y3", wures], [ures],
                             lambda e, kc=kc, up=up, jj=jj, wu3=wu3: e.matmul(up[:, :], wu3[:, kc, jj * 128:(jj + 1) * 128], h23[:, kc, :],
                                                                              start=(kc == 0), stop=(kc == NKC - 1)), sig=(kc == NKC - 1))
                    s.op("act", [gres], [bres], lambda e, gb=gb, gp=gp: e.mul(out=gb[:, 4:516], in_=gp[:, :], mul=1.0))
                    s.op("dve", ["gprev"], [bres], lambda e, gb=gb, j=j: e.tensor_copy(out=gb[:, 2:4], in_=gprev[:, j, :]))
                    s.op("dve", [bres], ["gprev"], lambda e, gb=gb, j=j: e.tensor_copy(out=gprev[:, j, :], in_=gb[:, 514:516]))
                    s.op("dve", [bres, "cv"], [ares],
                         lambda e, gb=gb, ac=ac, j=j: e.tensor_scalar(out=ac[:, :], in0=gb[:, 2:514], scalar1=cv[:, 4 * j:4 * j + 1],
                                                                      scalar2=cv[:, 4 * j + 3:4 * j + 4], op0=ALU.mult, op1=ALU.add))
                    s.op("dve", [bres, "cv", ares], [ares],
                         lambda e, gb=gb, ac=ac, j=j: e.scalar_tensor_tensor(out=ac[:, :], in0=gb[:, 3:515], scalar=cv[:, 4 * j + 1:4 * j + 2],
                                                                             in1=ac[:, :], op0=ALU.mult, op1=ALU.add))
                    s.op("dve", [bres, "cv", ares], [ares],
                         lambda e, gb=gb, ac=ac, j=j: e.scalar_tensor_tensor(out=ac[:, :], in0=gb[:, 4:516], scalar=cv[:, 4 * j + 2:4 * j + 3],
                                                                             in1=ac[:, :], op0=ALU.mult, op1=ALU.add))
                    s.op("act", [ares], [sres], lambda e, ac=ac, sl=sl: e.activation(out=sl[:, :], in_=ac[:, :], func=AF.Silu))
                    s.op("dve", [sres, ures], ["act3"],
                         lambda e, sl=sl, up=up, j=j: e.tensor_tensor(out=act3[:, j, :], in0=sl[:, :], in1=up[:, :], op=ALU.mult))

            if t + 1 < NT:
                if nh == 2 and t + 1 >= NT // 2:
                    nyh, nyc = 1, (t + 1 - NT // 2) * 512
                else:
                    nyh, nyc = 0, 2 + (t + 1) * 512
                s.dma("sp", "ldy", y3[:, :, :], yvs[nyh][:, :, nyc:nyc + 512], ["YR%d" % nyh], ["y3"])
            for m in range(16):
                wd3, wdres = ws.get(plan[(t, "d", m)])
                pres, ps = m_ps.next()
                for j in range(NJ):
                    s.op("pe", ["act3", wdres], [pres],
                         lambda e, j=j, wd3=wd3, ps=ps: e.matmul(ps[:, :], wd3[:, j, :], act3[:, j, :], start=(j == 0), stop=(j == NJ - 1)),
                         sig=(j == NJ - 1))
                s.op("dve", [pres, XR[m], "mods"], [XR[m]],
                     lambda e, m=m, ps=ps: e.scalar_tensor_tensor(out=x3[:, m, :], in0=ps[:, :], scalar=gt2[:, m:m + 1],
                                                                  in1=x3[:, m, :], op0=ALU.mult, op1=ALU.add))
                if not final:
                    s.dma("sp", "stx", xv[:, m, c0:c0 + 512], x3[:, m, :], [XR[m]], ["XS"])
            if not final:
                if t == NT - 1:
                    for sl_ in range(4):
                        s.op("dve", XR + ["msk"], ["xl4"],
                             lambda e, sl_=sl_: e.tensor_scalar(out=xl4[:, sl_, :, :], in0=x3[:, :, 510:512], scalar1=msk[:, 4 + sl_:5 + sl_], scalar2=None,
                                                                op0=ALU.mult))
                    s.dma("sp", "stxl", G["XL"].ap().rearrange("(s kc p) c -> p s kc c", s=4, p=128), xl4[:, :, :, :], ["xl4"], ["XL"])
                    s.collective("AllReduce", G["XL"], G["XG"], ["XL"], ["XG"])

            def next_x(kc, t=t):
                if t + 1 < NT:
                    nc0 = 2 + (t + 1) * 512
                    s.dma("sp", "ldx%d" % kc, x3[:, kc, :], xv[:, kc, nc0:nc0 + 512], ["XS"], [XR[kc]])

            if final:
                def sinkn(kc, tmp, tres, t=t):
                    next_x(kc)
                    hres, hsb_ = hst.next()
                    s.op("act", [tres, "zero"], [hres],
                         lambda e: e.activation(out=hsb_[:, :], in_=tmp[:, :], func=AF.Identity, bias=sh_n[:, kc:kc + 1]))
                    s.dma("sp", "sth_" + hres, ov[:, kc, t * 512:(t + 1) * 512], hsb_[:, :], [hres], [], is_out=True)
            else:
                def sinkn(kc, tmp, tres, t=t):
                    next_x(kc)
                    hs.sink(t, kc, tmp, tres)

            emit_norm(k, "nn", x3, XR, 512, aeffn, sq3, "act3", ones_bf, ss_ps, "ss_ps", rstd, "rstd", tmps, sinkn)

        if not final:
            s.dma("sp", "ldxg", xg4[:, :, :, :], G["XG"].ap().rearrange("(s kc p) c -> p s kc c", s=4, p=128), ["XG"], ["xg4"])
            s.op("dve", ["xg4", "msk"], ["xh"],
                 lambda e: e.tensor_scalar(out=xh[:, :, :], in0=xg4[:, 0, :, :], scalar1=msk[:, 0:1], scalar2=None, op0=ALU.mult))
            for sl_ in range(1, 4):
                s.op("dve", ["xg4", "msk", "xh"], ["xh"],
                     lambda e, sl_=sl_: e.scalar_tensor_tensor(out=xh[:, :, :], in0=xg4[:, sl_, :, :], scalar=msk[:, sl_:sl_ + 1], in1=xh[:, :, :],
                                                               op0=ALU.mult, op1=ALU.add))
            s.dma("sp", "stxh", xv[:, :, 0:2], xh[:, :, :], ["xh"], ["XS"])
        s.barrier()
    if not final:
        return lambda: emit_e1(k, G, "late" if split_e1 else "all")
    return None


def build_fused(S):
    k = KB()
    TOK = S // 4
    G = {}
    G["xT"] = k.din("xT", [D, TOK + 2])
    G["cT"] = k.din("cT", [128, 16])
    G["wada"] = k.din("wada", [DEPTH, D, 3072])
    G["bada"] = k.din("bada", [128, 384])
    G["gains"] = k.din("gains", [128, 144])
    G["msk_d"] = k.din("msk", [128, 9])
    G["win"] = k.din("win", [DEPTH, D, NIN_G])
    G["pwm"] = k.din("pwm", [DEPTH, 128, 384])
    G["pmat"] = k.din("pmat", [128, 384])
    G["cst"] = k.din("cst", [128, 640])
    G["pbp"] = k.din("pbp", [DEPTH, 128, 16])
    G["bfp"] = k.din("bfp", [DEPTH, 1, 2])
    G["w_out"] = k.din("w_out", [DEPTH, D, D])
    G["w_gate"] = k.din("w_gate", [DEPTH, D, DFF])
    G["w_up"] = k.din("w_up", [DEPTH, D, DFF])
    G["w_down"] = k.din("w_down", [DEPTH, 16, 128, NJ * 128])
    G["cv"] = k.din("cv", [DEPTH, 128, NJ * 4])
    G["out"] = k.dout("out", [D, TOK])
    G["XS"] = k.dscr("XS", [D, TOK + 2], F32)
    G["NH"] = max(1, TOK // 1024)
    G["PW"] = TOK // G["NH"]
    G["HB"] = [k.dscr("HB%d" % i, [D, G["PW"]], BF16) for i in range(4 * G["NH"])]
    G["HG"] = [k.dscr("HG%d" % i, [D, G["PW"]], BF16) for i in range(4 * G["NH"])]
    if TOK // 512 >= 2:
        G["YS"] = [k.dscr("YSa", [4 * D, TOK // 2 + 2], BF16), k.dscr("YSb", [4 * D, TOK // 2], BF16)]
        G["YR"] = [k.dscr("YRa", [D, TOK // 2 + 2], BF16), k.dscr("YRb", [D, TOK // 2], BF16)]
    else:
        G["YS"] = [k.dscr("YS", [4 * D, TOK + 2], BF16)]
        G["YR"] = [k.dscr("YR", [D, TOK + 2], BF16)]
    G["MB"] = k.dscr("MB", [512, 96], F32)
    G["MG"] = k.dscr("MG", [512, 96], F32)
    G["XL"] = k.dscr("XL", [4 * D, 2], F32)
    G["XG"] = k.dscr("XG", [4 * D, 2], F32)
    G["mods"] = k.sb("mods", [128, 384])
    G["gn"] = k.sb("gn", [128, 144])
    G["msk"] = k.sb("msk_sb", [128, 9])
    G["zero"] = k.sb("zero", [128, 32])
    phase_m(k, G)
    pending = phase_a(k, G, TOK)
    for l in range(DEPTH):
        phase_b(k, G, l, S, after_setup=pending)
        pending = phase_c(k, G, l, TOK)
    k.s.finish("sp")
    global _LAST_SCHED
    _LAST_SCHED = k.s
    return k.close()


def _fm(v):
    v = np.asarray(v, np.float32)
    return np.ascontiguousarray(v.reshape(-1, 128).T)


def _pool_mats(win):
    s_ = np.arange(128)[:, None]
    t_ = np.arange(128)[None, :]
    band = ((t_ - s_) >= 0) & ((t_ - s_) < win)
    eye = np.eye(128, dtype=np.float32)
    pd = band.astype(np.float32) / win - eye
    pd0 = band.astype(np.float32) / np.minimum(t_ + 1, win).astype(np.float32) - eye
    pp = (((t_ + 128 - s_) < win)).astype(np.float32) / win
    return np.ascontiguousarray(np.concatenate([pd0, pd, pp], axis=1).astype(np.float32))


_PROGS = {}
_LAST_SCHED = None


def kernel(x, c, w_ada, b_ada, g_mix, w_in, b_f, pool_w, pool_scale, lru_conv_w, lru_conv_b,
           lru_wa, lru_ba, lru_wi, lru_bi, lru_lambda, w_out, g_ffn, w_ffn_gate, w_ffn_up,
           ffn_conv_w, ffn_conv_b, w_ffn_down, final_g):
    f32 = np.float32
    A = lambda v: np.asarray(v, f32)
    x = A(x)
    B, S, _ = x.shape
    TOK = S // 4
    c, w_ada, b_ada, g_mix, w_in, b_f = A(c), A(w_ada), A(b_ada), A(g_mix), A(w_in), A(b_f)
    pool_w, pool_scale, lru_conv_w, lru_conv_b = A(pool_w), A(pool_scale), A(lru_conv_w), A(lru_conv_b)
    lru_wa, lru_ba, lru_wi, lru_bi, lru_lambda = A(lru_wa), A(lru_ba), A(lru_wi), A(lru_bi), A(lru_lambda)
    w_out, g_ffn, w_ffn_gate, w_ffn_up = A(w_out), A(g_ffn), A(w_ffn_gate), A(w_ffn_up)
    ffn_conv_w, ffn_conv_b, w_ffn_down, final_g = A(ffn_conv_w), A(ffn_conv_b), A(w_ffn_down), A(final_g)

    if S not in _PROGS:
        _PROGS[S] = build_fused(S)
    nc = _PROGS[S]

    bada = np.ascontiguousarray(np.concatenate([_fm(b_ada[l]) for l in range(DEPTH)], axis=1))
    gains = np.ascontiguousarray(np.concatenate([_fm(g_mix[l]) for l in range(DEPTH)] + [_fm(g_ffn[l]) for l in range(DEPTH)]
                                                + [_fm(final_g)], axis=1))
    cst = np.ascontiguousarray(np.concatenate([np.eye(128, dtype=f32),
                                               np.where(np.arange(128)[None, :] >= np.arange(128)[:, None], 0.0, MASKNEG).astype(f32),
                                               np.zeros((128, 384), f32)], axis=1))
    perm = []
    for g in range(4):
        perm += list(range(g * 128, (g + 1) * 128))
        perm += list(range(512 + 2 * g * 128, 512 + (2 * g + 2) * 128))
        perm += list(range(1536 + g * 128, 1536 + (g + 1) * 128))
    w_out_p = np.ascontiguousarray(w_out[:, perm, :])
    wd_l = np.ascontiguousarray(w_ffn_down.reshape(DEPTH, NJ, 128, 16, 128).transpose(0, 3, 2, 1, 4).reshape(DEPTH, 16, 128, NJ * 128))
    cvv = np.zeros((DEPTH, 128, NJ * 4), f32)
    for l in range(DEPTH):
        for kk in range(3):
            cvv[l][:, kk::4] = _fm(ffn_conv_w[l][kk])
        cvv[l][:, 3::4] = _fm(ffn_conv_b[l])
    per_g = []
    for g in range(4):
        cols = []
        for h in range(2):
            cols += list(range(512 + (2 * g + h) * 128, 512 + (2 * g + h + 1) * 128))
        for h in range(2):
            cols += list(range(1536 + (2 * g + h) * 128, 1536 + (2 * g + h + 1) * 128))
        cols += list(range(3592 + g * 128, 3592 + (g + 1) * 128))
        cols += list(range(4104 + g * 128, 4104 + (g + 1) * 128))
        for h in range(2):
            cols += list(range(2560 + (2 * g + h) * 128, 2560 + (2 * g + h + 1) * 128))
        cols += list(range(g * 128, (g + 1) * 128))
        cols += [3584 + 2 * g, 3584 + 2 * g + 1]
        gs = slice(g * 128, (g + 1) * 128)
        pbv = np.zeros((DEPTH, 128, 16), f32)
        for l in range(DEPTH):
            pbv[l][:, 0] = pool_scale[l][gs]
            for kk in range(4):
                pbv[l][:, 1 + kk] = lru_conv_w[l][kk][gs]
            pbv[l][:, 5] = lru_conv_b[l][gs]
            pbv[l][:, 6] = lru_ba[l][gs]
            pbv[l][:, 7] = lru_bi[l][gs]
            pbv[l][:, 8] = lru_lambda[l][gs]
        per_g.append({
            "win": np.ascontiguousarray(w_in[:, :, cols]),
            "pwm": np.ascontiguousarray(np.concatenate([pool_w[:, g], lru_wa[:, g], lru_wi[:, g]], axis=2)),
            "pmat": _pool_mats(POOL_WINDOWS[g]),
            "pbp": pbv,
            "bfp": np.ascontiguousarray(b_f[:, None, 2 * g:2 * g + 2]),
            "wada": np.ascontiguousarray(w_ada[:, :, g * 3072:(g + 1) * 3072]),
        })
    maps = []
    for cid in range(NCORE):
        b, j = cid // 4, cid % 4
        xin = np.zeros((D, TOK + 2), f32)
        if j == 0:
            xin[:, 2:] = x[b, 0:TOK, :].T
        else:
            xin[:, :] = x[b, j * TOK - 2:(j + 1) * TOK, :].T
        msk = np.zeros((128, 9), f32)
        msk[:, j] = 1.0
        if j + 1 < 4:
            msk[:, 4 + j + 1] = 1.0
        msk[:, 8] = 0.0 if j == 0 else 1.0
        m = {"xT": xin, "cT": _fm(c[b]), "bada": bada, "gains": gains, "msk": msk, "cst": cst,
             "w_out": w_out_p, "w_gate": w_ffn_gate, "w_up": w_ffn_up, "w_down": wd_l, "cv": cvv}
        m.update(per_g[j])
        maps.append(m)
    res = run_bass_kernel_spmd(nc, maps, core_ids=list(range(NCORE)))
    r = res.results
    out = np.empty((B, S, D), f32)
    for cid in range(NCORE):
        b, j = cid // 4, cid % 4
        out[b, j * TOK:(j + 1) * TOK, :] = r[cid]["out"].T
    return out
```
